# Optimizing a Trainium2 kernel written in Bass

```python
import math
import jax, jax.numpy as jnp
from jax import lax
import numpy as np

D_MODEL = 1024
BATCH = 2
SEQ = 16384
DEPTH = 1

GRID_W = 64
HEAD_DIM = 64
N_Q_HEADS = 8
N_KV_HEADS = 2
Q_PER_KV = N_Q_HEADS // N_KV_HEADS
ATTN_WIDTH = N_Q_HEADS * HEAD_DIM
KV_WIDTH = N_KV_HEADS * HEAD_DIM
Q_BLOCK = 128
ROPE_THETA = 10000.0
AXIAL_DIM = HEAD_DIM // 2
SSD_D_INNER = D_MODEL
SSD_HEADDIM = 64
SSD_HEADS = SSD_D_INNER // SSD_HEADDIM
SSD_GROUPS = 2
SSD_HEADS_PER_GROUP = SSD_HEADS // SSD_GROUPS
SSD_STATE = 128
CONV_K = 5
CHUNK = 128
CONV_CH = SSD_D_INNER + 2 * SSD_GROUPS * SSD_STATE
N_BRANCH = 2
D_FF = ((8 * D_MODEL // 3 + 255) // 256) * 256
EPS = 1e-6
IN_SPLITS = (ATTN_WIDTH, KV_WIDTH, KV_WIDTH, SSD_D_INNER, CONV_CH, 2 * SSD_HEADS, N_BRANCH * D_MODEL)
D_IN_PROJ = sum(IN_SPLITS)

kernel_name = "hybrid_gqa_ssd_gated_sandwich_block"


def rms_norm(x, g):
    xf = x.astype(jnp.float32)
    xf = xf * lax.rsqrt(jnp.mean(xf * xf, axis=-1, keepdims=True) + EPS)
    return (xf * g.astype(jnp.float32)).astype(x.dtype)


def split_cols(h, sizes):
    idx = np.cumsum(np.array(sizes[:-1]))
    return jnp.split(h, [int(i) for i in idx], axis=-1)


def axial_angles(seq_len):
    rows = seq_len // GRID_W
    row_idx = jnp.repeat(jnp.arange(rows, dtype=jnp.float32), GRID_W)
    col_idx = jnp.tile(jnp.arange(GRID_W, dtype=jnp.float32), rows)
    inv_freq = ROPE_THETA ** (-jnp.arange(0, AXIAL_DIM, 2, dtype=jnp.float32) / AXIAL_DIM)
    ang_row = row_idx[:, None] * inv_freq[None, :]
    ang_col = col_idx[:, None] * inv_freq[None, :]
    return ang_row, ang_col


def rotate(xh, ang):
    half = xh.shape[-1] // 2
    x1, x2 = xh[..., :half], xh[..., half:]
    cos, sin = jnp.cos(ang).astype(xh.dtype), jnp.sin(ang).astype(xh.dtype)
    return jnp.concatenate([x1 * cos - x2 * sin, x2 * cos + x1 * sin], axis=-1)


def axial_rope(x, ang_row, ang_col):
    shp = (1, x.shape[1]) + (1,) * (x.ndim - 3) + (AXIAL_DIM // 2,)
    xr = rotate(x[..., :AXIAL_DIM], ang_row.reshape(shp))
    xc = rotate(x[..., AXIAL_DIM:], ang_col.reshape(shp))
    return jnp.concatenate([xr, xc], axis=-1)


def gqa_attention(q, k, v):
    b, s = q.shape[0], q.shape[1]
    nblk = s // Q_BLOCK
    scale = 1.0 / math.sqrt(HEAD_DIM)
    qb = jnp.moveaxis(q.reshape(b, nblk, Q_BLOCK, N_KV_HEADS, Q_PER_KV, HEAD_DIM), 1, 0)

    def block(q_blk):
        sc = jnp.einsum("bqkgd,bskd->bkgqs", q_blk, k).astype(jnp.float32) * scale
        p = jax.nn.softmax(sc, axis=-1).astype(v.dtype)
        return jnp.einsum("bkgqs,bskd->bqkgd", p, v)

    out = lax.map(block, qb)
    return jnp.moveaxis(out, 0, 1).reshape(b, s, ATTN_WIDTH)


def ssd_chunked(xs, dt, a, bm, cm):
    b, l, g, r, p = xs.shape
    n = bm.shape[-1]
    c = l // CHUNK
    xd = (xs * dt[..., None]).reshape(b, c, CHUNK, g, r, p)
    adt = jnp.moveaxis((dt * a).reshape(b, c, CHUNK, g, r), 2, -1)
    bc = bm.reshape(b, c, CHUNK, g, n)
    cc = cm.reshape(b, c, CHUNK, g, n)
    a_cs = jnp.cumsum(adt, axis=-1)
    tri = jnp.tril(jnp.ones((CHUNK, CHUNK), dtype=bool))
    seg = a_cs[..., :, None] - a_cs[..., None, :]
    decay_in = jnp.exp(jnp.where(tri, seg, -jnp.inf))
    cb = jnp.einsum("bclgn,bcsgn->bcgls", cc, bc)
    y_diag = jnp.einsum("bcgls,bcgrls,bcsgrp->bclgrp", cb, decay_in, xd)
    decay_states = jnp.exp(a_cs[..., -1:] - a_cs)
    states = jnp.einsum("bclgn,bcgrl,bclgrp->bcgrpn", bc, decay_states, xd)
    chunk_decay = jnp.exp(a_cs[..., -1])

    def step(h, inp):
        st, dec = inp
        return h * dec[..., None, None] + st, h

    h0 = jnp.zeros((b, g, r, p, n), dtype=xs.dtype)
    _, prev = lax.scan(step, h0, (jnp.moveaxis(states, 1, 0), jnp.moveaxis(chunk_decay, 1, 0)))
    prev = jnp.moveaxis(prev, 0, 1)
    y_off = jnp.einsum("bclgn,bcgrpn,bcgrl->bclgrp", cc, prev, jnp.exp(a_cs))
    return (y_diag + y_off).reshape(b, l, g, r, p)


def centred_depthwise_conv(u, w, bias):
    pad = (CONV_K - 1) // 2
    out = lax.conv_general_dilated(
        u, w[:, None, :].astype(u.dtype), window_strides=(1,), padding=[(pad, pad)],
        dimension_numbers=("NWC", "WIO", "NWC"), feature_group_count=u.shape[-1])
    return out + bias


def ssd_branch(z, xbc, dt_raw, conv_w, conv_b, dt_bias_f, dt_bias_b, a_log_f, a_log_b, d_skip, ssd_norm):
    b, l, _ = z.shape
    xbc = jax.nn.silu(centred_depthwise_conv(xbc, conv_w, conv_b))
    xs, bm, cm = split_cols(xbc, (SSD_D_INNER, SSD_GROUPS * SSD_STATE, SSD_GROUPS * SSD_STATE))
    f32 = jnp.float32
    xs = xs.astype(f32).reshape(b, l, SSD_GROUPS, SSD_HEADS_PER_GROUP, SSD_HEADDIM)
    bm = bm.astype(f32).reshape(b, l, SSD_GROUPS, SSD_STATE)
    cm = cm.astype(f32).reshape(b, l, SSD_GROUPS, SSD_STATE)
    dt_raw = dt_raw.astype(f32)
    shp = (SSD_GROUPS, SSD_HEADS_PER_GROUP)
    dt_f = jax.nn.softplus(dt_raw[..., :SSD_HEADS] + dt_bias_f.astype(f32)).reshape(b, l, *shp)
    dt_b = jax.nn.softplus(dt_raw[..., SSD_HEADS:] + dt_bias_b.astype(f32)).reshape(b, l, *shp)
    a_f = -jnp.exp(a_log_f.astype(f32)).reshape(shp)
    a_b = -jnp.exp(a_log_b.astype(f32)).reshape(shp)
    y_fwd = ssd_chunked(xs, dt_f, a_f, bm, cm)
    flip = lambda t: jnp.flip(t, axis=1)
    y_bwd = flip(ssd_chunked(flip(xs), flip(dt_b), a_b, flip(bm), flip(cm)))
    y = y_fwd + y_bwd + d_skip.astype(f32).reshape(shp)[..., None] * xs
    y = y.reshape(b, l, SSD_D_INNER).astype(z.dtype)
    return rms_norm(y * jax.nn.silu(z), ssd_norm)


def setup_inputs(seed: int = 0) -> dict:
    key = jax.random.key(seed)
    ks = jax.random.split(key, 24)
    L = DEPTH
    nrm = lambda k, shape, fan_in: jax.random.normal(k, shape, jnp.float32) * fan_in ** -0.5
    gain = lambda k, shape: 1.0 + 0.05 * jax.random.normal(k, shape, jnp.float32)
    dt0 = jnp.exp(jax.random.uniform(ks[6], (2, L, SSD_HEADS), jnp.float32, math.log(1e-3), math.log(1e-1)))
    dt_bias = dt0 + jnp.log(-jnp.expm1(-dt0))
    a_log = jnp.log(jax.random.uniform(ks[7], (2, L, SSD_HEADS), jnp.float32, 1.0, 16.0))
    return {
        "x": jax.random.normal(ks[0], (BATCH, SEQ, D_MODEL), jnp.float32),
        "w_in": nrm(ks[1], (L, D_MODEL, D_IN_PROJ), D_MODEL),
        "q_norm": gain(ks[2], (L, HEAD_DIM)),
        "k_norm": gain(ks[3], (L, HEAD_DIM)),
        "conv_w": nrm(ks[4], (L, CONV_K, CONV_CH), CONV_K),
        "conv_b": 0.02 * jax.random.normal(ks[5], (L, CONV_CH), jnp.float32),
        "dt_bias_f": dt_bias[0],
        "dt_bias_b": dt_bias[1],
        "a_log_f": a_log[0],
        "a_log_b": a_log[1],
        "d_skip": 1.0 + 0.1 * jax.random.normal(ks[8], (L, SSD_HEADS), jnp.float32),
        "ssd_norm": gain(ks[9], (L, SSD_D_INNER)),
        "w_attn_proj": nrm(ks[10], (L, ATTN_WIDTH, D_MODEL), ATTN_WIDTH),
        "w_ssd_proj": nrm(ks[11], (L, SSD_D_INNER, D_MODEL), SSD_D_INNER),
        "w_out": nrm(ks[12], (L, D_MODEL, D_MODEL), D_MODEL),
        "norm1_pre": gain(ks[13], (L, D_MODEL)),
        "norm1_post": gain(ks[14], (L, D_MODEL)),
        "norm2_pre": gain(ks[15], (L, D_MODEL)),
        "norm2_post": gain(ks[16], (L, D_MODEL)),
        "w_gate_up": nrm(ks[17], (L, D_MODEL, 2 * D_FF), D_MODEL),
        "w_down": nrm(ks[18], (L, D_FF, D_MODEL), D_FF),
    }


def reference(x, w_in, q_norm, k_norm, conv_w, conv_b, dt_bias_f, dt_bias_b, a_log_f, a_log_b,
              d_skip, ssd_norm, w_attn_proj, w_ssd_proj, w_out, norm1_pre, norm1_post,
              norm2_pre, norm2_post, w_gate_up, w_down):
    b, s, _ = x.shape
    ang_row, ang_col = axial_angles(s)
    for i in range(DEPTH):
        h = rms_norm(x, norm1_pre[i])
        proj = h @ w_in[i]
        q, k, v, z, xbc, dt_raw, gates = split_cols(proj, IN_SPLITS)
        q = rms_norm(q.reshape(b, s, N_KV_HEADS, Q_PER_KV, HEAD_DIM), q_norm[i])
        k = rms_norm(k.reshape(b, s, N_KV_HEADS, HEAD_DIM), k_norm[i])
        v = v.reshape(b, s, N_KV_HEADS, HEAD_DIM)
        q = axial_rope(q, ang_row, ang_col)
        k = axial_rope(k, ang_row, ang_col)
        attn_out = gqa_attention(q, k, v) @ w_attn_proj[i]
        ssd_y = ssd_branch(z, xbc, dt_raw, conv_w[i], conv_b[i], dt_bias_f[i], dt_bias_b[i],
                           a_log_f[i], a_log_b[i], d_skip[i], ssd_norm[i])
        ssd_out = ssd_y @ w_ssd_proj[i]
        g_attn, g_ssd = jnp.split(jax.nn.sigmoid(gates), N_BRANCH, axis=-1)
        mixed = (g_attn * attn_out + g_ssd * ssd_out) @ w_out[i]
        x = x + rms_norm(mixed, norm1_post[i])
        h2 = rms_norm(x, norm2_pre[i])
        gate, up = jnp.split(h2 @ w_gate_up[i], 2, axis=-1)
        ffn = (jax.nn.silu(gate) * up) @ w_down[i]
        x = x + rms_norm(ffn, norm2_post[i])
    return x
```

```python
import math
import numpy as np
import concourse.bass as bass
import concourse.mybir as mybir
from concourse.bass_utils import run_bass_kernel_spmd
from contextlib import ExitStack

F32 = mybir.dt.float32
BF16 = mybir.dt.bfloat16
AF = mybir.ActivationFunctionType
ALU = mybir.AluOpType
AX = mybir.AxisListType

D = 1024
GRID_W = 64
HD = 64
NQH = 8
NKV = 2
SSD_H = 16
SSD_P = 64
SSD_N = 128
CONV_K = 5
D_FF = 2816
EPS = 1e-6
OQ, OK_, OV, OZ, OXS, OB, OC, ODT, OG = 0, 512, 640, 768, 1792, 2816, 3072, 3328, 3360
NEG = -30000.0


class Buf:
    __slots__ = ("name", "t", "lw", "rd", "dsem", "dram", "root")

    def __init__(self, name, t=None, dram=False, root=None):
        self.root = root if root is not None else self
        self.name = name
        self.t = t
        self.lw = None
        self.rd = {}
        self.dsem = None
        self.dram = dram

    def __getitem__(self, idx):
        return self.t[idx]


class Prog:
    ENG = ("pe", "act", "dve", "pool", "sp")

    def __init__(self, nc, stack):
        self.nc = nc
        self.stack = stack
        self.sems = []
        self.q = {}
        for e in self.ENG:
            s = self._new_sem("q_" + e)
            self.q[e] = {"ops": [], "cnt": 0, "sem": s, "seen": {}}
        self.dma_cnt = {}
        self.uid = 0
        self.clock = {}

    def _new_sem(self, name):
        name = f"{name}_{len(self.sems)}"
        h = self.stack.enter_context(self.nc.semaphore(name))
        self.sems.append(h)
        return len(self.sems) - 1

    def sbuf(self, name, shape, dt, stack=None):
        self.uid += 1
        t = (stack or self.stack).enter_context(self.nc.sbuf_tensor(f"{name}_{self.uid}", list(shape), dt))
        return Buf(name, t)

    def psum(self, name, shape, dt=F32, stack=None):
        self.uid += 1
        t = (stack or self.stack).enter_context(self.nc.psum_tensor(f"{name}_{self.uid}", list(shape), dt))
        return Buf(name, t)

    def op(self, qn, fn, reads=(), writes=(), dma=False):
        q = self.q[qn]
        deps = {}
        reads = [b.root for b in reads]
        writes = [b.root for b in writes]

        def add(ev):
            if ev is None:
                return
            s, v = ev
            if deps.get(s, 0) < v:
                deps[s] = v

        for b in reads:
            add(b.lw)
        for b in writes:
            add(b.lw)
            for s, v in b.rd.items():
                add((s, v))
        if dma:
            b0 = [b for b in list(writes) + list(reads) if not b.dram][0]
            if b0.dsem is None:
                b0.dsem = self._new_sem("d_" + b0.name)
            key = b0.dsem
            c = self.dma_cnt.get(key, 0)
            if c:
                add((key, c))
            c += 16
            self.dma_cnt[key] = c
            ev = (key, c)
            inc = 16
        else:
            q["cnt"] += 1
            ev = (q["sem"], q["cnt"])
            inc = 1
        seen = q["seen"]
        for s, v in sorted(deps.items(), key=lambda kv: -kv[1]):
            if qn == "pe" and s == q["sem"]:
                continue
            if seen.get(s, 0) >= v:
                continue
            seen[s] = v
            q["ops"].append(("w", s, v))
            snap = self.clock.get((s, v))
            if snap:
                for s2, v2 in snap.items():
                    if seen.get(s2, 0) < v2:
                        seen[s2] = v2
        q["ops"].append(("i", fn, ev[0], inc))
        snap = dict(seen)
        if not dma:
            snap[ev[0]] = max(snap.get(ev[0], 0), ev[1] - 1)
        self.clock[ev] = snap
        for b in reads:
            if b.rd.get(ev[0], 0) < ev[1]:
                b.rd[ev[0]] = ev[1]
        for b in writes:
            b.lw = ev
            b.rd = {}
        return ev

    def barrier(self):
        targets = [(self.q[e]["sem"], self.q[e]["cnt"]) for e in self.ENG if self.q[e]["cnt"]]
        targets += [(k, c) for k, c in self.dma_cnt.items()]
        for e in self.ENG:
            q = self.q[e]
            for s, v in targets:
                if s == q["sem"]:
                    continue
                if q["seen"].get(s, 0) >= v:
                    continue
                q["seen"][s] = v
                q["ops"].append(("w", s, v))

    def emit(self):
        nc = self.nc
        sems = self.sems

        def replay(e, ops):
            for o in ops:
                if o[0] == "w":
                    e.wait_ge(sems[o[1]], o[2])
                else:
                    o[1](e).then_inc(sems[o[2]], o[3])

        with nc.Block() as block:
            @block.tensor
            def _(e):
                replay(e, self.q["pe"]["ops"])

            @block.scalar
            def _(e):
                replay(e, self.q["act"]["ops"])

            @block.vector
            def _(e):
                replay(e, self.q["dve"]["ops"])

            @block.gpsimd
            def _(e):
                replay(e, self.q["pool"]["ops"])

            @block.sync
            def _(e):
                replay(e, self.q["sp"]["ops"])


def ap_(t, off, dims):
    return bass.AP(t.tensor, off, [list(d) for d in dims])


def build(S, T):
    NO = T // 128
    NS = (S - T) // 128
    NB = S // 128
    NQT = max(1, T // 512)
    QW = T // NQT

    nc = bass.Bass("TRN2", target_bir_lowering=False)

    def din(name, shape):
        return nc.dram_tensor(name, list(shape), F32, kind="ExternalInput").ap()

    xs_d = din("xs", [NS, 132, D])
    xo_d = din("xo", [NO, 132, D])
    poss_d = din("poss", [128, NS, 2])
    poso_d = din("poso", [128, NO, 2])
    mk_d = din("mk", [128, NS, 2])
    w_in_d = din("w_in", [D, 5408])
    qn_d = din("q_norm", [1, 64])
    kn_d = din("k_norm", [1, 64])
    cw_d = din("conv_w", [5, 1536])
    cb_d = din("conv_b", [1, 1536])
    dtb_d = din("dt_bias", [1, 32])
    alog_d = din("a_log", [1, 32])
    dsk_d = din("d_skip", [1, 16])
    sn_d = din("ssd_norm", [1, D])
    wap_d = din("w_attn_proj", [512, D])
    wsp_d = din("w_ssd_proj", [D, D])
    wo_d = din("w_out", [D, D])
    n1a_d = din("norm1_pre", [1, D])
    n1b_d = din("norm1_post", [1, D])
    n2a_d = din("norm2_pre", [1, D])
    n2b_d = din("norm2_post", [1, D])
    wgu_d = din("w_gate_up", [D, 2 * D_FF])
    wd_d = din("w_down", [D_FF, D])
    ident_d = din("c_ident", [128, 128])
    tri_d = din("c_tri", [128, 128])
    triu_d = din("c_triu", [128, 128])
    invf_d = din("c_invf", [1, 32])
    out_d = nc.dram_tensor("out", [NO, 128, D], F32, kind="ExternalOutput").ap()

    def dscr(name, shape, dt):
        return nc.dram_tensor(name, list(shape), dt, kind="Internal").ap()

    s_xs = dscr("s_xs", [NO, 128, 1024], BF16)
    s_bt = dscr("s_bt", [NO, 128, 256], BF16)
    s_bT = dscr("s_bT", [NO, 128, 256], BF16)
    s_cT = dscr("s_cT", [NO, 128, 256], BF16)
    s_yf = dscr("s_yf", [NO, 128, 1024], F32)
    s_ao = dscr("s_ao", [NO, 128, 1024], F32)
    s_x1 = dscr("s_x1", [NO, 128, 1024], F32)
    s_kt = dscr("s_kt", [NB, 128, 128], BF16)
    s_v = dscr("s_v", [NB, 128, 128], BF16)
    s_qt = dscr("s_qt", [NO, 128, 512], BF16)
    s_yb = dscr("s_yb", [NO, 128, 1024], BF16)
    s_ff = dscr("s_ff", [NO, 128, 1024], F32)
    D_kt = [Buf(f"dkt{i}", dram=True) for i in range(NB)]
    D_v = [Buf(f"dv{i}", dram=True) for i in range(NB)]
    D_qt = [Buf(f"dqt{i}", dram=True) for i in range(NO)]
    D_yb = [Buf(f"dyb{i}", dram=True) for i in range(NO)]
    D_ff = [Buf(f"dff{i}", dram=True) for i in range(NO)]
    D_xs = [Buf(f"dxs{i}", dram=True) for i in range(NO)]
    D_bt = [Buf(f"dbt{i}", dram=True) for i in range(NO)]
    D_bT = [Buf(f"dbT{i}", dram=True) for i in range(NO)]
    D_cT = [Buf(f"dcT{i}", dram=True) for i in range(NO)]
    D_yf = [Buf(f"dyf{i}", dram=True) for i in range(NO)]
    D_ao = [Buf(f"dao{i}", dram=True) for i in range(NO)]
    D_x1 = [Buf(f"dx1{i}", dram=True) for i in range(NO)]
    D_out = [Buf(f"dout{i}", dram=True) for i in range(NO)]

    with ExitStack() as st:
        P = Prog(nc, st)

        def DMA(qn, out, in_, R, W):
            P.op(qn, lambda e: e.dma_start(out=out, in_=in_), R, W, dma=True)

        def DMAS(qn, out, in_, R, W):
            P.op(qn, lambda e: e.dma_start(out=out, in_=in_, allow_slow_non_contiguous=True), R, W, dma=True)

        def ACT(out, in_, func, R, W, **kw):
            P.op("act", lambda e: e.activation(out=out, in_=in_, func=func, **kw), R, W)

        def TT(eng, out, a, b, op, R, W):
            P.op(eng, lambda e: e.tensor_tensor(out=out, in0=a, in1=b, op=op), R, W)

        def TS(eng, out, a, s1, s2, op0, op1, R, W):
            if s2 is None:
                P.op(eng, lambda e: e.tensor_scalar(out=out, in0=a, scalar1=s1, scalar2=None, op0=op0), R, W)
            else:
                P.op(eng, lambda e: e.tensor_scalar(out=out, in0=a, scalar1=s1, scalar2=s2, op0=op0, op1=op1), R, W)

        def STT(out, a, s, b, op0, op1, R, W):
            P.op("dve", lambda e: e.scalar_tensor_tensor(out=out, in0=a, scalar=s, in1=b, op0=op0, op1=op1), R, W)

        def CP(eng, out, in_, R, W):
            if eng == "act":
                P.op("act", lambda e: e.copy(out=out, in_=in_), R, W)
            else:
                P.op(eng, lambda e: e.tensor_copy(out=out, in_=in_), R, W)

        def MM(out, lhsT, rhs, start, stop, R, W):
            P.op("pe", lambda e: e.matmul(out, lhsT=lhsT, rhs=rhs, start=start, stop=stop), R, W)

        def TR(out, in_, idn, R, W):
            P.op("pe", lambda e: e.transpose(out=out, in_=in_, identity=idn), R, W)

        def RECIP(out, in_, R, W):
            P.op("dve", lambda e: e.reciprocal(out=out, in_=in_), R, W)

        def RSUM(out, in_, R, W):
            P.op("dve", lambda e: e.reduce_sum(out=out, in_=in_, axis=AX.X), R, W)

        def MEMSET(eng, ap, val, W):
            P.op(eng, lambda e: e.memset(ap, val), (), W)

        cast_rr = [0]

        def cast_eng():
            cast_rr[0] += 1
            return ("dve", "act")[cast_rr[0] % 2]

        identf = P.sbuf("identf", [128, 128], F32)
        ident = P.sbuf("ident", [128, 128], BF16)
        tri = P.sbuf("tri", [128, 128], F32)
        triu = P.sbuf("triu", [128, 128], F32)
        mnegf = P.sbuf("mnegf", [128, 128], BF16)
        mnegb = P.sbuf("mnegb", [128, 128], BF16)
        onesf = P.sbuf("onesf", [128, 128], F32)
        onesb = P.sbuf("onesb", [128, 128], BF16)
        epsc = P.sbuf("epsc", [128, 1], F32)
        onec = P.sbuf("onec", [128, 1], F32)
        npi = P.sbuf("npi", [128, 1], F32)
        invf = P.sbuf("invf", [128, 32], F32)
        gq = P.sbuf("gq", [128, 64], F32)
        gk = P.sbuf("gk", [128, 64], F32)
        negB = P.sbuf("negB", [128, 1], F32)
        dtbias = P.sbuf("dtbias", [128, 32], F32)
        aneg = P.sbuf("aneg", [128, 32], F32)
        dskip = P.sbuf("dskip", [128, 16], F32)
        cbrow = P.sbuf("cbrow", [1, 1536], BF16)
        cbcol = P.sbuf("cbcol", [128, 12], F32)
        cwcol = P.sbuf("cwcol", [128, 5, 12], F32)
        Hf = P.sbuf("Hf", [128, 1024], F32)
        Hb = P.sbuf("Hb", [128, 1024], F32)
        dtb_own = P.sbuf("dtb_own", [128, NO, 16], F32)

        gn1a = P.sbuf("gn1a", [128, D], F32)
        phSA = ExitStack()
        diag = P.sbuf("diag", [128, 5, 12, 128], BF16, phSA)
        tmpc = P.sbuf("tmpc", [1, 1536], F32, phSA)

        DMA("sp", identf[:], ident_d, [], [identf])
        DMA("sp", tri[:], tri_d, [], [tri])
        DMA("sp", triu[:], triu_d, [], [triu])
        DMA("sp", invf[:], ap_(invf_d, 0, [[0, 128], [1, 32]]), [], [invf])
        DMA("sp", gq[:], ap_(qn_d, 0, [[0, 128], [1, 64]]), [], [gq])
        DMA("sp", gk[:], ap_(kn_d, 0, [[0, 128], [1, 64]]), [], [gk])
        DMA("sp", dtbias[:], ap_(dtb_d, 0, [[0, 128], [1, 32]]), [], [dtbias])
        DMA("sp", aneg[:], ap_(alog_d, 0, [[0, 128], [1, 32]]), [], [aneg])
        DMA("sp", dskip[:], ap_(dsk_d, 0, [[0, 128], [1, 16]]), [], [dskip])
        DMA("sp", tmpc[0:1, :], cb_d, [], [tmpc])
        CP("dve", cbrow[:], tmpc[0:1, :], [tmpc], [cbrow])
        DMAS("sp", cbcol[:], ap_(cb_d, 0, [[1, 128], [128, 12]]), [], [cbcol])
        for j in range(5):
            DMAS("sp", cwcol[:, j, :], ap_(cw_d, j * 1536, [[1, 128], [128, 12]]), [], [cwcol])
        CP("dve", ident[:], identf[:], [identf], [ident])
        MEMSET("pool", onesf[:], 1.0, [onesf])
        MEMSET("pool", onesb[:], 1.0, [onesb])
        MEMSET("pool", epsc[:], EPS, [epsc])
        MEMSET("pool", onec[:], 1.0, [onec])
        MEMSET("pool", npi[:], -math.pi, [npi])
        TS("dve", mnegf[:], tri[:], -1.0, -NEG, ALU.add, ALU.mult, [tri], [mnegf])
        TS("dve", mnegb[:], triu[:], -1.0, -NEG, ALU.add, ALU.mult, [triu], [mnegb])
        ACT(aneg[:], aneg[:], AF.Exp, [aneg], [aneg])
        TS("dve", aneg[:], aneg[:], -1.0, None, ALU.mult, None, [aneg], [aneg])
        for j in range(5):
            for ct in range(12):
                TS("dve", diag[:, j, ct, :], identf[:], cwcol[:, j, ct:ct + 1], None,
                   ALU.mult, None, [identf, cwcol], [diag])
        mq = P.sbuf("mq", [128, 2], F32, phSA)
        absq = P.sbuf("absq", [128, 64], F32, phSA)
        TS("dve", absq[:], gq[:], -1.0, None, ALU.mult, None, [gq], [absq])
        TT("dve", absq[:], absq[:], gq[:], ALU.max, [absq, gq], [absq])
        P.op("dve", lambda e: e.reduce_max(out=mq[:, 0:1], in_=absq[:], axis=AX.X), [absq], [mq])
        TS("dve", absq[:], gk[:], -1.0, None, ALU.mult, None, [gk], [absq])
        TT("dve", absq[:], absq[:], gk[:], ALU.max, [absq, gk], [absq])
        P.op("dve", lambda e: e.reduce_max(out=mq[:, 1:2], in_=absq[:], axis=AX.X), [absq], [mq])
        TT("dve", negB[:], mq[:, 0:1], mq[:, 1:2], ALU.mult, [mq], [negB])
        TS("dve", negB[:], negB[:], -8.0, None, ALU.mult, None, [negB], [negB])

        DMA("sp", gn1a[:], ap_(n1a_d, 0, [[0, 128], [1, D]]), [], [gn1a])

        def load_w(dst, dcol0, src_ap, ncols, stage, cw_max=256):
            K = dst.t.shape[1]
            i = 0
            for c0 in range(0, ncols, cw_max):
                cw = min(cw_max, ncols - c0)
                sg = stage[i % len(stage)]
                i += 1
                sv = sg[:, 0:K * cw].rearrange("p (k n) -> p k n", n=cw)
                DMA("sp", sv, src_ap[:, c0:c0 + cw].rearrange("(k p) n -> p k n", p=128), [], [sg])
                CP(cast_eng(), dst[:, :, dcol0 + c0:dcol0 + c0 + cw], sv, [sg], [dst])

        def rmsnorm_tok(x_ap, np_, gain_ap, out_ap, R, W, tmp):
            sq, ss = tmp["sq"], tmp["ss"]
            ACT(sq[0:np_, :], x_ap, AF.Square, R, [sq, ss], accum_out=ss[0:np_, 0:1])
            ACT(ss[0:np_, 0:1], ss[0:np_, 0:1], AF.Sqrt, [ss, epsc], [ss], scale=1.0 / D, bias=epsc[0:np_, :])
            RECIP(ss[0:np_, 0:1], ss[0:np_, 0:1], [ss], [ss])
            STT(out_ap, x_ap, ss[0:np_, 0:1], gain_ap, ALU.mult, ALU.mult, list(R) + [ss], W)

        def transpose8(src_ap_fn, dst_fn, R, W, pb_bf, tpb):
            for half in range(2):
                for k in range(4):
                    TR(pb_bf[:, k * 128:(k + 1) * 128], src_ap_fn(half * 4 + k), ident[:], list(R) + [ident], [tpb])
                CP("act" if half else "dve", dst_fn(half), pb_bf[:, 0:512].rearrange("p (k t) -> p k t", t=128), [tpb], W)

        def ssd_chunk(E, direction, adt_buf, ac0, dt_buf, dc0, xst, bt, bTt, cTt, H, y_init, y_dst):
            pb_st, st_acs, st_tot, pb_cv, pb_tok, accs, sm = (E[k] for k in
                                                              ("pb_st", "st_acs", "st_tot", "pb_cv", "pb_tok", "accs", "sm"))
            Rm, Em, Mm, xd, xdd, Hbf, ytmp, Htmp, cbt = (E[k] for k in
                                                        ("Rm", "Em", "Mm", "xd", "xdd", "Hbf", "ytmp", "Htmp", "cbt"))
            trm = tri if direction == 0 else triu
            mneg = mnegf if direction == 0 else mnegb
            aw = adt_buf.t.shape[1]
            dw = dt_buf.t.shape[1]
            adt16 = adt_buf[:, ac0:ac0 + 16]
            dt16 = dt_buf[:, dc0:dc0 + 16]
            v3 = lambda ap: ap.rearrange("p (h d) -> p h d", d=64)
            bcs = lambda buf, c0, n, rep: ap_(buf.t[:], c0, [[buf.t.shape[1], 128], [1, n], [0, rep]])
            MM(pb_st[:, 0:16], trm[:], adt16, True, True, [trm, adt_buf], [st_acs])
            MM(pb_st[:, 16:32], onesf[:], adt16, True, True, [onesf, adt_buf], [st_tot])
            CP("dve", sm["acs_sb"][:], pb_st[:, 0:16], [st_acs], [sm["acs_sb"]])
            TS("dve", sm["nacs"][:], sm["acs_sb"][:], -1.0, None, ALU.mult, None, [sm["acs_sb"]], [sm["nacs"]])
            ACT(sm["ea"][:], sm["acs_sb"][:], AF.Exp, [sm["acs_sb"]], [sm["ea"]])
            TT("dve", sm["ds"][:], pb_st[:, 16:32], sm["acs_sb"][:], ALU.subtract, [st_tot, sm["acs_sb"]], [sm["ds"]])
            ACT(sm["ds"][:], sm["ds"][:], AF.Exp, [sm["ds"]], [sm["ds"]])
            ACT(sm["edec"][:], pb_st[:, 16:32], AF.Exp, [st_tot], [sm["edec"]])
            TT("dve", sm["dtds"][:], dt16, sm["ds"][:], ALU.mult, [dt_buf, sm["ds"]], [sm["dtds"]])
            TT("dve", Rm[:], ap_(trm.t[:], 0, [[128, 128], [0, 16], [1, 128]]),
               ap_(adt_buf.t[:], ac0, [[aw, 128], [1, 16], [0, 128]]), ALU.mult, [trm, adt_buf], [Rm])
            for g in range(2):
                MM(pb_cv[:, g * 128:(g + 1) * 128], bTt[:, g * 128:(g + 1) * 128], cTt[:, g * 128:(g + 1) * 128],
                   True, True, [bTt, cTt], [pb_cv])
            CP("act", cbt[:], pb_cv[:, 0:256], [pb_cv], [cbt])
            for qd in range(4):
                bank = accs[qd]
                MM(bank[:, :], onesf[:], Rm[:, qd * 4:(qd + 1) * 4, :], True, False, [onesf, Rm], [bank])
                MM(bank[:, :], ident[:], ap_(mneg.t[:], 0, [[128, 128], [0, 4], [1, 128]]), False, True, [ident, mneg], [bank])
                for hh in range(4):
                    h = qd * 4 + hh
                    ACT(Em[:, h, :], bank[:, hh * 128:(hh + 1) * 128], AF.Exp, [bank, sm["nacs"]], [Em],
                        bias=sm["nacs"][:, h:h + 1])
            for g in range(2):
                TT("dve", Mm[:, g * 8:(g + 1) * 8, :], Em[:, g * 8:(g + 1) * 8, :],
                   ap_(cbt.t[:], g * 128, [[256, 128], [0, 8], [1, 128]]), ALU.mult, [Em, cbt], [Mm])
            TT("dve", v3(xd[:]), v3(xst[:]), ap_(dt_buf.t[:], dc0, [[dw, 128], [1, 16], [0, 64]]), ALU.mult, [xst, dt_buf], [xd])
            TT("dve", v3(xdd[:]), v3(xst[:]), bcs(sm["dtds"], 0, 16, 64), ALU.mult, [xst, sm["dtds"]], [xdd])
            CP("act", Hbf[:], H[:], [H], [Hbf])
            for g in range(2):
                MM(pb_tok[:, :], cTt[:, g * 128:(g + 1) * 128], Hbf[:, g * 512:(g + 1) * 512], True, True, [cTt, Hbf], [pb_tok])
                TT("dve", v3(ytmp[:, g * 512:(g + 1) * 512]), v3(pb_tok[:, :]), bcs(sm["ea"], g * 8, 8, 64), ALU.mult,
                   [pb_tok, sm["ea"]], [ytmp])
            TT("dve", ytmp[:], ytmp[:], y_init[:], ALU.add, [ytmp, y_init], [ytmp])
            for g in range(2):
                for hh in range(8):
                    h = g * 8 + hh
                    MM(pb_cv[:, hh * 64:(hh + 1) * 64], Mm[:, h, :], xd[:, h * 64:(h + 1) * 64], True, True, [Mm, xd], [pb_cv])
                TT("dve", y_dst[:, g * 512:(g + 1) * 512], ytmp[:, g * 512:(g + 1) * 512], pb_cv[:, :], ALU.add,
                   [ytmp, pb_cv], [y_dst])
            for g in range(2):
                MM(pb_tok[:, :], bt[:, g * 128:(g + 1) * 128], xdd[:, g * 512:(g + 1) * 512], True, True, [bt, xdd], [pb_tok])
                TT("dve", v3(Htmp[:, g * 512:(g + 1) * 512]), v3(H[:, g * 512:(g + 1) * 512]), bcs(sm["edec"], g * 8, 8, 64),
                   ALU.mult, [H, sm["edec"]], [Htmp])
                TT("dve", H[:, g * 512:(g + 1) * 512], Htmp[:, g * 512:(g + 1) * 512], pb_tok[:, :], ALU.add,
                   [Htmp, pb_tok], [H])

        with phSA as ph:
            WQ0, WKV0, WX0, WDT0 = 0, 512, 768, 2304
            Wa = P.sbuf("Wa", [128, 8, 2336], BF16, ph)
            stage = [P.sbuf(f"stg{i}", [128, 2048], F32, ph) for i in range(2)]
            for i in range(4):
                for g in range(2):
                    h = 4 * g + i
                    load_w(Wa, (2 * i + g) * 64, w_in_d[:, OQ + h * 64:OQ + (h + 1) * 64], 64, stage)
            load_w(Wa, WKV0, w_in_d[:, OK_:OK_ + 256], 256, stage)
            load_w(Wa, WX0, w_in_d[:, OXS:OXS + 1536], 1536, stage)
            load_w(Wa, WDT0, w_in_d[:, ODT:ODT + 32], 32, stage)

            tmp = {"sq": P.sbuf("sq", [128, D], F32, ph), "ss": P.sbuf("ss", [128, 8], F32, ph)}
            x_main = [P.sbuf(f"xm{i}", [128, D], F32, ph) for i in range(2)]
            x_halo = [P.sbuf(f"xh{i}", [4, D], F32, ph) for i in range(2)]
            hbuf = P.sbuf("hbuf", [128, 2, D], BF16, ph)
            hT = [P.sbuf(f"hT{i}", [128, 8, 132], BF16, ph) for i in range(2)]
            pre = P.sbuf("pre", [128, 12, 132], BF16, ph)
            xs_tok = [P.sbuf(f"xstok{i}", [128, 1024], BF16, ph) for i in range(2)]
            b_tok = [P.sbuf(f"btok{i}", [128, 256], BF16, ph) for i in range(2)]
            bT = [P.sbuf(f"bT{i}", [128, 256], BF16, ph) for i in range(2)]
            cT = [P.sbuf(f"cT{i}", [128, 256], BF16, ph) for i in range(2)]
            poss = P.sbuf("poss", [128, max(NS, 1), 2], F32, ph)
            poso = P.sbuf("poso", [128, NO, 2], F32, ph)
            mk = P.sbuf("mk", [128, max(NS, 1), 2], F32, ph)
            cos_t = P.sbuf("cos_t", [128, 32], F32, ph)
            sin_t = P.sbuf("sin_t", [128, 32], F32, ph)
            ang_t = P.sbuf("ang_t", [128, 32], F32, ph)
            rr_x = P.sbuf("rr_x", [128, 32], F32, ph)
            rr_k = P.sbuf("rr_k", [128, 32], F32, ph)
            rr_i = P.sbuf("rr_i", [128, 32], mybir.dt.int32, ph)
            qf = P.sbuf("qf", [128, 512], F32, ph)
            sq2 = P.sbuf("sq2", [128, 512], F32, ph)
            ss2 = P.sbuf("ss2", [128, 8], F32, ph)
            qn = P.sbuf("qn", [128, 512], F32, ph)
            t1 = P.sbuf("t1", [128, 128], F32, ph)
            t2 = P.sbuf("t2", [128, 128], F32, ph)
            sp_x = P.sbuf("sp_x", [128, 32], F32, ph)
            sp_a = P.sbuf("sp_a", [128, 32], F32, ph)
            sp_e = P.sbuf("sp_e", [128, 32], F32, ph)
            dt = P.sbuf("dt", [128, 32], F32, ph)
            adt = P.sbuf("adt", [128, 32], F32, ph)
            krot = [P.sbuf(f"krot{i}", [128, 128], BF16, ph) for i in range(2)]
            ktb = [P.sbuf(f"ktb{i}", [128, 128], BF16, ph) for i in range(2)]
            vb = [P.sbuf(f"vb{i}", [128, 128], BF16, ph) for i in range(2)]
            qrot = P.sbuf("qrot", [128, 512], BF16, ph)
            qtb = [P.sbuf(f"qtb{i}", [128, 512], BF16, ph) for i in range(2)]
            sm = {n_: P.sbuf(n_, [128, 16], F32, ph) for n_ in
                  ("ds", "offf", "offb", "tmp16", "dtds", "ea", "nacs", "acs_sb", "edec")}
            xdd = P.sbuf("xdd", [128, 1024], BF16, ph)
            xd = P.sbuf("xd", [128, 1024], BF16, ph)
            bsel = P.sbuf("bsel", [128, 2, 256], BF16, ph)
            Rm = P.sbuf("Rm", [128, 16, 128], F32, ph)
            Em = P.sbuf("Em", [128, 16, 128], BF16, ph)
            Mm = P.sbuf("Mm", [128, 16, 128], BF16, ph)
            Hbf = P.sbuf("Hbf", [128, 1024], BF16, ph)
            ytmp = P.sbuf("ytmp", [128, 1024], F32, ph)
            yout = [P.sbuf(f"yout{i}", [128, 1024], F32, ph) for i in range(2)]
            Htmp = P.sbuf("Htmp", [128, 1024], F32, ph)
            cbt = P.sbuf("cbt", [128, 256], F32, ph)

            pb_bf = P.psum("pb_bf", [128, 1024], BF16, ph)
            tpb = tph = tpk = tpq = pb_bf
            pb_tok = P.psum("pb_tok", [128, 512], F32, ph)
            pb_pre = P.psum("pb_pre", [128, 512], F32, ph)
            pb_st = Buf("pb_st", pb_pre.t[:, 400:512], root=pb_pre)
            st_acs = st_tot = pb_st
            prebank = [pb_pre] * 3
            pb_cv = P.psum("pb_cv", [128, 512], F32, ph)
            accs = [P.psum(f"acc{i}", [128, 512], F32, ph) for i in range(4)]

            if NS:
                DMA("sp", poss[:], poss_d, [], [poss])
                DMA("sp", mk[:], mk_d, [], [mk])
            DMA("sp", poso[:], poso_d, [], [poso])
            MEMSET("pool", sm["offf"][:], 0.0, [sm["offf"]])
            MEMSET("pool", sm["offb"][:], 0.0, [sm["offb"]])

            def rope_chunk(pos_buf, ti):
                nt = pos_buf.t.shape[1]
                for a in range(2):
                    TT("dve", ang_t[:, a * 16:(a + 1) * 16],
                       ap_(pos_buf.t[:], ti * 2 + a, [[nt * 2, 128], [0, 16]]),
                       invf[:, a * 16:(a + 1) * 16], ALU.mult, [pos_buf, invf], [ang_t])
                for dst, shift in ((sin_t, 0.0), (cos_t, 0.5 * math.pi)):
                    TS("dve", rr_x[:], ang_t[:], shift, None, ALU.add, None, [ang_t], [rr_x])
                    TS("dve", rr_i[:], rr_x[:], 1.0 / (2 * math.pi), None, ALU.mult, None, [rr_x], [rr_i])
                    CP("dve", rr_k[:], rr_i[:], [rr_i], [rr_k])
                    STT(dst[:], rr_k[:], -2 * math.pi, rr_x[:], ALU.mult, ALU.add, [rr_k, rr_x], [dst])
                    TS("dve", rr_k[:], dst[:], math.pi, -2 * math.pi, ALU.is_gt, ALU.mult, [dst], [rr_k])
                    TT("dve", dst[:], dst[:], rr_k[:], ALU.add, [dst, rr_k], [dst])
                    TS("dve", rr_k[:], dst[:], -math.pi, 2 * math.pi, ALU.is_lt, ALU.mult, [dst], [rr_k])
                    TT("dve", dst[:], dst[:], rr_k[:], ALU.add, [dst, rr_k], [dst])
                    ACT(dst[:], dst[:], AF.Sin, [dst], [dst])

            def qknorm_rope(src_ps, src_buf, H, gain, out_bf):
                n = H * 64
                CP("act", qf[:, 0:n], src_ps, [src_buf], [qf])
                TT("dve", sq2[:, 0:n], qf[:, 0:n], qf[:, 0:n], ALU.mult, [qf], [sq2])
                RSUM(ss2[:, 0:H], sq2[:, 0:n].rearrange("p (h d) -> p h d", d=64), [sq2], [ss2])
                ACT(ss2[:, 0:H], ss2[:, 0:H], AF.Sqrt, [ss2, epsc], [ss2], scale=1.0 / 64, bias=epsc[:])
                RECIP(ss2[:, 0:H], ss2[:, 0:H], [ss2], [ss2])
                TT("dve", qn[:, 0:n].rearrange("p (h d) -> p h d", d=64), qf[:, 0:n].rearrange("p (h d) -> p h d", d=64),
                   ap_(ss2.t[:], 0, [[8, 128], [1, H], [0, 64]]), ALU.mult, [qf, ss2], [qn])
                TT("dve", qn[:, 0:n].rearrange("p (h d) -> p h d", d=64), qn[:, 0:n].rearrange("p (h d) -> p h d", d=64),
                   ap_(gain.t[:], 0, [[64, 128], [0, H], [1, 64]]), ALU.mult, [qn, gain], [qn])
                ow = out_bf.t.shape[1]
                for a in range(2):
                    def xv(buf, half, wdt):
                        return ap_(buf.t[:], a * 32 + half * 16, [[wdt, 128], [64, H], [1, 16]])
                    cs = ap_(cos_t.t[:], a * 16, [[32, 128], [0, H], [1, 16]])
                    sn = ap_(sin_t.t[:], a * 16, [[32, 128], [0, H], [1, 16]])
                    t1v = ap_(t1.t[:], 0, [[128, 128], [16, H], [1, 16]])
                    t2v = ap_(t2.t[:], 0, [[128, 128], [16, H], [1, 16]])
                    TT("dve", t1v, xv(qn, 0, 512), cs, ALU.mult, [qn, cos_t], [t1])
                    TT("dve", t2v, xv(qn, 1, 512), sn, ALU.mult, [qn, sin_t], [t2])
                    TT("dve", xv(out_bf, 0, ow), t1v, t2v, ALU.subtract, [t1, t2], [out_bf])
                    TT("dve", t1v, xv(qn, 1, 512), cs, ALU.mult, [qn, cos_t], [t1])
                    TT("dve", t2v, xv(qn, 0, 512), sn, ALU.mult, [qn, sin_t], [t2])
                    TT("dve", xv(out_bf, 1, ow), t1v, t2v, ALU.add, [t1, t2], [out_bf])

            def front(x_src_ap, hTi, xm, xh):
                DMA("sp", xm[:], x_src_ap[2:130, :], [], [xm])
                DMA("sp", xh[0:2, :], x_src_ap[0:2, :], [], [xh])
                DMA("sp", xh[2:4, :], x_src_ap[130:132, :], [], [xh])
                rmsnorm_tok(xm[:], 128, gn1a[:], hbuf[:, 0, :], [xm, gn1a], [hbuf], tmp)
                rmsnorm_tok(xh[0:4, :], 4, gn1a[0:4, :], hbuf[0:4, 1, :], [xh, gn1a], [hbuf], tmp)
                transpose8(lambda kk: hbuf[:, 0, kk * 128:(kk + 1) * 128],
                           lambda half: hTi[:, half * 4:(half + 1) * 4, 2:130], [hbuf], [hTi], pb_bf, tpb)
                for kk in range(8):
                    TR(pb_bf[:, 512 + kk * 4:512 + (kk + 1) * 4], hbuf[0:4, 1, kk * 128:(kk + 1) * 128], ident[0:4, 0:4],
                       [hbuf, ident], [tph])
                hv = pb_bf[:, 512:544].rearrange("p (k t) -> p k t", t=4)
                CP("dve", hTi[:, :, 0:2], hv[:, :, 0:2], [tph], [hTi])
                CP("dve", hTi[:, :, 130:132], hv[:, :, 2:4], [tph], [hTi])

            def featmaj_pre(hTi, ntiles):
                for ct in range(ntiles):
                    pbk = pb_pre if ct % 2 == 0 else pb_cv
                    pv = pbk[:, 0:132]
                    for k in range(8):
                        MM(pv, Wa[:, k, WX0 + ct * 128:WX0 + (ct + 1) * 128], hTi[:, k, 0:132], k == 0, k == 7, [Wa, hTi], [pbk])
                    CP("act" if ct % 2 else "dve", pre[:, ct, :], pv, [pbk], [pre])

            def conv_tok(tiles, cb_col0, ncols):
                MM(pb_cv[:, 0:ncols], onesb[0:1, 0:128], cbrow[0:1, cb_col0:cb_col0 + ncols], True, False, [onesb, cbrow], [pb_cv])
                n = len(tiles)
                for i, ct in enumerate(tiles):
                    for j in range(5):
                        MM(pb_cv[:, i * 128:(i + 1) * 128], pre[:, ct, j:j + 128], diag[:, j, ct, :], False,
                           (i == n - 1 and j == 4), [pre, diag], [pb_cv])

            def softplus_dt():
                TT("dve", sp_x[:], pb_tok[:, 256:288], dtbias[:], ALU.add, [pb_tok, dtbias], [sp_x])
                TS("dve", sp_a[:], sp_x[:], -1.0, None, ALU.mult, None, [sp_x], [sp_a])
                TT("dve", sp_a[:], sp_a[:], sp_x[:], ALU.max, [sp_a, sp_x], [sp_a])
                ACT(sp_e[:], sp_a[:], AF.Exp, [sp_a], [sp_e], scale=-1.0)
                ACT(sp_e[:], sp_e[:], AF.Ln, [sp_e, onec], [sp_e], bias=onec[:])
                TS("dve", sp_x[:], sp_x[:], 0.0, None, ALU.max, None, [sp_x], [sp_x])
                TT("dve", dt[:], sp_x[:], sp_e[:], ALU.add, [sp_x, sp_e], [dt])

            def kv_mm(hTi):
                for k in range(8):
                    MM(pb_tok[:, 0:256], hTi[:, k, 2:130], Wa[:, k, WKV0:WKV0 + 256], k == 0, k == 7, [hTi, Wa], [pb_tok])
                for k in range(8):
                    MM(pb_tok[:, 256:288], hTi[:, k, 2:130], Wa[:, k, WDT0:WDT0 + 32], k == 0, k == 7, [hTi, Wa], [pb_tok])

            def kv_post(blk, r):
                CP("act", vb[r][:], pb_tok[:, 128:256], [pb_tok], [vb[r]])
                DMA("pool", s_v[blk], vb[r][:], [vb[r]], [D_v[blk]])
                qknorm_rope(pb_tok[:, 0:128], pb_tok, 2, gk, krot[r])
                TR(pb_bf[:, 640:768], krot[r][:], ident[:], [krot[r], ident], [tpk])
                CP("act", ktb[r][:], pb_bf[:, 640:768], [tpk], [ktb[r]])
                DMA("pool", s_kt[blk], ktb[r][:], [ktb[r]], [D_kt[blk]])

            def xs_b_tok(xst, bt):
                for half in range(2):
                    conv_tok(list(range(half * 4, half * 4 + 4)), half * 512, 512)
                    ACT(xst[:, half * 512:(half + 1) * 512], pb_cv[:, 0:512], AF.Silu, [pb_cv], [xst])
                conv_tok([8, 9], 1024, 256)
                ACT(bt[:], pb_cv[:, 0:256], AF.Silu, [pb_cv], [bt])

            bc16 = lambda buf, c0, n, rep: ap_(buf.t[:], c0, [[buf.t.shape[1], 128], [1, n], [0, rep]])
            v3 = lambda ap: ap.rearrange("p (h d) -> p h d", d=64)

            for s in range(NS):
                r = s % 2
                hTi = hT[r]
                front(xs_d[s], hTi, x_main[r], x_halo[r])
                rope_chunk(poss, s)
                kv_mm(hTi)
                featmaj_pre(hTi, 10)
                softplus_dt()
                kv_post(NO + s, r)
                for a in range(2):
                    TS("dve", dt[:, a * 16:(a + 1) * 16], dt[:, a * 16:(a + 1) * 16], mk[:, s, a:a + 1], None,
                       ALU.mult, None, [dt, mk], [dt])
                TT("dve", adt[:], dt[:], aneg[:], ALU.mult, [dt, aneg], [adt])
                MM(pb_st[:, 0:16], tri[:], adt[:, 0:16], True, False, [tri, adt], [st_acs])
                MM(pb_st[:, 0:16], triu[:], adt[:, 16:32], False, True, [triu, adt], [st_acs])
                MM(pb_st[:, 16:32], onesf[:], adt[:, 0:16], True, False, [onesf, adt], [st_tot])
                MM(pb_st[:, 16:32], onesf[:], adt[:, 16:32], False, True, [onesf, adt], [st_tot])
                TS("dve", sm["tmp16"][:], sm["offf"][:], mk[:, s, 0:1], None, ALU.mult, None, [sm["offf"], mk], [sm["tmp16"]])
                STT(sm["tmp16"][:], sm["offb"][:], mk[:, s, 1:2], sm["tmp16"][:], ALU.mult, ALU.add,
                    [sm["offb"], mk, sm["tmp16"]], [sm["tmp16"]])
                TT("dve", sm["ds"][:], pb_st[:, 16:32], sm["tmp16"][:], ALU.add, [st_tot, sm["tmp16"]], [sm["ds"]])
                TT("dve", sm["ds"][:], sm["ds"][:], pb_st[:, 0:16], ALU.subtract, [sm["ds"], st_acs], [sm["ds"]])
                ACT(sm["ds"][:], sm["ds"][:], AF.Exp, [sm["ds"]], [sm["ds"]])
                TT("dve", sm["dtds"][:], dt[:, 0:16], dt[:, 16:32], ALU.add, [dt], [sm["dtds"]])
                TT("dve", sm["dtds"][:], sm["dtds"][:], sm["ds"][:], ALU.mult, [sm["dtds"], sm["ds"]], [sm["dtds"]])
                STT(sm["offf"][:], pb_st[:, 16:32], mk[:, s, 0:1], sm["offf"][:], ALU.mult, ALU.add,
                    [st_tot, mk, sm["offf"]], [sm["offf"]])
                STT(sm["offb"][:], pb_st[:, 16:32], mk[:, s, 1:2], sm["offb"][:], ALU.mult, ALU.add,
                    [st_tot, mk, sm["offb"]], [sm["offb"]])
                xst, bt = xs_tok[r], b_tok[r]
                xs_b_tok(xst, bt)
                TT("dve", v3(xdd[:]), v3(xst[:]), bc16(sm["dtds"], 0, 16, 64), ALU.mult, [xst, sm["dtds"]], [xdd])
                for a in range(2):
                    TS("dve", bsel[:, a, :], bt[:], mk[:, s, a:a + 1], None, ALU.mult, None, [bt, mk], [bsel])
                for a in range(2):
                    for g in range(2):
                        MM(accs[a * 2 + g][:, :], bsel[:, a, g * 128:(g + 1) * 128], xdd[:, g * 512:(g + 1) * 512],
                           s == 0, s == NS - 1, [bsel, xdd], [accs[a * 2 + g]])
            if NS:
                for g in range(2):
                    CP("dve", Hf[:, g * 512:(g + 1) * 512], accs[g][:, :], [accs[g]], [Hf])
                    CP("act", Hb[:, g * 512:(g + 1) * 512], accs[2 + g][:, :], [accs[2 + g]], [Hb])
            else:
                MEMSET("pool", Hf[:], 0.0, [Hf])
                MEMSET("pool", Hb[:], 0.0, [Hb])

            for j in range(NO):
                r = j % 2
                hTi = hT[r]
                front(xo_d[j], hTi, x_main[r], x_halo[r])
                rope_chunk(poso, j)
                kv_mm(hTi)
                for k in range(8):
                    MM(accs[0][:, :], hTi[:, k, 2:130], Wa[:, k, WQ0:WQ0 + 512], k == 0, k == 7, [hTi, Wa], [accs[0]])
                featmaj_pre(hTi, 12)
                softplus_dt()
                kv_post(j, r)
                qknorm_rope(accs[0][:, :], accs[0], 8, gq, qrot)
                for i in range(4):
                    TR(pb_bf[:, 768:896], qrot[:, i * 128:(i + 1) * 128], ident[:], [qrot, ident], [tpq])
                    CP("act", qtb[r][:, i * 128:(i + 1) * 128], pb_bf[:, 768:896], [tpq], [qtb[r]])
                DMA("pool", s_qt[j], qtb[r][:], [qtb[r]], [D_qt[j]])
                CP("dve", dtb_own[:, j, :], dt[:, 16:32], [dt], [dtb_own])
                TT("dve", adt[:], dt[:], aneg[:], ALU.mult, [dt, aneg], [adt])
                xst, bt, bTt, cTt = xs_tok[r], b_tok[r], bT[r], cT[r]
                xs_b_tok(xst, bt)
                for i, (ct, dstb) in enumerate(((8, bTt), (9, bTt), (10, cTt), (11, cTt))):
                    for jj in range(5):
                        MM(pb_cv[:, i * 128:(i + 1) * 128], diag[:, jj, ct, :], pre[:, ct, jj:jj + 128], jj == 0, jj == 4,
                           [pre, diag], [pb_cv])
                    ACT(dstb[:, (i % 2) * 128:(i % 2 + 1) * 128], pb_cv[:, i * 128:(i + 1) * 128], AF.Silu, [pb_cv, cbcol], [dstb],
                        bias=cbcol[:, ct:ct + 1])
                DMA("pool", s_xs[j], xst[:], [xst], [D_xs[j]])
                DMA("pool", s_bt[j], bt[:], [bt], [D_bt[j]])
                DMA("pool", s_bT[j], bTt[:], [bTt], [D_bT[j]])
                DMA("pool", s_cT[j], cTt[:], [cTt], [D_cT[j]])
                yo = yout[r]
                TT("dve", v3(yo[:]), v3(xst[:]), bc16(dskip, 0, 16, 64), ALU.mult, [xst, dskip], [yo])
                ssd_env = dict(pb_st=pb_st, st_acs=st_acs, st_tot=st_tot, pb_cv=pb_cv, pb_tok=pb_tok, accs=accs, sm=sm,
                               Rm=Rm, Em=Em, Mm=Mm, xd=xd, xdd=xdd, Hbf=Hbf, ytmp=ytmp, Htmp=Htmp, cbt=cbt)
                ssd_chunk(ssd_env, 0, adt, 0, dt, 0, xst, bt, bTt, cTt, Hf, yo, yo)
                DMA("pool", s_yf[j], yo[:], [yo], [D_yf[j]])
        P.barrier()

        with ExitStack() as ph:
            KT = P.sbuf("KT", [128, NB, 128], BF16, ph)
            Vs = P.sbuf("Vs", [128, NB, 2, 65], BF16, ph)
            QT = P.sbuf("QT", [128, 4, NO, 128], BF16, ph)
            MEMSET("pool", Vs[:], 1.0, [Vs])
            for b0 in range(0, NB, 16):
                b1 = min(NB, b0 + 16)
                DMA("sp", KT[:, b0:b1, :], s_kt[b0:b1].rearrange("b p k -> p b k"), D_kt[b0:b1], [KT])
                for g in range(2):
                    DMA("sp", Vs[:, b0:b1, g, 0:64], s_v[b0:b1, :, g * 64:(g + 1) * 64].rearrange("b p d -> p b d"),
                        D_v[b0:b1], [Vs])
            for i in range(4):
                for j0 in range(0, NO, 16):
                    j1 = min(NO, j0 + 16)
                    DMA("sp", QT[:, i, j0:j1, :], s_qt[j0:j1, :, i * 128:(i + 1) * 128].rearrange("j p t -> p j t"),
                        D_qt[j0:j1], [QT])
            Wp = P.sbuf("Wp", [64, 8, D], BF16, ph)
            stg = P.sbuf("stgp", [64, 8, D], F32, ph)
            DMA("sp", stg[:], wap_d.rearrange("(h p) n -> p h n", p=64), [], [stg])
            CP("dve", Wp[:], stg[:], [stg], [Wp])
            KG = 2 if NB % 2 == 0 else 1
            PT = [P.sbuf(f"PT{i}", [128, KG, 512], BF16, ph) for i in range(3)]
            oacc = P.sbuf("oacc", [65, 512], F32, ph)
            rrow = P.sbuf("rrow", [65, 512], F32, ph)
            OTn = P.sbuf("OTn", [64, 8, 512], BF16, ph)
            aosb = [P.sbuf(f"aosb{i}", [128, D], F32, ph) for i in range(2)]
            ps_s = [P.psum(f"ps_s{i}", [128, KG, 512], F32, ph) for i in range(2)]
            ps_o = [P.psum(f"ps_o{i}", [128, 512], F32, ph) for i in range(2)]
            ps_bc = P.psum("ps_bc", [128, 512], F32, ph)
            ps_pj = [P.psum(f"ps_pj{i}", [128, 512], F32, ph) for i in range(1)]
            it = 0
            CQ = QW // 128
            for qt in range(NQT):
                c0 = qt * CQ
                for h in range(8):
                    g, i = h // 4, h % 4
                    po = ps_o[h % 2]
                    for kb0 in range(0, NB, KG):
                        ss_ = ps_s[it % 2]
                        pt = PT[it % 3]
                        it += 1
                        for kk in range(KG):
                            MM(ss_[:, kk, 0:QW], KT[g * 64:(g + 1) * 64, kb0 + kk, :],
                               QT[g * 64:(g + 1) * 64, i, c0:c0 + CQ, :], True, True, [KT, QT], [ss_])
                        ACT(pt[:, :, 0:QW], ss_[:, :, 0:QW], AF.Exp, [ss_, negB], [pt], scale=0.125, bias=negB[:])
                        for kk in range(KG):
                            kb = kb0 + kk
                            MM(po[0:65, 0:QW], Vs[:, kb, g, :], pt[:, kk, 0:QW], kb == 0, kb == NB - 1, [Vs, pt], [po])
                    CP("dve", oacc[:, 0:QW], po[0:65, 0:QW], [po], [oacc])
                    RECIP(rrow[64:65, 0:QW], oacc[64:65, 0:QW], [oacc], [rrow])
                    MM(ps_bc[0:64, 0:QW], onesf[64:65, 0:64], rrow[64:65, 0:QW], True, True, [onesf, rrow], [ps_bc])
                    TT("dve", OTn[:, h, 0:QW], oacc[0:64, 0:QW], ps_bc[0:64, 0:QW], ALU.mult, [oacc, ps_bc], [OTn])
                for tt in range(CQ):
                    cj = c0 + tt
                    ao = aosb[cj % 2]
                    for half in range(2):
                        pj = ps_pj[0]
                        for h in range(8):
                            MM(pj[:, :], OTn[:, h, tt * 128:(tt + 1) * 128], Wp[:, h, half * 512:(half + 1) * 512],
                               h == 0, h == 7, [OTn, Wp], [pj])
                        CP("act" if half else "dve", ao[:, half * 512:(half + 1) * 512], pj[:, :], [pj], [ao])
                    DMA("pool", s_ao[cj], ao[:], [ao], [D_ao[cj]])
        P.barrier()

        with ExitStack() as ph:
            Wz = P.sbuf("Wz", [128, 8, 1024], BF16, ph)
            stage = [P.sbuf(f"stgb{i}", [128, 2048], F32, ph) for i in range(2)]
            load_w(Wz, 0, w_in_d[:, OZ:OZ + 1024], 1024, stage)
            gsn = P.sbuf("gsn", [128, D], F32, ph)
            DMA("sp", gsn[:], ap_(sn_d, 0, [[0, 128], [1, D]]), [], [gsn])
            tmp = {"sq": P.sbuf("sqb", [128, D], F32, ph), "ss": P.sbuf("ssb", [128, 8], F32, ph)}
            x_main = [P.sbuf(f"xmb{i}", [128, D], F32, ph) for i in range(2)]
            hbuf = P.sbuf("hbufb", [128, D], BF16, ph)
            hTb = P.sbuf("hTb", [128, 8, 128], BF16, ph)
            xs_tok = [P.sbuf(f"xstokb{i}", [128, 1024], BF16, ph) for i in range(2)]
            b_tok = [P.sbuf(f"btokb{i}", [128, 256], BF16, ph) for i in range(2)]
            bT = [P.sbuf(f"bTb{i}", [128, 256], BF16, ph) for i in range(2)]
            cT = [P.sbuf(f"cTb{i}", [128, 256], BF16, ph) for i in range(2)]
            yfb = [P.sbuf(f"yfb{i}", [128, 1024], F32, ph) for i in range(2)]
            dtB = P.sbuf("dtB", [128, 16], F32, ph)
            adtB = P.sbuf("adtB", [128, 16], F32, ph)
            sm = {n_: P.sbuf(n_ + "B", [128, 16], F32, ph) for n_ in
                  ("ds", "dtds", "ea", "nacs", "acs_sb", "edec")}
            env = dict(
                pb_st=P.psum("pb_stB", [128, 512], F32, ph), pb_cv=P.psum("pb_cvB", [128, 512], F32, ph),
                pb_tok=P.psum("pb_tokB", [128, 512], F32, ph),
                accs=[P.psum(f"accB{i}", [128, 512], F32, ph) for i in range(4)], sm=sm,
                Rm=P.sbuf("RmB", [128, 16, 128], F32, ph), Em=P.sbuf("EmB", [128, 16, 128], BF16, ph),
                Mm=P.sbuf("MmB", [128, 16, 128], BF16, ph), xd=P.sbuf("xdB", [128, 1024], BF16, ph),
                xdd=P.sbuf("xddB", [128, 1024], BF16, ph), Hbf=P.sbuf("HbfB", [128, 1024], BF16, ph),
                ytmp=P.sbuf("ytmpB", [128, 1024], F32, ph), Htmp=P.sbuf("HtmpB", [128, 1024], F32, ph),
                cbt=P.sbuf("cbtB", [128, 256], F32, ph))
            env["st_acs"] = env["pb_st"]
            env["st_tot"] = env["pb_st"]
            pb_bf = P.psum("pb_bfB", [128, 1024], BF16, ph)
            tpb = pb_bf
            pb_tok = env["pb_tok"]
            yout = P.sbuf("youtB", [128, 1024], F32, ph)
            zs = P.sbuf("zs", [128, 1024], F32, ph)
            ybf = [P.sbuf(f"ybf{i}", [128, 1024], BF16, ph) for i in range(2)]
            for jj in range(NO):
                j = NO - 1 - jj
                r = jj % 2
                xm = x_main[r]
                DMA("sp", xm[:], xo_d[j][2:130, :], [], [xm])
                DMA("sp", xs_tok[r][:], s_xs[j], [D_xs[j]], [xs_tok[r]])
                DMA("sp", b_tok[r][:], s_bt[j], [D_bt[j]], [b_tok[r]])
                DMA("sp", bT[r][:], s_bT[j], [D_bT[j]], [bT[r]])
                DMA("sp", cT[r][:], s_cT[j], [D_cT[j]], [cT[r]])
                DMA("sp", yfb[r][:], s_yf[j], [D_yf[j]], [yfb[r]])
                CP("dve", dtB[:], dtb_own[:, j, :], [dtb_own], [dtB])
                TT("dve", adtB[:], dtB[:], aneg[:, 16:32], ALU.mult, [dtB, aneg], [adtB])
                ssd_chunk(env, 1, adtB, 0, dtB, 0, xs_tok[r], b_tok[r], bT[r], cT[r], Hb, yfb[r], yout)
                rmsnorm_tok(xm[:], 128, gn1a[:], hbuf[:], [xm, gn1a], [hbuf], tmp)
                transpose8(lambda kk: hbuf[:, kk * 128:(kk + 1) * 128], lambda half: hTb[:, half * 4:(half + 1) * 4, :],
                           [hbuf], [hTb], pb_bf, tpb)
                for half in range(2):
                    for k in range(8):
                        MM(pb_tok[:, :], hTb[:, k, :], Wz[:, k, half * 512:(half + 1) * 512], k == 0, k == 7, [hTb, Wz], [pb_tok])
                    ACT(zs[:, half * 512:(half + 1) * 512], pb_tok[:, :], AF.Silu, [pb_tok], [zs])
                TT("dve", yout[:], yout[:], zs[:], ALU.mult, [yout, zs], [yout])
                rmsnorm_tok(yout[:], 128, gsn[:], ybf[r][:], [yout, gsn], [ybf[r]], tmp)
                DMA("pool", s_yb[j], ybf[r][:], [ybf[r]], [D_yb[j]])
        P.barrier()

        with ExitStack() as ph:
            Wg = P.sbuf("Wg", [128, 8, 2048], BF16, ph)
            Wsp = P.sbuf("Wsp", [128, 8, D], BF16, ph)
            Wo = P.sbuf("Wo", [128, 8, D], BF16, ph)
            stage = [P.sbuf(f"stgc{i}", [128, 2048], F32, ph) for i in range(2)]
            load_w(Wg, 0, w_in_d[:, OG:OG + 2048], 2048, stage)
            load_w(Wsp, 0, wsp_d, D, stage)
            load_w(Wo, 0, wo_d, D, stage)
            gn1b = P.sbuf("gn1b", [128, D], F32, ph)
            DMA("sp", gn1b[:], ap_(n1b_d, 0, [[0, 128], [1, D]]), [], [gn1b])
            tmp = {"sq": P.sbuf("sqc", [128, D], F32, ph), "ss": P.sbuf("ssc", [128, 8], F32, ph)}
            x_main = [P.sbuf(f"xmc{i}", [128, D], F32, ph) for i in range(2)]
            aob = [P.sbuf(f"aob{i}", [128, D], F32, ph) for i in range(2)]
            ybf = [P.sbuf(f"ybfc{i}", [128, D], BF16, ph) for i in range(2)]
            hbuf = P.sbuf("hbufc", [128, D], BF16, ph)
            hTb = P.sbuf("hTc", [128, 8, 128], BF16, ph)
            yT = P.sbuf("yT", [128, 8, 128], BF16, ph)
            mT = P.sbuf("mT", [128, 8, 128], BF16, ph)
            gsig = P.sbuf("gsig", [128, 2048], F32, ph)
            mix = P.sbuf("mix", [128, D], F32, ph)
            sso = P.sbuf("sso", [128, D], F32, ph)
            mixbf = P.sbuf("mixbf", [128, D], BF16, ph)
            x1 = [P.sbuf(f"x1_{i}", [128, D], F32, ph) for i in range(2)]
            pb_bf = P.psum("pb_bfC", [128, 1024], BF16, ph)
            tpb = pb_bf
            pbs = [P.psum(f"pbC{i}", [128, 512], F32, ph) for i in range(4)]
            for j in range(NO):
                r = j % 2
                xm = x_main[r]
                DMA("sp", xm[:], xo_d[j][2:130, :], [], [xm])
                DMA("sp", aob[r][:], s_ao[j], [D_ao[j]], [aob[r]])
                DMA("sp", ybf[r][:], s_yb[j], [D_yb[j]], [ybf[r]])
                rmsnorm_tok(xm[:], 128, gn1a[:], hbuf[:], [xm, gn1a], [hbuf], tmp)
                transpose8(lambda kk: hbuf[:, kk * 128:(kk + 1) * 128], lambda half: hTb[:, half * 4:(half + 1) * 4, :],
                           [hbuf], [hTb], pb_bf, tpb)
                for qd in range(4):
                    for k in range(8):
                        MM(pbs[qd][:, :], hTb[:, k, :], Wg[:, k, qd * 512:(qd + 1) * 512], k == 0, k == 7, [hTb, Wg], [pbs[qd]])
                    ACT(gsig[:, qd * 512:(qd + 1) * 512], pbs[qd][:, :], AF.Sigmoid, [pbs[qd]], [gsig])
                transpose8(lambda kk: ybf[r][:, kk * 128:(kk + 1) * 128], lambda half: yT[:, half * 4:(half + 1) * 4, :],
                           [ybf[r]], [yT], pb_bf, tpb)
                TT("dve", mix[:], aob[r][:], gsig[:, 0:1024], ALU.mult, [aob[r], gsig], [mix])
                for half in range(2):
                    pb = pbs[half]
                    for k in range(8):
                        MM(pb[:, :], yT[:, k, :], Wsp[:, k, half * 512:(half + 1) * 512], k == 0, k == 7, [yT, Wsp], [pb])
                    TT("dve", sso[:, half * 512:(half + 1) * 512], pb[:, :], gsig[:, 1024 + half * 512:1024 + (half + 1) * 512],
                       ALU.mult, [pb, gsig], [sso])
                TT("dve", mixbf[:], mix[:], sso[:], ALU.add, [mix, sso], [mixbf])
                transpose8(lambda kk: mixbf[:, kk * 128:(kk + 1) * 128], lambda half: mT[:, half * 4:(half + 1) * 4, :],
                           [mixbf], [mT], pb_bf, tpb)
                for half in range(2):
                    pb = pbs[2 + half]
                    for k in range(8):
                        MM(pb[:, :], mT[:, k, :], Wo[:, k, half * 512:(half + 1) * 512], k == 0, k == 7, [mT, Wo], [pb])
                    CP("act", mix[:, half * 512:(half + 1) * 512], pb[:, :], [pb], [mix])
                rmsnorm_tok(mix[:], 128, gn1b[:], sso[:], [mix, gn1b], [sso], tmp)
                TT("dve", x1[r][:], sso[:], xm[:], ALU.add, [sso, xm], [x1[r]])
                DMA("pool", s_x1[j], x1[r][:], [x1[r]], [D_x1[j]])
        P.barrier()

        NF = D_FF // 128
        NFH = NF // 2
        UW = min(512, T)
        NT = UW // 128
        for hf in range(2):
            with ExitStack() as ph:
                Wgu = P.sbuf("Wgu", [128, 8, 2 * NFH * 128], BF16, ph)
                Wd = P.sbuf("Wd", [128, NFH, D], BF16, ph)
                stage = [P.sbuf(f"stgd{i}", [128, 2816], F32, ph) for i in range(2)]
                f0 = hf * NFH * 128
                load_w(Wgu, 0, wgu_d[:, f0:f0 + NFH * 128], NFH * 128, stage)
                load_w(Wgu, NFH * 128, wgu_d[:, D_FF + f0:D_FF + f0 + NFH * 128], NFH * 128, stage)
                load_w(Wd, 0, wd_d[f0:f0 + NFH * 128, :], D, stage)
                gn2a = P.sbuf("gn2a", [128, D], F32, ph)
                gn2b = P.sbuf("gn2b", [128, D], F32, ph)
                DMA("sp", gn2a[:], ap_(n2a_d, 0, [[0, 128], [1, D]]), [], [gn2a])
                DMA("sp", gn2b[:], ap_(n2b_d, 0, [[0, 128], [1, D]]), [], [gn2b])
                tmp = {"sq": P.sbuf("sqd", [128, D], F32, ph), "ss": P.sbuf("ssd", [128, 8], F32, ph)}
                x1b = [P.sbuf(f"x1c{i}", [128, NT, D], F32, ph) for i in range(2)]
                hb = P.sbuf("hbc", [128, D], BF16, ph)
                h2T = P.sbuf("h2T", [128, 8, UW], BF16, ph)
                gact = P.sbuf("gact", [128, UW], F32, ph)
                actT = P.sbuf("actT", [128, NFH, UW], BF16, ph)
                ffn = [P.sbuf(f"ffn{i}", [128, D], F32, ph) for i in range(2)]
                ffp = [P.sbuf(f"ffp{i}", [128, D], F32, ph) for i in range(2)]
                ob = [P.sbuf(f"ob{i}", [128, D], F32, ph) for i in range(2)]
                pb_bf = P.psum("pb_bfD", [128, 1024], BF16, ph)
                tpb = pb_bf
                pg = [P.psum(f"pg{i}", [128, 512], F32, ph) for i in range(2)]
                pu = [P.psum(f"pu{i}", [128, 512], F32, ph) for i in range(2)]
                pd = [P.psum(f"pd{i}", [128, 512], F32, ph) for i in range(2)]
                for u in range(T // UW):
                    xb = x1b[u % 2]
                    for t in range(NT):
                        cj = u * NT + t
                        DMA("sp", xb[:, t, :], s_x1[cj], [D_x1[cj]], [xb])
                    for t in range(NT):
                        rmsnorm_tok(xb[:, t, :], 128, gn2a[:], hb[:], [xb, gn2a], [hb], tmp)
                        transpose8(lambda kk: hb[:, kk * 128:(kk + 1) * 128],
                                   lambda half: h2T[:, half * 4:(half + 1) * 4, t * 128:(t + 1) * 128], [hb], [h2T], pb_bf, tpb)
                    for f in range(NFH):
                        g_, u_ = pg[f % 2], pu[f % 2]
                        for k in range(8):
                            MM(g_[:, 0:UW], Wgu[:, k, f * 128:(f + 1) * 128], h2T[:, k, :], k == 0, k == 7, [Wgu, h2T], [g_])
                        for k in range(8):
                            MM(u_[:, 0:UW], Wgu[:, k, (NFH + f) * 128:(NFH + f + 1) * 128], h2T[:, k, :], k == 0, k == 7,
                               [Wgu, h2T], [u_])
                        ACT(gact[:, :], g_[:, 0:UW], AF.Silu, [g_], [gact])
                        TT("dve", actT[:, f, :], gact[:, :], u_[:, 0:UW], ALU.mult, [gact, u_], [actT])
                    for t in range(NT):
                        cj = u * NT + t
                        ff = ffn[cj % 2]
                        if hf == 1:
                            DMA("sp", ffp[cj % 2][:], s_ff[cj], [D_ff[cj]], [ffp[cj % 2]])
                        for half in range(2):
                            p_ = pd[half]
                            for f in range(NFH):
                                MM(p_[:, :], actT[:, f, t * 128:(t + 1) * 128], Wd[:, f, half * 512:(half + 1) * 512],
                                   f == 0, f == NFH - 1, [actT, Wd], [p_])
                            if hf == 0:
                                CP("act", ff[:, half * 512:(half + 1) * 512], p_[:, :], [p_], [ff])
                            else:
                                TT("dve", ff[:, half * 512:(half + 1) * 512], p_[:, :], ffp[cj % 2][:, half * 512:(half + 1) * 512],
                                   ALU.add, [p_, ffp[cj % 2]], [ff])
                        if hf == 0:
                            DMA("pool", s_ff[cj], ff[:], [ff], [D_ff[cj]])
                        else:
                            o_ = ob[cj % 2]
                            rmsnorm_tok(ff[:], 128, gn2b[:], o_[:], [ff, gn2b], [o_], tmp)
                            TT("dve", o_[:], o_[:], xb[:, t, :], ALU.add, [o_, xb], [o_])
                            DMA("pool", out_d[cj], o_[:], [o_], [D_out[cj]])
            P.barrier()
        P.emit()
    return nc


def make_in_maps(inputs, S, T, n_cores):
    x = np.asarray(inputs["x"], dtype=np.float32)
    B = x.shape[0]
    NQ = S // T
    NO = T // 128
    NS = (S - T) // 128
    f32 = lambda a: np.ascontiguousarray(np.asarray(a, dtype=np.float32))
    common = {
        "w_in": f32(inputs["w_in"][0]),
        "q_norm": f32(inputs["q_norm"][0])[None, :],
        "k_norm": f32(inputs["k_norm"][0])[None, :],
        "conv_w": f32(inputs["conv_w"][0]),
        "conv_b": f32(inputs["conv_b"][0])[None, :],
        "dt_bias": f32(np.concatenate([inputs["dt_bias_f"][0], inputs["dt_bias_b"][0]]))[None, :],
        "a_log": f32(np.concatenate([inputs["a_log_f"][0], inputs["a_log_b"][0]]))[None, :],
        "d_skip": f32(inputs["d_skip"][0])[None, :],
        "ssd_norm": f32(inputs["ssd_norm"][0])[None, :],
        "w_attn_proj": f32(inputs["w_attn_proj"][0]),
        "w_ssd_proj": f32(inputs["w_ssd_proj"][0]),
        "w_out": f32(inputs["w_out"][0]),
        "norm1_pre": f32(inputs["norm1_pre"][0])[None, :],
        "norm1_post": f32(inputs["norm1_post"][0])[None, :],
        "norm2_pre": f32(inputs["norm2_pre"][0])[None, :],
        "norm2_post": f32(inputs["norm2_post"][0])[None, :],
        "w_gate_up": f32(inputs["w_gate_up"][0]),
        "w_down": f32(inputs["w_down"][0]),
        "c_ident": np.eye(128, dtype=np.float32),
        "c_tri": np.triu(np.ones((128, 128), np.float32)),
        "c_triu": np.tril(np.ones((128, 128), np.float32)),
    }
    inv = (10000.0 ** (-np.arange(0, 32, 2, dtype=np.float32) / 32)).astype(np.float32)
    common["c_invf"] = np.concatenate([inv, inv])[None, :].astype(np.float32)
    maps = []
    for c in range(n_cores):
        b, q = c // NQ, c % NQ
        xp = np.zeros((S + 4, D), np.float32)
        xp[2:S + 2] = x[b]
        nch = S // 128
        own = list(range(q * NO, (q + 1) * NO))
        prev = list(range(q * NO - 1, -1, -1))
        nxt = list(range((q + 1) * NO, nch))
        slots = prev + nxt

        def gather(chs):
            if not chs:
                return np.zeros((0, 132, D), np.float32)
            return np.stack([xp[ch * 128:ch * 128 + 132] for ch in chs])

        def pos(chs):
            n = max(len(chs), 1)
            p = np.zeros((128, n, 2), np.float32)
            for i, ch in enumerate(chs):
                t = ch * 128 + np.arange(128)
                p[:, i, 0] = t // GRID_W
                p[:, i, 1] = t % GRID_W
            return p
        mk = np.zeros((128, max(NS, 1), 2), np.float32)
        mk[:, :len(prev), 0] = 1.0
        mk[:, len(prev):len(slots), 1] = 1.0
        m = dict(common)
        m["xs"] = gather(slots)
        m["xo"] = gather(own)
        m["poss"] = pos(slots)
        m["poso"] = pos(own)
        m["mk"] = mk
        maps.append(m)
    return maps


_NC_CACHE = {}


def kernel(**inputs):
    x = np.asarray(inputs["x"])
    B, S, _ = x.shape
    n_cores = 8
    NQ = n_cores // B
    T = S // NQ
    key = (S, T)
    if key not in _NC_CACHE:
        _NC_CACHE[key] = build(S, T)
    nc = _NC_CACHE[key]
    maps = make_in_maps(inputs, S, T, n_cores)
    res = run_bass_kernel_spmd(nc, maps, core_ids=list(range(n_cores)))
    out = np.zeros((B, S, D), np.float32)
    for c in range(n_cores):
        b, q = c // NQ, c % NQ
        out[b, q * T:(q + 1) * T] = np.asarray(res.results[c]["out"]).reshape(T, D)
    return out
```

```python
import math
import numpy as np
import concourse.bass as bass
import concourse.mybir as mybir
from concourse.bass_utils import run_bass_kernel_spmd
from contextlib import ExitStack

F32 = mybir.dt.float32
BF16 = mybir.dt.bfloat16
AF = mybir.ActivationFunctionType
ALU = mybir.AluOpType
AX = mybir.AxisListType

D = 1024
GRID_W = 64
HD = 64
NQH = 8
NKV = 2
SSD_H = 16
SSD_P = 64
SSD_N = 128
CONV_K = 5
D_FF = 2816
EPS = 1e-6
OQ, OK_, OV, OZ, OXS, OB, OC, ODT, OG = 0, 512, 640, 768, 1792, 2816, 3072, 3328, 3360
NEG = -30000.0


class Buf:
    __slots__ = ("name", "t", "lw", "rd", "dsem", "dram", "root")

    def __init__(self, name, t=None, dram=False, root=None):
        self.root = root if root is not None else self
        self.name = name
        self.t = t
        self.lw = None
        self.rd = {}
        self.dsem = None
        self.dram = dram

    def __getitem__(self, idx):
        return self.t[idx]


class Prog:
    ENG = ("pe", "act", "dve", "pool", "sp")

    def __init__(self, nc, stack):
        self.nc = nc
        self.stack = stack
        self.sems = []
        self.q = {}
        for e in self.ENG:
            s = self._new_sem("q_" + e)
            self.q[e] = {"ops": [], "cnt": 0, "sem": s, "seen": {}}
        self.dma_cnt = {}
        self.uid = 0
        self.clock = {}

    def _new_sem(self, name):
        name = f"{name}_{len(self.sems)}"
        h = self.stack.enter_context(self.nc.semaphore(name))
        self.sems.append(h)
        return len(self.sems) - 1

    def sbuf(self, name, shape, dt, stack=None):
        self.uid += 1
        t = (stack or self.stack).enter_context(self.nc.sbuf_tensor(f"{name}_{self.uid}", list(shape), dt))
        return Buf(name, t)

    def psum(self, name, shape, dt=F32, stack=None):
        self.uid += 1
        t = (stack or self.stack).enter_context(self.nc.psum_tensor(f"{name}_{self.uid}", list(shape), dt))
        return Buf(name, t)

    def op(self, qn, fn, reads=(), writes=(), dma=False):
        q = self.q[qn]
        deps = {}
        reads = [b.root for b in reads]
        writes = [b.root for b in writes]

        def add(ev):
            if ev is None:
                return
            s, v = ev
            if deps.get(s, 0) < v:
                deps[s] = v

        for b in reads:
            add(b.lw)
        for b in writes:
            add(b.lw)
            for s, v in b.rd.items():
                add((s, v))
        if dma:
            b0 = [b for b in list(writes) + list(reads) if not b.dram][0]
            if b0.dsem is None:
                b0.dsem = self._new_sem("d_" + b0.name)
            key = b0.dsem
            c = self.dma_cnt.get(key, 0)
            if c:
                add((key, c))
            c += 16
            self.dma_cnt[key] = c
            ev = (key, c)
            inc = 16
        else:
            q["cnt"] += 1
            ev = (q["sem"], q["cnt"])
            inc = 1
        seen = q["seen"]
        for s, v in sorted(deps.items(), key=lambda kv: -kv[1]):
            if qn == "pe" and s == q["sem"]:
                continue
            if seen.get(s, 0) >= v:
                continue
            seen[s] = v
            q["ops"].append(("w", s, v))
            snap = self.clock.get((s, v))
            if snap:
                for s2, v2 in snap.items():
                    if seen.get(s2, 0) < v2:
                        seen[s2] = v2
        q["ops"].append(("i", fn, ev[0], inc))
        snap = dict(seen)
        if not dma:
            snap[ev[0]] = max(snap.get(ev[0], 0), ev[1] - 1)
        self.clock[ev] = snap
        for b in reads:
            if b.rd.get(ev[0], 0) < ev[1]:
                b.rd[ev[0]] = ev[1]
        for b in writes:
            b.lw = ev
            b.rd = {}
        return ev

    def barrier(self):
        targets = [(self.q[e]["sem"], self.q[e]["cnt"]) for e in self.ENG if self.q[e]["cnt"]]
        targets += [(k, c) for k, c in self.dma_cnt.items()]
        for e in self.ENG:
            q = self.q[e]
            for s, v in targets:
                if s == q["sem"]:
                    continue
                if q["seen"].get(s, 0) >= v:
                    continue
                q["seen"][s] = v
                q["ops"].append(("w", s, v))

    def emit(self):
        nc = self.nc
        sems = self.sems

        def replay(e, ops):
            for o in ops:
                if o[0] == "w":
                    e.wait_ge(sems[o[1]], o[2])
                else:
                    o[1](e).then_inc(sems[o[2]], o[3])

        with nc.Block() as block:
            @block.tensor
            def _(e):
                replay(e, self.q["pe"]["ops"])

            @block.scalar
            def _(e):
                replay(e, self.q["act"]["ops"])

            @block.vector
            def _(e):
                replay(e, self.q["dve"]["ops"])

            @block.gpsimd
            def _(e):
                replay(e, self.q["pool"]["ops"])

            @block.sync
            def _(e):
                replay(e, self.q["sp"]["ops"])


def ap_(t, off, dims):
    return bass.AP(t.tensor, off, [list(d) for d in dims])


def build(S, T):
    NO = T // 128
    NS = (S - T) // 128
    NB = S // 128
    NQT = max(1, T // 512)
    QW = T // NQT

    nc = bass.Bass("TRN2", target_bir_lowering=False)

    def din(name, shape):
        return nc.dram_tensor(name, list(shape), F32, kind="ExternalInput").ap()

    xs_d = din("xs", [NS, 132, D])
    xo_d = din("xo", [NO, 132, D])
    poss_d = din("poss", [128, NS, 2])
    poso_d = din("poso", [128, NO, 2])
    mk_d = din("mk", [128, NS, 2])
    w_in_d = din("w_in", [D, 5408])
    qn_d = din("q_norm", [1, 64])
    kn_d = din("k_norm", [1, 64])
    cw_d = din("conv_w", [5, 1536])
    cb_d = din("conv_b", [1, 1536])
    dtb_d = din("dt_bias", [1, 32])
    alog_d = din("a_log", [1, 32])
    dsk_d = din("d_skip", [1, 16])
    sn_d = din("ssd_norm", [1, D])
    wap_d = din("w_attn_proj", [512, D])
    wsp_d = din("w_ssd_proj", [D, D])
    wo_d = din("w_out", [D, D])
    n1a_d = din("norm1_pre", [1, D])
    n1b_d = din("norm1_post", [1, D])
    n2a_d = din("norm2_pre", [1, D])
    n2b_d = din("norm2_post", [1, D])
    wgu_d = din("w_gate_up", [D, 2 * D_FF])
    wd_d = din("w_down", [D_FF, D])
    ident_d = din("c_ident", [128, 128])
    tri_d = din("c_tri", [128, 128])
    triu_d = din("c_triu", [128, 128])
    invf_d = din("c_invf", [1, 32])
    out_d = nc.dram_tensor("out", [NO, 128, D], F32, kind="ExternalOutput").ap()

    def dscr(name, shape, dt):
        return nc.dram_tensor(name, list(shape), dt, kind="Internal").ap()

    s_xs = dscr("s_xs", [NO, 128, 1024], BF16)
    s_bt = dscr("s_bt", [NO, 128, 256], BF16)
    s_bT = dscr("s_bT", [NO, 128, 256], BF16)
    s_cT = dscr("s_cT", [NO, 128, 256], BF16)
    s_yf = dscr("s_yf", [NO, 128, 1024], F32)
    s_ao = dscr("s_ao", [NO, 128, 1024], F32)
    s_x1 = dscr("s_x1", [NO, 128, 1024], F32)
    s_kt = dscr("s_kt", [NB, 128, 128], BF16)
    s_v = dscr("s_v", [NB, 128, 128], BF16)
    s_qt = dscr("s_qt", [NO, 128, 512], BF16)
    s_yb = dscr("s_yb", [NO, 128, 1024], BF16)
    s_ff = dscr("s_ff", [NO, 128, 1024], F32)
    D_kt = [Buf(f"dkt{i}", dram=True) for i in range(NB)]
    D_v = [Buf(f"dv{i}", dram=True) for i in range(NB)]
    D_qt = [Buf(f"dqt{i}", dram=True) for i in range(NO)]
    D_yb = [Buf(f"dyb{i}", dram=True) for i in range(NO)]
    D_ff = [Buf(f"dff{i}", dram=True) for i in range(NO)]
    D_xs = [Buf(f"dxs{i}", dram=True) for i in range(NO)]
    D_bt = [Buf(f"dbt{i}", dram=True) for i in range(NO)]
    D_bT = [Buf(f"dbT{i}", dram=True) for i in range(NO)]
    D_cT = [Buf(f"dcT{i}", dram=True) for i in range(NO)]
    D_yf = [Buf(f"dyf{i}", dram=True) for i in range(NO)]
    D_ao = [Buf(f"dao{i}", dram=True) for i in range(NO)]
    D_x1 = [Buf(f"dx1{i}", dram=True) for i in range(NO)]
    D_out = [Buf(f"dout{i}", dram=True) for i in range(NO)]

    with ExitStack() as st:
        P = Prog(nc, st)

        def DMA(qn, out, in_, R, W):
            P.op(qn, lambda e: e.dma_start(out=out, in_=in_), R, W, dma=True)

        def DMAS(qn, out, in_, R, W):
            P.op(qn, lambda e: e.dma_start(out=out, in_=in_, allow_slow_non_contiguous=True), R, W, dma=True)

        def ACT(out, in_, func, R, W, **kw):
            P.op("act", lambda e: e.activation(out=out, in_=in_, func=func, **kw), R, W)

        def TT(eng, out, a, b, op, R, W):
            P.op(eng, lambda e: e.tensor_tensor(out=out, in0=a, in1=b, op=op), R, W)

        def TS(eng, out, a, s1, s2, op0, op1, R, W):
            if s2 is None:
                P.op(eng, lambda e: e.tensor_scalar(out=out, in0=a, scalar1=s1, scalar2=None, op0=op0), R, W)
            else:
                P.op(eng, lambda e: e.tensor_scalar(out=out, in0=a, scalar1=s1, scalar2=s2, op0=op0, op1=op1), R, W)

        def STT(out, a, s, b, op0, op1, R, W):
            P.op("dve", lambda e: e.scalar_tensor_tensor(out=out, in0=a, scalar=s, in1=b, op0=op0, op1=op1), R, W)

        def CP(eng, out, in_, R, W):
            if eng == "act":
                P.op("act", lambda e: e.copy(out=out, in_=in_), R, W)
            else:
                P.op(eng, lambda e: e.tensor_copy(out=out, in_=in_), R, W)

        def MM(out, lhsT, rhs, start, stop, R, W):
            P.op("pe", lambda e: e.matmul(out, lhsT=lhsT, rhs=rhs, start=start, stop=stop), R, W)

        def TR(out, in_, idn, R, W):
            P.op("pe", lambda e: e.transpose(out=out, in_=in_, identity=idn), R, W)

        def RECIP(out, in_, R, W):
            P.op("dve", lambda e: e.reciprocal(out=out, in_=in_), R, W)

        def RSUM(out, in_, R, W):
            P.op("dve", lambda e: e.reduce_sum(out=out, in_=in_, axis=AX.X), R, W)

        def MEMSET(eng, ap, val, W):
            P.op(eng, lambda e: e.memset(ap, val), (), W)

        cast_rr = [0]

        def cast_eng():
            cast_rr[0] += 1
            return ("dve", "act")[cast_rr[0] % 2]

        identf = P.sbuf("identf", [128, 128], F32)
        ident = P.sbuf("ident", [128, 128], BF16)
        tri = P.sbuf("tri", [128, 128], F32)
        triu = P.sbuf("triu", [128, 128], F32)
        mnegf = P.sbuf("mnegf", [128, 128], BF16)
        mnegb = P.sbuf("mnegb", [128, 128], BF16)
        onesf = P.sbuf("onesf", [128, 128], F32)
        onesb = P.sbuf("onesb", [128, 128], BF16)
        epsc = P.sbuf("epsc", [128, 1], F32)
        onec = P.sbuf("onec", [128, 1], F32)
        npi = P.sbuf("npi", [128, 1], F32)
        invf = P.sbuf("invf", [128, 32], F32)
        gq = P.sbuf("gq", [128, 64], F32)
        gk = P.sbuf("gk", [128, 64], F32)
        negB = P.sbuf("negB", [128, 1], F32)
        dtbias = P.sbuf("dtbias", [128, 32], F32)
        aneg = P.sbuf("aneg", [128, 32], F32)
        dskip = P.sbuf("dskip", [128, 16], F32)
        cbrow = P.sbuf("cbrow", [1, 1536], BF16)
        cbcol = P.sbuf("cbcol", [128, 12], F32)
        cwcol = P.sbuf("cwcol", [128, 5, 12], F32)
        Hf = P.sbuf("Hf", [128, 1024], F32)
        Hb = P.sbuf("Hb", [128, 1024], F32)
        dtb_own = P.sbuf("dtb_own", [128, NO, 16], F32)

        gn1a = P.sbuf("gn1a", [128, D], F32)
        phSA = ExitStack()
        diag = P.sbuf("diag", [128, 5, 12, 128], BF16, phSA)
        tmpc = P.sbuf("tmpc", [1, 1536], F32, phSA)

        DMA("sp", identf[:], ident_d, [], [identf])
        DMA("sp", tri[:], tri_d, [], [tri])
        DMA("sp", triu[:], triu_d, [], [triu])
        DMA("sp", invf[:], ap_(invf_d, 0, [[0, 128], [1, 32]]), [], [invf])
        DMA("sp", gq[:], ap_(qn_d, 0, [[0, 128], [1, 64]]), [], [gq])
        DMA("sp", gk[:], ap_(kn_d, 0, [[0, 128], [1, 64]]), [], [gk])
        DMA("sp", dtbias[:], ap_(dtb_d, 0, [[0, 128], [1, 32]]), [], [dtbias])
        DMA("sp", aneg[:], ap_(alog_d, 0, [[0, 128], [1, 32]]), [], [aneg])
        DMA("sp", dskip[:], ap_(dsk_d, 0, [[0, 128], [1, 16]]), [], [dskip])
        DMA("sp", tmpc[0:1, :], cb_d, [], [tmpc])
        CP("dve", cbrow[:], tmpc[0:1, :], [tmpc], [cbrow])
        DMAS("sp", cbcol[:], ap_(cb_d, 0, [[1, 128], [128, 12]]), [], [cbcol])
        for j in range(5):
            DMAS("sp", cwcol[:, j, :], ap_(cw_d, j * 1536, [[1, 128], [128, 12]]), [], [cwcol])
        CP("dve", ident[:], identf[:], [identf], [ident])
        MEMSET("pool", onesf[:], 1.0, [onesf])
        MEMSET("pool", onesb[:], 1.0, [onesb])
        MEMSET("pool", epsc[:], EPS, [epsc])
        MEMSET("pool", onec[:], 1.0, [onec])
        MEMSET("pool", npi[:], -math.pi, [npi])
        TS("dve", mnegf[:], tri[:], -1.0, -NEG, ALU.add, ALU.mult, [tri], [mnegf])
        TS("dve", mnegb[:], triu[:], -1.0, -NEG, ALU.add, ALU.mult, [triu], [mnegb])
        ACT(aneg[:], aneg[:], AF.Exp, [aneg], [aneg])
        TS("dve", aneg[:], aneg[:], -1.0, None, ALU.mult, None, [aneg], [aneg])
        for j in range(5):
            for ct in range(12):
                TS("dve", diag[:, j, ct, :], identf[:], cwcol[:, j, ct:ct + 1], None,
                   ALU.mult, None, [identf, cwcol], [diag])
        mq = P.sbuf("mq", [128, 2], F32, phSA)
        absq = P.sbuf("absq", [128, 64], F32, phSA)
        TS("dve", absq[:], gq[:], -1.0, None, ALU.mult, None, [gq], [absq])
        TT("dve", absq[:], absq[:], gq[:], ALU.max, [absq, gq], [absq])
        P.op("dve", lambda e: e.reduce_max(out=mq[:, 0:1], in_=absq[:], axis=AX.X), [absq], [mq])
        TS("dve", absq[:], gk[:], -1.0, None, ALU.mult, None, [gk], [absq])
        TT("dve", absq[:], absq[:], gk[:], ALU.max, [absq, gk], [absq])
        P.op("dve", lambda e: e.reduce_max(out=mq[:, 1:2], in_=absq[:], axis=AX.X), [absq], [mq])
        TT("dve", negB[:], mq[:, 0:1], mq[:, 1:2], ALU.mult, [mq], [negB])
        TS("dve", negB[:], negB[:], -8.0, None, ALU.mult, None, [negB], [negB])

        DMA("sp", gn1a[:], ap_(n1a_d, 0, [[0, 128], [1, D]]), [], [gn1a])

        def load_w(dst, dcol0, src_ap, ncols, stage, cw_max=256):
            K = dst.t.shape[1]
            i = 0
            for c0 in range(0, ncols, cw_max):
                cw = min(cw_max, ncols - c0)
                sg = stage[i % len(stage)]
                i += 1
                sv = sg[:, 0:K * cw].rearrange("p (k n) -> p k n", n=cw)
                DMA("sp", sv, src_ap[:, c0:c0 + cw].rearrange("(k p) n -> p k n", p=128), [], [sg])
                CP(cast_eng(), dst[:, :, dcol0 + c0:dcol0 + c0 + cw], sv, [sg], [dst])

        def rmsnorm_tok(x_ap, np_, gain_ap, out_ap, R, W, tmp):
            sq, ss = tmp["sq"], tmp["ss"]
            ACT(sq[0:np_, :], x_ap, AF.Square, R, [sq, ss], accum_out=ss[0:np_, 0:1])
            ACT(ss[0:np_, 0:1], ss[0:np_, 0:1], AF.Sqrt, [ss, epsc], [ss], scale=1.0 / D, bias=epsc[0:np_, :])
            RECIP(ss[0:np_, 0:1], ss[0:np_, 0:1], [ss], [ss])
            STT(out_ap, x_ap, ss[0:np_, 0:1], gain_ap, ALU.mult, ALU.mult, list(R) + [ss], W)

        def transpose8(src_ap_fn, dst_fn, R, W, pb_bf, tpb):
            for half in range(2):
                for k in range(4):
                    TR(pb_bf[:, k * 128:(k + 1) * 128], src_ap_fn(half * 4 + k), ident[:], list(R) + [ident], [tpb])
                CP("act" if half else "dve", dst_fn(half), pb_bf[:, 0:512].rearrange("p (k t) -> p k t", t=128), [tpb], W)

        def ssd_chunk(E, direction, adt_buf, ac0, dt_buf, dc0, xst, bt, bTt, cTt, H, y_init, y_dst):
            pb_st, st_acs, st_tot, pb_cv, pb_tok, accs, sm = (E[k] for k in
                                                              ("pb_st", "st_acs", "st_tot", "pb_cv", "pb_tok", "accs", "sm"))
            Rm, Em, Mm, xd, xdd, Hbf, ytmp, Htmp, cbt = (E[k] for k in
                                                        ("Rm", "Em", "Mm", "xd", "xdd", "Hbf", "ytmp", "Htmp", "cbt"))
            trm = tri if direction == 0 else triu
            mneg = mnegf if direction == 0 else mnegb
            aw = adt_buf.t.shape[1]
            dw = dt_buf.t.shape[1]
            adt16 = adt_buf[:, ac0:ac0 + 16]
            dt16 = dt_buf[:, dc0:dc0 + 16]
            v3 = lambda ap: ap.rearrange("p (h d) -> p h d", d=64)
            bcs = lambda buf, c0, n, rep: ap_(buf.t[:], c0, [[buf.t.shape[1], 128], [1, n], [0, rep]])
            MM(pb_st[:, 0:16], trm[:], adt16, True, True, [trm, adt_buf], [st_acs])
            MM(pb_st[:, 16:32], onesf[:], adt16, True, True, [onesf, adt_buf], [st_tot])
            CP("dve", sm["acs_sb"][:], pb_st[:, 0:16], [st_acs], [sm["acs_sb"]])
            TS("dve", sm["nacs"][:], sm["acs_sb"][:], -1.0, None, ALU.mult, None, [sm["acs_sb"]], [sm["nacs"]])
            ACT(sm["ea"][:], sm["acs_sb"][:], AF.Exp, [sm["acs_sb"]], [sm["ea"]])
            TT("dve", sm["ds"][:], pb_st[:, 16:32], sm["acs_sb"][:], ALU.subtract, [st_tot, sm["acs_sb"]], [sm["ds"]])
            ACT(sm["ds"][:], sm["ds"][:], AF.Exp, [sm["ds"]], [sm["ds"]])
            ACT(sm["edec"][:], pb_st[:, 16:32], AF.Exp, [st_tot], [sm["edec"]])
            TT("dve", sm["dtds"][:], dt16, sm["ds"][:], ALU.mult, [dt_buf, sm["ds"]], [sm["dtds"]])
            TT("dve", Rm[:], ap_(trm.t[:], 0, [[128, 128], [0, 16], [1, 128]]),
               ap_(adt_buf.t[:], ac0, [[aw, 128], [1, 16], [0, 128]]), ALU.mult, [trm, adt_buf], [Rm])
            for g in range(2):
                MM(pb_cv[:, g * 128:(g + 1) * 128], bTt[:, g * 128:(g + 1) * 128], cTt[:, g * 128:(g + 1) * 128],
                   True, True, [bTt, cTt], [pb_cv])
            CP("act", cbt[:], pb_cv[:, 0:256], [pb_cv], [cbt])
            for qd in range(4):
                bank = accs[qd]
                MM(bank[:, :], onesf[:], Rm[:, qd * 4:(qd + 1) * 4, :], True, False, [onesf, Rm], [bank])
                MM(bank[:, :], ident[:], ap_(mneg.t[:], 0, [[128, 128], [0, 4], [1, 128]]), False, True, [ident, mneg], [bank])
                for hh in range(4):
                    h = qd * 4 + hh
                    ACT(Em[:, h, :], bank[:, hh * 128:(hh + 1) * 128], AF.Exp, [bank, sm["nacs"]], [Em],
                        bias=sm["nacs"][:, h:h + 1])
            for g in range(2):
                TT("dve", Mm[:, g * 8:(g + 1) * 8, :], Em[:, g * 8:(g + 1) * 8, :],
                   ap_(cbt.t[:], g * 128, [[256, 128], [0, 8], [1, 128]]), ALU.mult, [Em, cbt], [Mm])
            TT("dve", v3(xd[:]), v3(xst[:]), ap_(dt_buf.t[:], dc0, [[dw, 128], [1, 16], [0, 64]]), ALU.mult, [xst, dt_buf], [xd])
            TT("dve", v3(xdd[:]), v3(xst[:]), bcs(sm["dtds"], 0, 16, 64), ALU.mult, [xst, sm["dtds"]], [xdd])
            CP("act", Hbf[:], H[:], [H], [Hbf])
            for g in range(2):
                MM(pb_tok[:, :], cTt[:, g * 128:(g + 1) * 128], Hbf[:, g * 512:(g + 1) * 512], True, True, [cTt, Hbf], [pb_tok])
                TT("dve", v3(ytmp[:, g * 512:(g + 1) * 512]), v3(pb_tok[:, :]), bcs(sm["ea"], g * 8, 8, 64), ALU.mult,
                   [pb_tok, sm["ea"]], [ytmp])
            TT("dve", ytmp[:], ytmp[:], y_init[:], ALU.add, [ytmp, y_init], [ytmp])
            for g in range(2):
                for hh in range(8):
                    h = g * 8 + hh
                    MM(pb_cv[:, hh * 64:(hh + 1) * 64], Mm[:, h, :], xd[:, h * 64:(h + 1) * 64], True, True, [Mm, xd], [pb_cv])
                TT("dve", y_dst[:, g * 512:(g + 1) * 512], ytmp[:, g * 512:(g + 1) * 512], pb_cv[:, :], ALU.add,
                   [ytmp, pb_cv], [y_dst])
            for g in range(2):
                MM(pb_tok[:, :], bt[:, g * 128:(g + 1) * 128], xdd[:, g * 512:(g + 1) * 512], True, True, [bt, xdd], [pb_tok])
                TT("dve", v3(Htmp[:, g * 512:(g + 1) * 512]), v3(H[:, g * 512:(g + 1) * 512]), bcs(sm["edec"], g * 8, 8, 64),
                   ALU.mult, [H, sm["edec"]], [Htmp])
                TT("dve", H[:, g * 512:(g + 1) * 512], Htmp[:, g * 512:(g + 1) * 512], pb_tok[:, :], ALU.add,
                   [Htmp, pb_tok], [H])

        with phSA as ph:
            WQ0, WKV0, WX0, WDT0 = 0, 512, 768, 2304
            Wa = P.sbuf("Wa", [128, 8, 2336], BF16, ph)
            stage = [P.sbuf(f"stg{i}", [128, 2048], F32, ph) for i in range(2)]
            for i in range(4):
                for g in range(2):
                    h = 4 * g + i
                    load_w(Wa, (2 * i + g) * 64, w_in_d[:, OQ + h * 64:OQ + (h + 1) * 64], 64, stage)
            load_w(Wa, WKV0, w_in_d[:, OK_:OK_ + 256], 256, stage)
            load_w(Wa, WX0, w_in_d[:, OXS:OXS + 1536], 1536, stage)
            load_w(Wa, WDT0, w_in_d[:, ODT:ODT + 32], 32, stage)

            tmp = {"sq": P.sbuf("sq", [128, D], F32, ph), "ss": P.sbuf("ss", [128, 8], F32, ph)}
            x_main = [P.sbuf(f"xm{i}", [128, D], F32, ph) for i in range(2)]
            x_halo = [P.sbuf(f"xh{i}", [4, D], F32, ph) for i in range(2)]
            hbuf = P.sbuf("hbuf", [128, 2, D], BF16, ph)
            hT = [P.sbuf(f"hT{i}", [128, 8, 132], BF16, ph) for i in range(2)]
            pre = P.sbuf("pre", [128, 12, 132], BF16, ph)
            xs_tok = [P.sbuf(f"xstok{i}", [128, 1024], BF16, ph) for i in range(2)]
            b_tok = [P.sbuf(f"btok{i}", [128, 256], BF16, ph) for i in range(2)]
            bT = [P.sbuf(f"bT{i}", [128, 256], BF16, ph) for i in range(2)]
            cT = [P.sbuf(f"cT{i}", [128, 256], BF16, ph) for i in range(2)]
            poss = P.sbuf("poss", [128, max(NS, 1), 2], F32, ph)
            poso = P.sbuf("poso", [128, NO, 2], F32, ph)
            mk = P.sbuf("mk", [128, max(NS, 1), 2], F32, ph)
            cos_t = P.sbuf("cos_t", [128, 32], F32, ph)
            sin_t = P.sbuf("sin_t", [128, 32], F32, ph)
            ang_t = P.sbuf("ang_t", [128, 32], F32, ph)
            rr_x = P.sbuf("rr_x", [128, 32], F32, ph)
            rr_k = P.sbuf("rr_k", [128, 32], F32, ph)
            rr_i = P.sbuf("rr_i", [128, 32], mybir.dt.int32, ph)
            qf = P.sbuf("qf", [128, 512], F32, ph)
            sq2 = P.sbuf("sq2", [128, 512], F32, ph)
            ss2 = P.sbuf("ss2", [128, 8], F32, ph)
            qn = P.sbuf("qn", [128, 512], F32, ph)
            t1 = P.sbuf("t1", [128, 128], F32, ph)
            t2 = P.sbuf("t2", [128, 128], F32, ph)
            sp_x = P.sbuf("sp_x", [128, 32], F32, ph)
            sp_a = P.sbuf("sp_a", [128, 32], F32, ph)
            sp_e = P.sbuf("sp_e", [128, 32], F32, ph)
            dt = P.sbuf("dt", [128, 32], F32, ph)
            adt = P.sbuf("adt", [128, 32], F32, ph)
            krot = [P.sbuf(f"krot{i}", [128, 128], BF16, ph) for i in range(2)]
            ktb = [P.sbuf(f"ktb{i}", [128, 128], BF16, ph) for i in range(2)]
            vb = [P.sbuf(f"vb{i}", [128, 128], BF16, ph) for i in range(2)]
            qrot = P.sbuf("qrot", [128, 512], BF16, ph)
            qtb = [P.sbuf(f"qtb{i}", [128, 512], BF16, ph) for i in range(2)]
            sm = {n_: P.sbuf(n_, [128, 16], F32, ph) for n_ in
                  ("ds", "offf", "offb", "tmp16", "dtds", "ea", "nacs", "acs_sb", "edec")}
            xdd = P.sbuf("xdd", [128, 1024], BF16, ph)
            xd = P.sbuf("xd", [128, 1024], BF16, ph)
            bsel = P.sbuf("bsel", [128, 2, 256], BF16, ph)
            Rm = P.sbuf("Rm", [128, 16, 128], F32, ph)
            Em = P.sbuf("Em", [128, 16, 128], BF16, ph)
            Mm = P.sbuf("Mm", [128, 16, 128], BF16, ph)
            Hbf = P.sbuf("Hbf", [128, 1024], BF16, ph)
            ytmp = P.sbuf("ytmp", [128, 1024], F32, ph)
            yout = [P.sbuf(f"yout{i}", [128, 1024], F32, ph) for i in range(2)]
            Htmp = P.sbuf("Htmp", [128, 1024], F32, ph)
            cbt = P.sbuf("cbt", [128, 256], F32, ph)

            pb_bf = P.psum("pb_bf", [128, 1024], BF16, ph)
            tpb = tph = tpk = tpq = pb_bf
            pb_tok = P.psum("pb_tok", [128, 512], F32, ph)
            pb_pre = P.psum("pb_pre", [128, 512], F32, ph)
            pb_st = Buf("pb_st", pb_pre.t[:, 400:512], root=pb_pre)
            st_acs = st_tot = pb_st
            prebank = [pb_pre] * 3
            pb_cv = P.psum("pb_cv", [128, 512], F32, ph)
            accs = [P.psum(f"acc{i}", [128, 512], F32, ph) for i in range(4)]

            if NS:
                DMA("sp", poss[:], poss_d, [], [poss])
                DMA("sp", mk[:], mk_d, [], [mk])
            DMA("sp", poso[:], poso_d, [], [poso])
            MEMSET("pool", sm["offf"][:], 0.0, [sm["offf"]])
            MEMSET("pool", sm["offb"][:], 0.0, [sm["offb"]])

            def rope_chunk(pos_buf, ti):
                nt = pos_buf.t.shape[1]
                for a in range(2):
                    TT("dve", ang_t[:, a * 16:(a + 1) * 16],
                       ap_(pos_buf.t[:], ti * 2 + a, [[nt * 2, 128], [0, 16]]),
                       invf[:, a * 16:(a + 1) * 16], ALU.mult, [pos_buf, invf], [ang_t])
                for dst, shift in ((sin_t, 0.0), (cos_t, 0.5 * math.pi)):
                    TS("dve", rr_x[:], ang_t[:], shift, None, ALU.add, None, [ang_t], [rr_x])
                    TS("dve", rr_i[:], rr_x[:], 1.0 / (2 * math.pi), None, ALU.mult, None, [rr_x], [rr_i])
                    CP("dve", rr_k[:], rr_i[:], [rr_i], [rr_k])
                    STT(dst[:], rr_k[:], -2 * math.pi, rr_x[:], ALU.mult, ALU.add, [rr_k, rr_x], [dst])
                    TS("dve", rr_k[:], dst[:], math.pi, -2 * math.pi, ALU.is_gt, ALU.mult, [dst], [rr_k])
                    TT("dve", dst[:], dst[:], rr_k[:], ALU.add, [dst, rr_k], [dst])
                    TS("dve", rr_k[:], dst[:], -math.pi, 2 * math.pi, ALU.is_lt, ALU.mult, [dst], [rr_k])
                    TT("dve", dst[:], dst[:], rr_k[:], ALU.add, [dst, rr_k], [dst])
                    ACT(dst[:], dst[:], AF.Sin, [dst], [dst])

            def qknorm_rope(src_ps, src_buf, H, gain, out_bf):
                n = H * 64
                CP("act", qf[:, 0:n], src_ps, [src_buf], [qf])
                TT("dve", sq2[:, 0:n], qf[:, 0:n], qf[:, 0:n], ALU.mult, [qf], [sq2])
                RSUM(ss2[:, 0:H], sq2[:, 0:n].rearrange("p (h d) -> p h d", d=64), [sq2], [ss2])
                ACT(ss2[:, 0:H], ss2[:, 0:H], AF.Sqrt, [ss2, epsc], [ss2], scale=1.0 / 64, bias=epsc[:])
                RECIP(ss2[:, 0:H], ss2[:, 0:H], [ss2], [ss2])
                TT("dve", qn[:, 0:n].rearrange("p (h d) -> p h d", d=64), qf[:, 0:n].rearrange("p (h d) -> p h d", d=64),
                   ap_(ss2.t[:], 0, [[8, 128], [1, H], [0, 64]]), ALU.mult, [qf, ss2], [qn])
                TT("dve", qn[:, 0:n].rearrange("p (h d) -> p h d", d=64), qn[:, 0:n].rearrange("p (h d) -> p h d", d=64),
                   ap_(gain.t[:], 0, [[64, 128], [0, H], [1, 64]]), ALU.mult, [qn, gain], [qn])
                ow = out_bf.t.shape[1]
                for a in range(2):
                    def xv(buf, half, wdt):
                        return ap_(buf.t[:], a * 32 + half * 16, [[wdt, 128], [64, H], [1, 16]])
                    cs = ap_(cos_t.t[:], a * 16, [[32, 128], [0, H], [1, 16]])
                    sn = ap_(sin_t.t[:], a * 16, [[32, 128], [0, H], [1, 16]])
                    t1v = ap_(t1.t[:], 0, [[128, 128], [16, H], [1, 16]])
                    t2v = ap_(t2.t[:], 0, [[128, 128], [16, H], [1, 16]])
                    TT("dve", t1v, xv(qn, 0, 512), cs, ALU.mult, [qn, cos_t], [t1])
                    TT("dve", t2v, xv(qn, 1, 512), sn, ALU.mult, [qn, sin_t], [t2])
                    TT("dve", xv(out_bf, 0, ow), t1v, t2v, ALU.subtract, [t1, t2], [out_bf])
                    TT("dve", t1v, xv(qn, 1, 512), cs, ALU.mult, [qn, cos_t], [t1])
                    TT("dve", t2v, xv(qn, 0, 512), sn, ALU.mult, [qn, sin_t], [t2])
                    TT("dve", xv(out_bf, 1, ow), t1v, t2v, ALU.add, [t1, t2], [out_bf])

            def front(x_src_ap, hTi, xm, xh):
                DMA("sp", xm[:], x_src_ap[2:130, :], [], [xm])
                DMA("sp", xh[0:2, :], x_src_ap[0:2, :], [], [xh])
                DMA("sp", xh[2:4, :], x_src_ap[130:132, :], [], [xh])
                rmsnorm_tok(xm[:], 128, gn1a[:], hbuf[:, 0, :], [xm, gn1a], [hbuf], tmp)
                rmsnorm_tok(xh[0:4, :], 4, gn1a[0:4, :], hbuf[0:4, 1, :], [xh, gn1a], [hbuf], tmp)
                transpose8(lambda kk: hbuf[:, 0, kk * 128:(kk + 1) * 128],
                           lambda half: hTi[:, half * 4:(half + 1) * 4, 2:130], [hbuf], [hTi], pb_bf, tpb)
                for kk in range(8):
                    TR(pb_bf[:, 512 + kk * 4:512 + (kk + 1) * 4], hbuf[0:4, 1, kk * 128:(kk + 1) * 128], ident[0:4, 0:4],
                       [hbuf, ident], [tph])
                hv = pb_bf[:, 512:544].rearrange("p (k t) -> p k t", t=4)
                CP("dve", hTi[:, :, 0:2], hv[:, :, 0:2], [tph], [hTi])
                CP("dve", hTi[:, :, 130:132], hv[:, :, 2:4], [tph], [hTi])

            def featmaj_pre(hTi, ntiles):
                for ct in range(ntiles):
                    pbk = pb_pre if ct % 2 == 0 else pb_cv
                    pv = pbk[:, 0:132]
                    for k in range(8):
                        MM(pv, Wa[:, k, WX0 + ct * 128:WX0 + (ct + 1) * 128], hTi[:, k, 0:132], k == 0, k == 7, [Wa, hTi], [pbk])
                    CP("act" if ct % 2 else "dve", pre[:, ct, :], pv, [pbk], [pre])

            def conv_tok(tiles, cb_col0, ncols):
                MM(pb_cv[:, 0:ncols], onesb[0:1, 0:128], cbrow[0:1, cb_col0:cb_col0 + ncols], True, False, [onesb, cbrow], [pb_cv])
                n = len(tiles)
                for i, ct in enumerate(tiles):
                    for j in range(5):
                        MM(pb_cv[:, i * 128:(i + 1) * 128], pre[:, ct, j:j + 128], diag[:, j, ct, :], False,
                           (i == n - 1 and j == 4), [pre, diag], [pb_cv])

            def softplus_dt():
                TT("dve", sp_x[:], pb_tok[:, 256:288], dtbias[:], ALU.add, [pb_tok, dtbias], [sp_x])
                TS("dve", sp_a[:], sp_x[:], -1.0, None, ALU.mult, None, [sp_x], [sp_a])
                TT("dve", sp_a[:], sp_a[:], sp_x[:], ALU.max, [sp_a, sp_x], [sp_a])
                ACT(sp_e[:], sp_a[:], AF.Exp, [sp_a], [sp_e], scale=-1.0)
                ACT(sp_e[:], sp_e[:], AF.Ln, [sp_e, onec], [sp_e], bias=onec[:])
                TS("dve", sp_x[:], sp_x[:], 0.0, None, ALU.max, None, [sp_x], [sp_x])
                TT("dve", dt[:], sp_x[:], sp_e[:], ALU.add, [sp_x, sp_e], [dt])

            def kv_mm(hTi):
                for k in range(8):
                    MM(pb_tok[:, 0:256], hTi[:, k, 2:130], Wa[:, k, WKV0:WKV0 + 256], k == 0, k == 7, [hTi, Wa], [pb_tok])
                for k in range(8):
                    MM(pb_tok[:, 256:288], hTi[:, k, 2:130], Wa[:, k, WDT0:WDT0 + 32], k == 0, k == 7, [hTi, Wa], [pb_tok])

            def kv_post(blk, r):
                CP("act", vb[r][:], pb_tok[:, 128:256], [pb_tok], [vb[r]])
                DMA("pool", s_v[blk], vb[r][:], [vb[r]], [D_v[blk]])
                qknorm_rope(pb_tok[:, 0:128], pb_tok, 2, gk, krot[r])
                TR(pb_bf[:, 640:768], krot[r][:], ident[:], [krot[r], ident], [tpk])
                CP("act", ktb[r][:], pb_bf[:, 640:768], [tpk], [ktb[r]])
                DMA("pool", s_kt[blk], ktb[r][:], [ktb[r]], [D_kt[blk]])

            def xs_b_tok(xst, bt):
                for half in range(2):
                    conv_tok(list(range(half * 4, half * 4 + 4)), half * 512, 512)
                    ACT(xst[:, half * 512:(half + 1) * 512], pb_cv[:, 0:512], AF.Silu, [pb_cv], [xst])
                conv_tok([8, 9], 1024, 256)
                ACT(bt[:], pb_cv[:, 0:256], AF.Silu, [pb_cv], [bt])

            bc16 = lambda buf, c0, n, rep: ap_(buf.t[:], c0, [[buf.t.shape[1], 128], [1, n], [0, rep]])
            v3 = lambda ap: ap.rearrange("p (h d) -> p h d", d=64)

            for s in range(NS):
                r = s % 2
                hTi = hT[r]
                front(xs_d[s], hTi, x_main[r], x_halo[r])
                rope_chunk(poss, s)
                kv_mm(hTi)
                featmaj_pre(hTi, 10)
                softplus_dt()
                kv_post(NO + s, r)
                for a in range(2):
                    TS("dve", dt[:, a * 16:(a + 1) * 16], dt[:, a * 16:(a + 1) * 16], mk[:, s, a:a + 1], None,
                       ALU.mult, None, [dt, mk], [dt])
                TT("dve", adt[:], dt[:], aneg[:], ALU.mult, [dt, aneg], [adt])
                MM(pb_st[:, 0:16], tri[:], adt[:, 0:16], True, False, [tri, adt], [st_acs])
                MM(pb_st[:, 0:16], triu[:], adt[:, 16:32], False, True, [triu, adt], [st_acs])
                MM(pb_st[:, 16:32], onesf[:], adt[:, 0:16], True, False, [onesf, adt], [st_tot])
                MM(pb_st[:, 16:32], onesf[:], adt[:, 16:32], False, True, [onesf, adt], [st_tot])
                TS("dve", sm["tmp16"][:], sm["offf"][:], mk[:, s, 0:1], None, ALU.mult, None, [sm["offf"], mk], [sm["tmp16"]])
                STT(sm["tmp16"][:], sm["offb"][:], mk[:, s, 1:2], sm["tmp16"][:], ALU.mult, ALU.add,
                    [sm["offb"], mk, sm["tmp16"]], [sm["tmp16"]])
                TT("dve", sm["ds"][:], pb_st[:, 16:32], sm["tmp16"][:], ALU.add, [st_tot, sm["tmp16"]], [sm["ds"]])
                TT("dve", sm["ds"][:], sm["ds"][:], pb_st[:, 0:16], ALU.subtract, [sm["ds"], st_acs], [sm["ds"]])
                ACT(sm["ds"][:], sm["ds"][:], AF.Exp, [sm["ds"]], [sm["ds"]])
                TT("dve", sm["dtds"][:], dt[:, 0:16], dt[:, 16:32], ALU.add, [dt], [sm["dtds"]])
                TT("dve", sm["dtds"][:], sm["dtds"][:], sm["ds"][:], ALU.mult, [sm["dtds"], sm["ds"]], [sm["dtds"]])
                STT(sm["offf"][:], pb_st[:, 16:32], mk[:, s, 0:1], sm["offf"][:], ALU.mult, ALU.add,
                    [st_tot, mk, sm["offf"]], [sm["offf"]])
                STT(sm["offb"][:], pb_st[:, 16:32], mk[:, s, 1:2], sm["offb"][:], ALU.mult, ALU.add,
                    [st_tot, mk, sm["offb"]], [sm["offb"]])
                xst, bt = xs_tok[r], b_tok[r]
                xs_b_tok(xst, bt)
                TT("dve", v3(xdd[:]), v3(xst[:]), bc16(sm["dtds"], 0, 16, 64), ALU.mult, [xst, sm["dtds"]], [xdd])
                for a in range(2):
                    TS("dve", bsel[:, a, :], bt[:], mk[:, s, a:a + 1], None, ALU.mult, None, [bt, mk], [bsel])
                for a in range(2):
                    for g in range(2):
                        MM(accs[a * 2 + g][:, :], bsel[:, a, g * 128:(g + 1) * 128], xdd[:, g * 512:(g + 1) * 512],
                           s == 0, s == NS - 1, [bsel, xdd], [accs[a * 2 + g]])
            if NS:
                for g in range(2):
                    CP("dve", Hf[:, g * 512:(g + 1) * 512], accs[g][:, :], [accs[g]], [Hf])
                    CP("act", Hb[:, g * 512:(g + 1) * 512], accs[2 + g][:, :], [accs[2 + g]], [Hb])
            else:
                MEMSET("pool", Hf[:], 0.0, [Hf])
                MEMSET("pool", Hb[:], 0.0, [Hb])

            for j in range(NO):
                r = j % 2
                hTi = hT[r]
                front(xo_d[j], hTi, x_main[r], x_halo[r])
                rope_chunk(poso, j)
                kv_mm(hTi)
                for k in range(8):
                    MM(accs[0][:, :], hTi[:, k, 2:130], Wa[:, k, WQ0:WQ0 + 512], k == 0, k == 7, [hTi, Wa], [accs[0]])
                featmaj_pre(hTi, 12)
                softplus_dt()
                kv_post(j, r)
                qknorm_rope(accs[0][:, :], accs[0], 8, gq, qrot)
                for i in range(4):
                    TR(pb_bf[:, 768:896], qrot[:, i * 128:(i + 1) * 128], ident[:], [qrot, ident], [tpq])
                    CP("act", qtb[r][:, i * 128:(i + 1) * 128], pb_bf[:, 768:896], [tpq], [qtb[r]])
                DMA("pool", s_qt[j], qtb[r][:], [qtb[r]], [D_qt[j]])
                CP("dve", dtb_own[:, j, :], dt[:, 16:32], [dt], [dtb_own])
                TT("dve", adt[:], dt[:], aneg[:], ALU.mult, [dt, aneg], [adt])
                xst, bt, bTt, cTt = xs_tok[r], b_tok[r], bT[r], cT[r]
                xs_b_tok(xst, bt)
                for i, (ct, dstb) in enumerate(((8, bTt), (9, bTt), (10, cTt), (11, cTt))):
                    for jj in range(5):
                        MM(pb_cv[:, i * 128:(i + 1) * 128], diag[:, jj, ct, :], pre[:, ct, jj:jj + 128], jj == 0, jj == 4,
                           [pre, diag], [pb_cv])
                    ACT(dstb[:, (i % 2) * 128:(i % 2 + 1) * 128], pb_cv[:, i * 128:(i + 1) * 128], AF.Silu, [pb_cv, cbcol], [dstb],
                        bias=cbcol[:, ct:ct + 1])
                DMA("pool", s_xs[j], xst[:], [xst], [D_xs[j]])
                DMA("pool", s_bt[j], bt[:], [bt], [D_bt[j]])
                DMA("pool", s_bT[j], bTt[:], [bTt], [D_bT[j]])
                DMA("pool", s_cT[j], cTt[:], [cTt], [D_cT[j]])
                yo = yout[r]
                TT("dve", v3(yo[:]), v3(xst[:]), bc16(dskip, 0, 16, 64), ALU.mult, [xst, dskip], [yo])
                ssd_env = dict(pb_st=pb_st, st_acs=st_acs, st_tot=st_tot, pb_cv=pb_cv, pb_tok=pb_tok, accs=accs, sm=sm,
                               Rm=Rm, Em=Em, Mm=Mm, xd=xd, xdd=xdd, Hbf=Hbf, ytmp=ytmp, Htmp=Htmp, cbt=cbt)
                ssd_chunk(ssd_env, 0, adt, 0, dt, 0, xst, bt, bTt, cTt, Hf, yo, yo)
                DMA("pool", s_yf[j], yo[:], [yo], [D_yf[j]])
        P.barrier()

        with ExitStack() as ph:
            KT = P.sbuf("KT", [128, NB, 128], BF16, ph)
            Vs = P.sbuf("Vs", [128, NB, 2, 65], BF16, ph)
            QT = P.sbuf("QT", [128, 4, NO, 128], BF16, ph)
            MEMSET("pool", Vs[:], 1.0, [Vs])
            for b0 in range(0, NB, 16):
                b1 = min(NB, b0 + 16)
                DMA("sp", KT[:, b0:b1, :], s_kt[b0:b1].rearrange("b p k -> p b k"), D_kt[b0:b1], [KT])
                for g in range(2):
                    DMA("sp", Vs[:, b0:b1, g, 0:64], s_v[b0:b1, :, g * 64:(g + 1) * 64].rearrange("b p d -> p b d"),
                        D_v[b0:b1], [Vs])
            for i in range(4):
                for j0 in range(0, NO, 16):
                    j1 = min(NO, j0 + 16)
                    DMA("sp", QT[:, i, j0:j1, :], s_qt[j0:j1, :, i * 128:(i + 1) * 128].rearrange("j p t -> p j t"),
                        D_qt[j0:j1], [QT])
            Wp = P.sbuf("Wp", [64, 8, D], BF16, ph)
            stg = P.sbuf("stgp", [64, 8, D], F32, ph)
            DMA("sp", stg[:], wap_d.rearrange("(h p) n -> p h n", p=64), [], [stg])
            CP("dve", Wp[:], stg[:], [stg], [Wp])
            KG = 2 if NB % 2 == 0 else 1
            PT = [P.sbuf(f"PT{i}", [128, KG, 512], BF16, ph) for i in range(3)]
            oacc = P.sbuf("oacc", [65, 512], F32, ph)
            rrow = P.sbuf("rrow", [65, 512], F32, ph)
            OTn = P.sbuf("OTn", [64, 8, 512], BF16, ph)
            aosb = [P.sbuf(f"aosb{i}", [128, D], F32, ph) for i in range(2)]
            ps_s = [P.psum(f"ps_s{i}", [128, KG, 512], F32, ph) for i in range(2)]
            ps_o = [P.psum(f"ps_o{i}", [128, 512], F32, ph) for i in range(2)]
            ps_bc = P.psum("ps_bc", [128, 512], F32, ph)
            ps_pj = [P.psum(f"ps_pj{i}", [128, 512], F32, ph) for i in range(1)]
            it = 0
            CQ = QW // 128
            deferred = []

            def flush():
                for f_ in deferred:
                    f_()
                del deferred[:]

            def pv_ops(po, pt, kb0, g):
                def run():
                    for kk in range(KG):
                        kb = kb0 + kk
                        MM(po[0:65, 0:QW], Vs[:, kb, g, :], pt[:, kk, 0:QW], kb == 0, kb == NB - 1, [Vs, pt], [po])
                return run

            def fin_ops(po, h):
                def run():
                    CP("dve", oacc[:, 0:QW], po[0:65, 0:QW], [po], [oacc])
                    RECIP(rrow[64:65, 0:QW], oacc[64:65, 0:QW], [oacc], [rrow])
                    MM(ps_bc[0:64, 0:QW], onesf[64:65, 0:64], rrow[64:65, 0:QW], True, True, [onesf, rrow], [ps_bc])
                    TT("dve", OTn[:, h, 0:QW], oacc[0:64, 0:QW], ps_bc[0:64, 0:QW], ALU.mult, [oacc, ps_bc], [OTn])
                return run

            for qt in range(NQT):
                c0 = qt * CQ
                for h in range(8):
                    g, i = h // 4, h % 4
                    po = ps_o[h % 2]
                    for kb0 in range(0, NB, KG):
                        ss_ = ps_s[it % 2]
                        pt = PT[it % 3]
                        it += 1
                        for kk in range(KG):
                            MM(ss_[:, kk, 0:QW], KT[g * 64:(g + 1) * 64, kb0 + kk, :],
                               QT[g * 64:(g + 1) * 64, i, c0:c0 + CQ, :], True, True, [KT, QT], [ss_])
                        ACT(pt[:, :, 0:QW], ss_[:, :, 0:QW], AF.Exp, [ss_, negB], [pt], scale=0.125, bias=negB[:])
                        flush()
                        deferred.append(pv_ops(po, pt, kb0, g))
                        if kb0 + KG >= NB:
                            deferred.append(fin_ops(po, h))
                flush()
                for tt in range(CQ):
                    cj = c0 + tt
                    ao = aosb[cj % 2]
                    for half in range(2):
                        pj = ps_pj[0]
                        for h in range(8):
                            MM(pj[:, :], OTn[:, h, tt * 128:(tt + 1) * 128], Wp[:, h, half * 512:(half + 1) * 512],
                               h == 0, h == 7, [OTn, Wp], [pj])
                        CP("act" if half else "dve", ao[:, half * 512:(half + 1) * 512], pj[:, :], [pj], [ao])
                    DMA("pool", s_ao[cj], ao[:], [ao], [D_ao[cj]])
        P.barrier()

        with ExitStack() as ph:
            Wz = P.sbuf("Wz", [128, 8, 1024], BF16, ph)
            stage = [P.sbuf(f"stgb{i}", [128, 2048], F32, ph) for i in range(2)]
            load_w(Wz, 0, w_in_d[:, OZ:OZ + 1024], 1024, stage)
            gsn = P.sbuf("gsn", [128, D], F32, ph)
            DMA("sp", gsn[:], ap_(sn_d, 0, [[0, 128], [1, D]]), [], [gsn])
            tmp = {"sq": P.sbuf("sqb", [128, D], F32, ph), "ss": P.sbuf("ssb", [128, 8], F32, ph)}
            x_main = [P.sbuf(f"xmb{i}", [128, D], F32, ph) for i in range(2)]
            hbuf = P.sbuf("hbufb", [128, D], BF16, ph)
            hTb = P.sbuf("hTb", [128, 8, 128], BF16, ph)
            xs_tok = [P.sbuf(f"xstokb{i}", [128, 1024], BF16, ph) for i in range(2)]
            b_tok = [P.sbuf(f"btokb{i}", [128, 256], BF16, ph) for i in range(2)]
            bT = [P.sbuf(f"bTb{i}", [128, 256], BF16, ph) for i in range(2)]
            cT = [P.sbuf(f"cTb{i}", [128, 256], BF16, ph) for i in range(2)]
            yfb = [P.sbuf(f"yfb{i}", [128, 1024], F32, ph) for i in range(2)]
            dtB = P.sbuf("dtB", [128, 16], F32, ph)
            adtB = P.sbuf("adtB", [128, 16], F32, ph)
            sm = {n_: P.sbuf(n_ + "B", [128, 16], F32, ph) for n_ in
                  ("ds", "dtds", "ea", "nacs", "acs_sb", "edec")}
            env = dict(
                pb_st=P.psum("pb_stB", [128, 512], F32, ph), pb_cv=P.psum("pb_cvB", [128, 512], F32, ph),
                pb_tok=P.psum("pb_tokB", [128, 512], F32, ph),
                accs=[P.psum(f"accB{i}", [128, 512], F32, ph) for i in range(4)], sm=sm,
                Rm=P.sbuf("RmB", [128, 16, 128], F32, ph), Em=P.sbuf("EmB", [128, 16, 128], BF16, ph),
                Mm=P.sbuf("MmB", [128, 16, 128], BF16, ph), xd=P.sbuf("xdB", [128, 1024], BF16, ph),
                xdd=P.sbuf("xddB", [128, 1024], BF16, ph), Hbf=P.sbuf("HbfB", [128, 1024], BF16, ph),
                ytmp=P.sbuf("ytmpB", [128, 1024], F32, ph), Htmp=P.sbuf("HtmpB", [128, 1024], F32, ph),
                cbt=P.sbuf("cbtB", [128, 256], F32, ph))
            env["st_acs"] = env["pb_st"]
            env["st_tot"] = env["pb_st"]
            pb_bf = P.psum("pb_bfB", [128, 1024], BF16, ph)
            tpb = pb_bf
            pb_tok = env["pb_tok"]
            yout = P.sbuf("youtB", [128, 1024], F32, ph)
            zs = P.sbuf("zs", [128, 1024], F32, ph)
            ybf = [P.sbuf(f"ybf{i}", [128, 1024], BF16, ph) for i in range(2)]
            for jj in range(NO):
                j = NO - 1 - jj
                r = jj % 2
                xm = x_main[r]
                DMA("sp", xm[:], xo_d[j][2:130, :], [], [xm])
                DMA("sp", xs_tok[r][:], s_xs[j], [D_xs[j]], [xs_tok[r]])
                DMA("sp", b_tok[r][:], s_bt[j], [D_bt[j]], [b_tok[r]])
                DMA("sp", bT[r][:], s_bT[j], [D_bT[j]], [bT[r]])
                DMA("sp", cT[r][:], s_cT[j], [D_cT[j]], [cT[r]])
                DMA("sp", yfb[r][:], s_yf[j], [D_yf[j]], [yfb[r]])
                CP("dve", dtB[:], dtb_own[:, j, :], [dtb_own], [dtB])
                TT("dve", adtB[:], dtB[:], aneg[:, 16:32], ALU.mult, [dtB, aneg], [adtB])
                ssd_chunk(env, 1, adtB, 0, dtB, 0, xs_tok[r], b_tok[r], bT[r], cT[r], Hb, yfb[r], yout)
                rmsnorm_tok(xm[:], 128, gn1a[:], hbuf[:], [xm, gn1a], [hbuf], tmp)
                transpose8(lambda kk: hbuf[:, kk * 128:(kk + 1) * 128], lambda half: hTb[:, half * 4:(half + 1) * 4, :],
                           [hbuf], [hTb], pb_bf, tpb)
                for half in range(2):
                    for k in range(8):
                        MM(pb_tok[:, :], hTb[:, k, :], Wz[:, k, half * 512:(half + 1) * 512], k == 0, k == 7, [hTb, Wz], [pb_tok])
                    ACT(zs[:, half * 512:(half + 1) * 512], pb_tok[:, :], AF.Silu, [pb_tok], [zs])
                TT("dve", yout[:], yout[:], zs[:], ALU.mult, [yout, zs], [yout])
                rmsnorm_tok(yout[:], 128, gsn[:], ybf[r][:], [yout, gsn], [ybf[r]], tmp)
                DMA("pool", s_yb[j], ybf[r][:], [ybf[r]], [D_yb[j]])
        P.barrier()

        with ExitStack() as ph:
            Wg = P.sbuf("Wg", [128, 8, 2048], BF16, ph)
            Wsp = P.sbuf("Wsp", [128, 8, D], BF16, ph)
            Wo = P.sbuf("Wo", [128, 8, D], BF16, ph)
            stage = [P.sbuf(f"stgc{i}", [128, 2048], F32, ph) for i in range(2)]
            load_w(Wg, 0, w_in_d[:, OG:OG + 2048], 2048, stage)
            load_w(Wsp, 0, wsp_d, D, stage)
            load_w(Wo, 0, wo_d, D, stage)
            gn1b = P.sbuf("gn1b", [128, D], F32, ph)
            DMA("sp", gn1b[:], ap_(n1b_d, 0, [[0, 128], [1, D]]), [], [gn1b])
            tmp = {"sq": P.sbuf("sqc", [128, D], F32, ph), "ss": P.sbuf("ssc", [128, 8], F32, ph)}
            x_main = [P.sbuf(f"xmc{i}", [128, D], F32, ph) for i in range(2)]
            aob = [P.sbuf(f"aob{i}", [128, D], F32, ph) for i in range(2)]
            ybf = [P.sbuf(f"ybfc{i}", [128, D], BF16, ph) for i in range(2)]
            hbuf = P.sbuf("hbufc", [128, D], BF16, ph)
            hTb = P.sbuf("hTc", [128, 8, 128], BF16, ph)
            yT = P.sbuf("yT", [128, 8, 128], BF16, ph)
            mT = P.sbuf("mT", [128, 8, 128], BF16, ph)
            gsig = P.sbuf("gsig", [128, 2048], F32, ph)
            mix = P.sbuf("mix", [128, D], F32, ph)
            sso = P.sbuf("sso", [128, D], F32, ph)
            mixbf = P.sbuf("mixbf", [128, D], BF16, ph)
            x1 = [P.sbuf(f"x1_{i}", [128, D], F32, ph) for i in range(2)]
            pb_bf = P.psum("pb_bfC", [128, 1024], BF16, ph)
            tpb = pb_bf
            pbs = [P.psum(f"pbC{i}", [128, 512], F32, ph) for i in range(4)]
            for j in range(NO):
                r = j % 2
                xm = x_main[r]
                DMA("sp", xm[:], xo_d[j][2:130, :], [], [xm])
                DMA("sp", aob[r][:], s_ao[j], [D_ao[j]], [aob[r]])
                DMA("sp", ybf[r][:], s_yb[j], [D_yb[j]], [ybf[r]])
                rmsnorm_tok(xm[:], 128, gn1a[:], hbuf[:], [xm, gn1a], [hbuf], tmp)
                transpose8(lambda kk: hbuf[:, kk * 128:(kk + 1) * 128], lambda half: hTb[:, half * 4:(half + 1) * 4, :],
                           [hbuf], [hTb], pb_bf, tpb)
                for qd in range(4):
                    for k in range(8):
                        MM(pbs[qd][:, :], hTb[:, k, :], Wg[:, k, qd * 512:(qd + 1) * 512], k == 0, k == 7, [hTb, Wg], [pbs[qd]])
                    ACT(gsig[:, qd * 512:(qd + 1) * 512], pbs[qd][:, :], AF.Sigmoid, [pbs[qd]], [gsig])
                transpose8(lambda kk: ybf[r][:, kk * 128:(kk + 1) * 128], lambda half: yT[:, half * 4:(half + 1) * 4, :],
                           [ybf[r]], [yT], pb_bf, tpb)
                TT("dve", mix[:], aob[r][:], gsig[:, 0:1024], ALU.mult, [aob[r], gsig], [mix])
                for half in range(2):
                    pb = pbs[half]
                    for k in range(8):
                        MM(pb[:, :], yT[:, k, :], Wsp[:, k, half * 512:(half + 1) * 512], k == 0, k == 7, [yT, Wsp], [pb])
                    TT("dve", sso[:, half * 512:(half + 1) * 512], pb[:, :], gsig[:, 1024 + half * 512:1024 + (half + 1) * 512],
                       ALU.mult, [pb, gsig], [sso])
                TT("dve", mixbf[:], mix[:], sso[:], ALU.add, [mix, sso], [mixbf])
                transpose8(lambda kk: mixbf[:, kk * 128:(kk + 1) * 128], lambda half: mT[:, half * 4:(half + 1) * 4, :],
                           [mixbf], [mT], pb_bf, tpb)
                for half in range(2):
                    pb = pbs[2 + half]
                    for k in range(8):
                        MM(pb[:, :], mT[:, k, :], Wo[:, k, half * 512:(half + 1) * 512], k == 0, k == 7, [mT, Wo], [pb])
                    CP("act", mix[:, half * 512:(half + 1) * 512], pb[:, :], [pb], [mix])
                rmsnorm_tok(mix[:], 128, gn1b[:], sso[:], [mix, gn1b], [sso], tmp)
                TT("dve", x1[r][:], sso[:], xm[:], ALU.add, [sso, xm], [x1[r]])
                DMA("pool", s_x1[j], x1[r][:], [x1[r]], [D_x1[j]])
        P.barrier()

        NF = D_FF // 128
        NFH = NF // 2
        UW = min(512, T)
        NT = UW // 128
        for hf in range(2):
            with ExitStack() as ph:
                Wgu = P.sbuf("Wgu", [128, 8, 2 * NFH * 128], BF16, ph)
                Wd = P.sbuf("Wd", [128, NFH, D], BF16, ph)
                stage = [P.sbuf(f"stgd{i}", [128, 2816], F32, ph) for i in range(2)]
                f0 = hf * NFH * 128
                load_w(Wgu, 0, wgu_d[:, f0:f0 + NFH * 128], NFH * 128, stage)
                load_w(Wgu, NFH * 128, wgu_d[:, D_FF + f0:D_FF + f0 + NFH * 128], NFH * 128, stage)
                load_w(Wd, 0, wd_d[f0:f0 + NFH * 128, :], D, stage)
                gn2a = P.sbuf("gn2a", [128, D], F32, ph)
                gn2b = P.sbuf("gn2b", [128, D], F32, ph)
                DMA("sp", gn2a[:], ap_(n2a_d, 0, [[0, 128], [1, D]]), [], [gn2a])
                DMA("sp", gn2b[:], ap_(n2b_d, 0, [[0, 128], [1, D]]), [], [gn2b])
                tmp = {"sq": P.sbuf("sqd", [128, D], F32, ph), "ss": P.sbuf("ssd", [128, 8], F32, ph)}
                x1b = [P.sbuf(f"x1c{i}", [128, NT, D], F32, ph) for i in range(2)]
                hb = P.sbuf("hbc", [128, D], BF16, ph)
                h2T = P.sbuf("h2T", [128, 8, UW], BF16, ph)
                gact = P.sbuf("gact", [128, UW], F32, ph)
                actT = P.sbuf("actT", [128, NFH, UW], BF16, ph)
                ffn = [P.sbuf(f"ffn{i}", [128, D], F32, ph) for i in range(2)]
                ffp = [P.sbuf(f"ffp{i}", [128, D], F32, ph) for i in range(2)]
                ob = [P.sbuf(f"ob{i}", [128, D], F32, ph) for i in range(2)]
                pb_bf = P.psum("pb_bfD", [128, 1024], BF16, ph)
                tpb = pb_bf
                pg = [P.psum(f"pg{i}", [128, 512], F32, ph) for i in range(2)]
                pu = [P.psum(f"pu{i}", [128, 512], F32, ph) for i in range(2)]
                pd = [P.psum(f"pd{i}", [128, 512], F32, ph) for i in range(2)]
                for u in range(T // UW):
                    xb = x1b[u % 2]
                    for t in range(NT):
                        cj = u * NT + t
                        DMA("sp", xb[:, t, :], s_x1[cj], [D_x1[cj]], [xb])
                    for t in range(NT):
                        rmsnorm_tok(xb[:, t, :], 128, gn2a[:], hb[:], [xb, gn2a], [hb], tmp)
                        transpose8(lambda kk: hb[:, kk * 128:(kk + 1) * 128],
                                   lambda half: h2T[:, half * 4:(half + 1) * 4, t * 128:(t + 1) * 128], [hb], [h2T], pb_bf, tpb)
                    for f in range(NFH):
                        g_, u_ = pg[f % 2], pu[f % 2]
                        for k in range(8):
                            MM(g_[:, 0:UW], Wgu[:, k, f * 128:(f + 1) * 128], h2T[:, k, :], k == 0, k == 7, [Wgu, h2T], [g_])
                        for k in range(8):
                            MM(u_[:, 0:UW], Wgu[:, k, (NFH + f) * 128:(NFH + f + 1) * 128], h2T[:, k, :], k == 0, k == 7,
                               [Wgu, h2T], [u_])
                        ACT(gact[:, :], g_[:, 0:UW], AF.Silu, [g_], [gact])
                        TT("dve", actT[:, f, :], gact[:, :], u_[:, 0:UW], ALU.mult, [gact, u_], [actT])
                    for t in range(NT):
                        cj = u * NT + t
                        ff = ffn[cj % 2]
                        if hf == 1:
                            DMA("sp", ffp[cj % 2][:], s_ff[cj], [D_ff[cj]], [ffp[cj % 2]])
                        for half in range(2):
                            p_ = pd[half]
                            for f in range(NFH):
                                MM(p_[:, :], actT[:, f, t * 128:(t + 1) * 128], Wd[:, f, half * 512:(half + 1) * 512],
                                   f == 0, f == NFH - 1, [actT, Wd], [p_])
                            if hf == 0:
                                CP("act", ff[:, half * 512:(half + 1) * 512], p_[:, :], [p_], [ff])
                            else:
                                TT("dve", ff[:, half * 512:(half + 1) * 512], p_[:, :], ffp[cj % 2][:, half * 512:(half + 1) * 512],
                                   ALU.add, [p_, ffp[cj % 2]], [ff])
                        if hf == 0:
                            DMA("pool", s_ff[cj], ff[:], [ff], [D_ff[cj]])
                        else:
                            o_ = ob[cj % 2]
                            rmsnorm_tok(ff[:], 128, gn2b[:], o_[:], [ff, gn2b], [o_], tmp)
                            TT("dve", o_[:], o_[:], xb[:, t, :], ALU.add, [o_, xb], [o_])
                            DMA("pool", out_d[cj], o_[:], [o_], [D_out[cj]])
            P.barrier()
        P.emit()
    return nc


def make_in_maps(inputs, S, T, n_cores):
    x = np.asarray(inputs["x"], dtype=np.float32)
    B = x.shape[0]
    NQ = S // T
    NO = T // 128
    NS = (S - T) // 128
    f32 = lambda a: np.ascontiguousarray(np.asarray(a, dtype=np.float32))
    common = {
        "w_in": f32(inputs["w_in"][0]),
        "q_norm": f32(inputs["q_norm"][0])[None, :],
        "k_norm": f32(inputs["k_norm"][0])[None, :],
        "conv_w": f32(inputs["conv_w"][0]),
        "conv_b": f32(inputs["conv_b"][0])[None, :],
        "dt_bias": f32(np.concatenate([inputs["dt_bias_f"][0], inputs["dt_bias_b"][0]]))[None, :],
        "a_log": f32(np.concatenate([inputs["a_log_f"][0], inputs["a_log_b"][0]]))[None, :],
        "d_skip": f32(inputs["d_skip"][0])[None, :],
        "ssd_norm": f32(inputs["ssd_norm"][0])[None, :],
        "w_attn_proj": f32(inputs["w_attn_proj"][0]),
        "w_ssd_proj": f32(inputs["w_ssd_proj"][0]),
        "w_out": f32(inputs["w_out"][0]),
        "norm1_pre": f32(inputs["norm1_pre"][0])[None, :],
        "norm1_post": f32(inputs["norm1_post"][0])[None, :],
        "norm2_pre": f32(inputs["norm2_pre"][0])[None, :],
        "norm2_post": f32(inputs["norm2_post"][0])[None, :],
        "w_gate_up": f32(inputs["w_gate_up"][0]),
        "w_down": f32(inputs["w_down"][0]),
        "c_ident": np.eye(128, dtype=np.float32),
        "c_tri": np.triu(np.ones((128, 128), np.float32)),
        "c_triu": np.tril(np.ones((128, 128), np.float32)),
    }
    inv = (10000.0 ** (-np.arange(0, 32, 2, dtype=np.float32) / 32)).astype(np.float32)
    common["c_invf"] = np.concatenate([inv, inv])[None, :].astype(np.float32)
    maps = []
    for c in range(n_cores):
        b, q = c // NQ, c % NQ
        xp = np.zeros((S + 4, D), np.float32)
        xp[2:S + 2] = x[b]
        nch = S // 128
        own = list(range(q * NO, (q + 1) * NO))
        prev = list(range(q * NO - 1, -1, -1))
        nxt = list(range((q + 1) * NO, nch))
        slots = prev + nxt

        def gather(chs):
            if not chs:
                return np.zeros((0, 132, D), np.float32)
            return np.stack([xp[ch * 128:ch * 128 + 132] for ch in chs])

        def pos(chs):
            n = max(len(chs), 1)
            p = np.zeros((128, n, 2), np.float32)
            for i, ch in enumerate(chs):
                t = ch * 128 + np.arange(128)
                p[:, i, 0] = t // GRID_W
                p[:, i, 1] = t % GRID_W
            return p
        mk = np.zeros((128, max(NS, 1), 2), np.float32)
        mk[:, :len(prev), 0] = 1.0
        mk[:, len(prev):len(slots), 1] = 1.0
        m = dict(common)
        m["xs"] = gather(slots)
        m["xo"] = gather(own)
        m["poss"] = pos(slots)
        m["poso"] = pos(own)
        m["mk"] = mk
        maps.append(m)
    return maps


_NC_CACHE = {}


def kernel(**inputs):
    x = np.asarray(inputs["x"])
    B, S, _ = x.shape
    n_cores = 8
    NQ = n_cores // B
    T = S // NQ
    key = (S, T)
    if key not in _NC_CACHE:
        _NC_CACHE[key] = build(S, T)
    nc = _NC_CACHE[key]
    maps = make_in_maps(inputs, S, T, n_cores)
    res = run_bass_kernel_spmd(nc, maps, core_ids=list(range(n_cores)))
    out = np.zeros((B, S, D), np.float32)
    for c in range(n_cores):
        b, q = c // NQ, c % NQ
        out[b, q * T:(q + 1) * T] = np.asarray(res.results[c]["out"]).reshape(T, D)
    return out
```

```python
import math
import numpy as np
import concourse.bass as bass
import concourse.mybir as mybir
from concourse.bass_utils import run_bass_kernel_spmd
from contextlib import ExitStack

F32 = mybir.dt.float32
BF16 = mybir.dt.bfloat16
AF = mybir.ActivationFunctionType
ALU = mybir.AluOpType
AX = mybir.AxisListType

D = 1024
GRID_W = 64
HD = 64
NQH = 8
NKV = 2
SSD_H = 16
SSD_P = 64
SSD_N = 128
CONV_K = 5
D_FF = 2816
EPS = 1e-6
OQ, OK_, OV, OZ, OXS, OB, OC, ODT, OG = 0, 512, 640, 768, 1792, 2816, 3072, 3328, 3360
NEG = -30000.0


class Buf:
    __slots__ = ("name", "t", "lw", "rd", "dsem", "dram", "root")

    def __init__(self, name, t=None, dram=False, root=None):
        self.root = root if root is not None else self
        self.name = name
        self.t = t
        self.lw = None
        self.rd = {}
        self.dsem = None
        self.dram = dram

    def __getitem__(self, idx):
        return self.t[idx]


class Prog:
    ENG = ("pe", "act", "dve", "pool", "sp")

    def __init__(self, nc, stack):
        self.nc = nc
        self.stack = stack
        self.sems = []
        self.q = {}
        for e in self.ENG:
            s = self._new_sem("q_" + e)
            self.q[e] = {"ops": [], "cnt": 0, "sem": s, "seen": {}}
        self.dma_cnt = {}
        self.uid = 0
        self.clock = {}

    def _new_sem(self, name):
        name = f"{name}_{len(self.sems)}"
        h = self.stack.enter_context(self.nc.semaphore(name))
        self.sems.append(h)
        return len(self.sems) - 1

    def sbuf(self, name, shape, dt, stack=None):
        self.uid += 1
        t = (stack or self.stack).enter_context(self.nc.sbuf_tensor(f"{name}_{self.uid}", list(shape), dt))
        return Buf(name, t)

    def psum(self, name, shape, dt=F32, stack=None):
        self.uid += 1
        t = (stack or self.stack).enter_context(self.nc.psum_tensor(f"{name}_{self.uid}", list(shape), dt))
        return Buf(name, t)

    def op(self, qn, fn, reads=(), writes=(), dma=False):
        q = self.q[qn]
        deps = {}
        reads = [b.root for b in reads]
        writes = [b.root for b in writes]

        def add(ev):
            if ev is None:
                return
            s, v = ev
            if deps.get(s, 0) < v:
                deps[s] = v

        for b in reads:
            add(b.lw)
        for b in writes:
            add(b.lw)
            for s, v in b.rd.items():
                add((s, v))
        if dma:
            b0 = [b for b in list(writes) + list(reads) if not b.dram][0]
            if b0.dsem is None:
                b0.dsem = self._new_sem("d_" + b0.name)
            key = b0.dsem
            c = self.dma_cnt.get(key, 0)
            if c:
                add((key, c))
            c += 16
            self.dma_cnt[key] = c
            ev = (key, c)
            inc = 16
        else:
            q["cnt"] += 1
            ev = (q["sem"], q["cnt"])
            inc = 1
        seen = q["seen"]
        for s, v in sorted(deps.items(), key=lambda kv: -kv[1]):
            if qn == "pe" and s == q["sem"]:
                continue
            if seen.get(s, 0) >= v:
                continue
            seen[s] = v
            q["ops"].append(("w", s, v))
            snap = self.clock.get((s, v))
            if snap:
                for s2, v2 in snap.items():
                    if seen.get(s2, 0) < v2:
                        seen[s2] = v2
        q["ops"].append(("i", fn, ev[0], inc))
        snap = dict(seen)
        if not dma:
            snap[ev[0]] = max(snap.get(ev[0], 0), ev[1] - 1)
        self.clock[ev] = snap
        for b in reads:
            if b.rd.get(ev[0], 0) < ev[1]:
                b.rd[ev[0]] = ev[1]
        for b in writes:
            b.lw = ev
            b.rd = {}
        return ev

    def barrier(self):
        targets = [(self.q[e]["sem"], self.q[e]["cnt"]) for e in self.ENG if self.q[e]["cnt"]]
        targets += [(k, c) for k, c in self.dma_cnt.items()]
        for e in self.ENG:
            q = self.q[e]
            for s, v in targets:
                if s == q["sem"]:
                    continue
                if q["seen"].get(s, 0) >= v:
                    continue
                q["seen"][s] = v
                q["ops"].append(("w", s, v))

    def emit(self):
        nc = self.nc
        sems = self.sems

        def replay(e, ops):
            for o in ops:
                if o[0] == "w":
                    e.wait_ge(sems[o[1]], o[2])
                else:
                    o[1](e).then_inc(sems[o[2]], o[3])

        with nc.Block() as block:
            @block.tensor
            def _(e):
                replay(e, self.q["pe"]["ops"])

            @block.scalar
            def _(e):
                replay(e, self.q["act"]["ops"])

            @block.vector
            def _(e):
                replay(e, self.q["dve"]["ops"])

            @block.gpsimd
            def _(e):
                replay(e, self.q["pool"]["ops"])

            @block.sync
            def _(e):
                replay(e, self.q["sp"]["ops"])


def ap_(t, off, dims):
    return bass.AP(t.tensor, off, [list(d) for d in dims])


def build(S, T):
    NO = T // 128
    NS = (S - T) // 128
    NB = S // 128
    NQT = max(1, T // 512)
    QW = T // NQT

    nc = bass.Bass("TRN2", target_bir_lowering=False)

    def din(name, shape):
        return nc.dram_tensor(name, list(shape), F32, kind="ExternalInput").ap()

    xs_d = din("xs", [NS, 132, D])
    xo_d = din("xo", [NO, 132, D])
    poss_d = din("poss", [128, NS, 2])
    poso_d = din("poso", [128, NO, 2])
    mk_d = din("mk", [128, NS, 2])
    w_in_d = din("w_in", [D, 5408])
    qn_d = din("q_norm", [1, 64])
    kn_d = din("k_norm", [1, 64])
    cw_d = din("conv_w", [5, 1536])
    cb_d = din("conv_b", [1, 1536])
    dtb_d = din("dt_bias", [1, 32])
    alog_d = din("a_log", [1, 32])
    dsk_d = din("d_skip", [1, 16])
    sn_d = din("ssd_norm", [1, D])
    wap_d = din("w_attn_proj", [512, D])
    wsp_d = din("w_ssd_proj", [D, D])
    wo_d = din("w_out", [D, D])
    n1a_d = din("norm1_pre", [1, D])
    n1b_d = din("norm1_post", [1, D])
    n2a_d = din("norm2_pre", [1, D])
    n2b_d = din("norm2_post", [1, D])
    wgu_d = din("w_gate_up", [D, 2 * D_FF])
    wd_d = din("w_down", [D_FF, D])
    ident_d = din("c_ident", [128, 128])
    tri_d = din("c_tri", [128, 128])
    triu_d = din("c_triu", [128, 128])
    invf_d = din("c_invf", [1, 32])
    out_d = nc.dram_tensor("out", [NO, 128, D], F32, kind="ExternalOutput").ap()

    def dscr(name, shape, dt):
        return nc.dram_tensor(name, list(shape), dt, kind="Internal").ap()

    s_xs = dscr("s_xs", [NO, 128, 1024], BF16)
    s_bt = dscr("s_bt", [NO, 128, 256], BF16)
    s_bT = dscr("s_bT", [NO, 128, 256], BF16)
    s_cT = dscr("s_cT", [NO, 128, 256], BF16)
    s_yf = dscr("s_yf", [NO, 128, 1024], F32)
    s_ao = dscr("s_ao", [NO, 128, 1024], F32)
    s_x1 = dscr("s_x1", [NO, 128, 1024], F32)
    s_kt = dscr("s_kt", [NB, 128, 128], BF16)
    s_v = dscr("s_v", [NB, 128, 128], BF16)
    s_qt = dscr("s_qt", [NO, 128, 512], BF16)
    s_yb = dscr("s_yb", [NO, 128, 1024], BF16)
    s_ff = dscr("s_ff", [NO, 128, 1024], F32)
    D_kt = [Buf(f"dkt{i}", dram=True) for i in range(NB)]
    D_v = [Buf(f"dv{i}", dram=True) for i in range(NB)]
    D_qt = [Buf(f"dqt{i}", dram=True) for i in range(NO)]
    D_yb = [Buf(f"dyb{i}", dram=True) for i in range(NO)]
    D_ff = [Buf(f"dff{i}", dram=True) for i in range(NO)]
    D_xs = [Buf(f"dxs{i}", dram=True) for i in range(NO)]
    D_bt = [Buf(f"dbt{i}", dram=True) for i in range(NO)]
    D_bT = [Buf(f"dbT{i}", dram=True) for i in range(NO)]
    D_cT = [Buf(f"dcT{i}", dram=True) for i in range(NO)]
    D_yf = [Buf(f"dyf{i}", dram=True) for i in range(NO)]
    D_ao = [Buf(f"dao{i}", dram=True) for i in range(NO)]
    D_x1 = [Buf(f"dx1{i}", dram=True) for i in range(NO)]
    D_out = [Buf(f"dout{i}", dram=True) for i in range(NO)]

    with ExitStack() as st:
        P = Prog(nc, st)

        def DMA(qn, out, in_, R, W):
            P.op(qn, lambda e: e.dma_start(out=out, in_=in_), R, W, dma=True)

        def DMAS(qn, out, in_, R, W):
            P.op(qn, lambda e: e.dma_start(out=out, in_=in_, allow_slow_non_contiguous=True), R, W, dma=True)

        def ACT(out, in_, func, R, W, **kw):
            P.op("act", lambda e: e.activation(out=out, in_=in_, func=func, **kw), R, W)

        def TT(eng, out, a, b, op, R, W):
            P.op(eng, lambda e: e.tensor_tensor(out=out, in0=a, in1=b, op=op), R, W)

        def TS(eng, out, a, s1, s2, op0, op1, R, W):
            if s2 is None:
                P.op(eng, lambda e: e.tensor_scalar(out=out, in0=a, scalar1=s1, scalar2=None, op0=op0), R, W)
            else:
                P.op(eng, lambda e: e.tensor_scalar(out=out, in0=a, scalar1=s1, scalar2=s2, op0=op0, op1=op1), R, W)

        def STT(out, a, s, b, op0, op1, R, W):
            P.op("dve", lambda e: e.scalar_tensor_tensor(out=out, in0=a, scalar=s, in1=b, op0=op0, op1=op1), R, W)

        def CP(eng, out, in_, R, W):
            if eng == "act":
                P.op("act", lambda e: e.copy(out=out, in_=in_), R, W)
            else:
                P.op(eng, lambda e: e.tensor_copy(out=out, in_=in_), R, W)

        def MM(out, lhsT, rhs, start, stop, R, W):
            P.op("pe", lambda e: e.matmul(out, lhsT=lhsT, rhs=rhs, start=start, stop=stop), R, W)

        def TR(out, in_, idn, R, W):
            P.op("pe", lambda e: e.transpose(out=out, in_=in_, identity=idn), R, W)

        def RECIP(out, in_, R, W):
            P.op("dve", lambda e: e.reciprocal(out=out, in_=in_), R, W)

        def RSUM(out, in_, R, W):
            P.op("dve", lambda e: e.reduce_sum(out=out, in_=in_, axis=AX.X), R, W)

        def MEMSET(eng, ap, val, W):
            P.op(eng, lambda e: e.memset(ap, val), (), W)

        cast_rr = [0]

        def cast_eng():
            cast_rr[0] += 1
            return ("dve", "act")[cast_rr[0] % 2]

        identf = P.sbuf("identf", [128, 128], F32)
        ident = P.sbuf("ident", [128, 128], BF16)
        tri = P.sbuf("tri", [128, 128], F32)
        triu = P.sbuf("triu", [128, 128], F32)
        mnegf = P.sbuf("mnegf", [128, 128], BF16)
        mnegb = P.sbuf("mnegb", [128, 128], BF16)
        onesf = P.sbuf("onesf", [128, 128], F32)
        onesb = P.sbuf("onesb", [128, 128], BF16)
        epsc = P.sbuf("epsc", [128, 1], F32)
        onec = P.sbuf("onec", [128, 1], F32)
        npi = P.sbuf("npi", [128, 1], F32)
        invf = P.sbuf("invf", [128, 32], F32)
        gq = P.sbuf("gq", [128, 64], F32)
        gk = P.sbuf("gk", [128, 64], F32)
        negB = P.sbuf("negB", [128, 1], F32)
        dtbias = P.sbuf("dtbias", [128, 32], F32)
        aneg = P.sbuf("aneg", [128, 32], F32)
        dskip = P.sbuf("dskip", [128, 16], F32)
        cbrow = P.sbuf("cbrow", [1, 1536], BF16)
        cbcol = P.sbuf("cbcol", [128, 12], F32)
        cwcol = P.sbuf("cwcol", [128, 5, 12], F32)
        Hf = P.sbuf("Hf", [128, 1024], F32)
        Hb = P.sbuf("Hb", [128, 1024], F32)
        dtb_own = P.sbuf("dtb_own", [128, NO, 16], F32)

        gn1a = P.sbuf("gn1a", [128, D], F32)
        phSA = ExitStack()
        diag = P.sbuf("diag", [128, 5, 12, 128], BF16, phSA)
        tmpc = P.sbuf("tmpc", [1, 1536], F32, phSA)

        DMA("sp", identf[:], ident_d, [], [identf])
        DMA("sp", tri[:], tri_d, [], [tri])
        DMA("sp", triu[:], triu_d, [], [triu])
        DMA("sp", invf[:], ap_(invf_d, 0, [[0, 128], [1, 32]]), [], [invf])
        DMA("sp", gq[:], ap_(qn_d, 0, [[0, 128], [1, 64]]), [], [gq])
        DMA("sp", gk[:], ap_(kn_d, 0, [[0, 128], [1, 64]]), [], [gk])
        DMA("sp", dtbias[:], ap_(dtb_d, 0, [[0, 128], [1, 32]]), [], [dtbias])
        DMA("sp", aneg[:], ap_(alog_d, 0, [[0, 128], [1, 32]]), [], [aneg])
        DMA("sp", dskip[:], ap_(dsk_d, 0, [[0, 128], [1, 16]]), [], [dskip])
        DMA("sp", tmpc[0:1, :], cb_d, [], [tmpc])
        CP("dve", cbrow[:], tmpc[0:1, :], [tmpc], [cbrow])
        DMAS("sp", cbcol[:], ap_(cb_d, 0, [[1, 128], [128, 12]]), [], [cbcol])
        for j in range(5):
            DMAS("sp", cwcol[:, j, :], ap_(cw_d, j * 1536, [[1, 128], [128, 12]]), [], [cwcol])
        CP("dve", ident[:], identf[:], [identf], [ident])
        MEMSET("pool", onesf[:], 1.0, [onesf])
        MEMSET("pool", onesb[:], 1.0, [onesb])
        MEMSET("pool", epsc[:], EPS, [epsc])
        MEMSET("pool", onec[:], 1.0, [onec])
        MEMSET("pool", npi[:], -math.pi, [npi])
        TS("dve", mnegf[:], tri[:], -1.0, -NEG, ALU.add, ALU.mult, [tri], [mnegf])
        TS("dve", mnegb[:], triu[:], -1.0, -NEG, ALU.add, ALU.mult, [triu], [mnegb])
        ACT(aneg[:], aneg[:], AF.Exp, [aneg], [aneg])
        TS("dve", aneg[:], aneg[:], -1.0, None, ALU.mult, None, [aneg], [aneg])
        for j in range(5):
            for ct in range(12):
                TS("dve", diag[:, j, ct, :], identf[:], cwcol[:, j, ct:ct + 1], None,
                   ALU.mult, None, [identf, cwcol], [diag])
        mq = P.sbuf("mq", [128, 2], F32, phSA)
        absq = P.sbuf("absq", [128, 64], F32, phSA)
        TS("dve", absq[:], gq[:], -1.0, None, ALU.mult, None, [gq], [absq])
        TT("dve", absq[:], absq[:], gq[:], ALU.max, [absq, gq], [absq])
        P.op("dve", lambda e: e.reduce_max(out=mq[:, 0:1], in_=absq[:], axis=AX.X), [absq], [mq])
        TS("dve", absq[:], gk[:], -1.0, None, ALU.mult, None, [gk], [absq])
        TT("dve", absq[:], absq[:], gk[:], ALU.max, [absq, gk], [absq])
        P.op("dve", lambda e: e.reduce_max(out=mq[:, 1:2], in_=absq[:], axis=AX.X), [absq], [mq])
        TT("dve", negB[:], mq[:, 0:1], mq[:, 1:2], ALU.mult, [mq], [negB])
        TS("dve", negB[:], negB[:], -8.0, None, ALU.mult, None, [negB], [negB])

        DMA("sp", gn1a[:], ap_(n1a_d, 0, [[0, 128], [1, D]]), [], [gn1a])

        def load_w(dst, dcol0, src_ap, ncols, stage, cw_max=256):
            K = dst.t.shape[1]
            i = 0
            for c0 in range(0, ncols, cw_max):
                cw = min(cw_max, ncols - c0)
                sg = stage[i % len(stage)]
                i += 1
                sv = sg[:, 0:K * cw].rearrange("p (k n) -> p k n", n=cw)
                DMA("sp", sv, src_ap[:, c0:c0 + cw].rearrange("(k p) n -> p k n", p=128), [], [sg])
                CP(cast_eng(), dst[:, :, dcol0 + c0:dcol0 + c0 + cw], sv, [sg], [dst])

        def rmsnorm_tok(x_ap, np_, gain_ap, out_ap, R, W, tmp):
            sq, ss = tmp["sq"], tmp["ss"]
            ACT(sq[0:np_, :], x_ap, AF.Square, R, [sq, ss], accum_out=ss[0:np_, 0:1])
            ACT(ss[0:np_, 0:1], ss[0:np_, 0:1], AF.Sqrt, [ss, epsc], [ss], scale=1.0 / D, bias=epsc[0:np_, :])
            RECIP(ss[0:np_, 0:1], ss[0:np_, 0:1], [ss], [ss])
            STT(out_ap, x_ap, ss[0:np_, 0:1], gain_ap, ALU.mult, ALU.mult, list(R) + [ss], W)

        def transpose8(src_ap_fn, dst_fn, R, W, pb_bf, tpb):
            for half in range(2):
                for k in range(4):
                    TR(pb_bf[:, k * 128:(k + 1) * 128], src_ap_fn(half * 4 + k), ident[:], list(R) + [ident], [tpb])
                CP("act" if half else "dve", dst_fn(half), pb_bf[:, 0:512].rearrange("p (k t) -> p k t", t=128), [tpb], W)

        def ssd_chunk(E, direction, adt_buf, ac0, dt_buf, dc0, xst, bt, bTt, cTt, H, y_init, y_dst):
            pb_st, st_acs, st_tot, pb_cv, pb_tok, accs, sm = (E[k] for k in
                                                              ("pb_st", "st_acs", "st_tot", "pb_cv", "pb_tok", "accs", "sm"))
            Rm, Em, Mm, xd, xdd, Hbf, ytmp, Htmp, cbt = (E[k] for k in
                                                        ("Rm", "Em", "Mm", "xd", "xdd", "Hbf", "ytmp", "Htmp", "cbt"))
            trm = tri if direction == 0 else triu
            mneg = mnegf if direction == 0 else mnegb
            aw = adt_buf.t.shape[1]
            dw = dt_buf.t.shape[1]
            adt16 = adt_buf[:, ac0:ac0 + 16]
            dt16 = dt_buf[:, dc0:dc0 + 16]
            v3 = lambda ap: ap.rearrange("p (h d) -> p h d", d=64)
            bcs = lambda buf, c0, n, rep: ap_(buf.t[:], c0, [[buf.t.shape[1], 128], [1, n], [0, rep]])
            MM(pb_st[:, 0:16], trm[:], adt16, True, True, [trm, adt_buf], [st_acs])
            MM(pb_st[:, 16:32], onesf[:], adt16, True, True, [onesf, adt_buf], [st_tot])
            CP("dve", sm["acs_sb"][:], pb_st[:, 0:16], [st_acs], [sm["acs_sb"]])
            TS("dve", sm["nacs"][:], sm["acs_sb"][:], -1.0, None, ALU.mult, None, [sm["acs_sb"]], [sm["nacs"]])
            ACT(sm["ea"][:], sm["acs_sb"][:], AF.Exp, [sm["acs_sb"]], [sm["ea"]])
            TT("dve", sm["ds"][:], pb_st[:, 16:32], sm["acs_sb"][:], ALU.subtract, [st_tot, sm["acs_sb"]], [sm["ds"]])
            ACT(sm["ds"][:], sm["ds"][:], AF.Exp, [sm["ds"]], [sm["ds"]])
            ACT(sm["edec"][:], pb_st[:, 16:32], AF.Exp, [st_tot], [sm["edec"]])
            TT("dve", sm["dtds"][:], dt16, sm["ds"][:], ALU.mult, [dt_buf, sm["ds"]], [sm["dtds"]])
            TT("dve", Rm[:], ap_(trm.t[:], 0, [[128, 128], [0, 16], [1, 128]]),
               ap_(adt_buf.t[:], ac0, [[aw, 128], [1, 16], [0, 128]]), ALU.mult, [trm, adt_buf], [Rm])
            for g in range(2):
                MM(pb_cv[:, g * 128:(g + 1) * 128], bTt[:, g * 128:(g + 1) * 128], cTt[:, g * 128:(g + 1) * 128],
                   True, True, [bTt, cTt], [pb_cv])
            CP("act", cbt[:], pb_cv[:, 0:256], [pb_cv], [cbt])
            for qd in range(4):
                bank = accs[qd]
                MM(bank[:, :], onesf[:], Rm[:, qd * 4:(qd + 1) * 4, :], True, False, [onesf, Rm], [bank])
                MM(bank[:, :], ident[:], ap_(mneg.t[:], 0, [[128, 128], [0, 4], [1, 128]]), False, True, [ident, mneg], [bank])
                for hh in range(4):
                    h = qd * 4 + hh
                    ACT(Em[:, h, :], bank[:, hh * 128:(hh + 1) * 128], AF.Exp, [bank, sm["nacs"]], [Em],
                        bias=sm["nacs"][:, h:h + 1])
            for g in range(2):
                TT("dve", Mm[:, g * 8:(g + 1) * 8, :], Em[:, g * 8:(g + 1) * 8, :],
                   ap_(cbt.t[:], g * 128, [[256, 128], [0, 8], [1, 128]]), ALU.mult, [Em, cbt], [Mm])
            TT("dve", v3(xd[:]), v3(xst[:]), ap_(dt_buf.t[:], dc0, [[dw, 128], [1, 16], [0, 64]]), ALU.mult, [xst, dt_buf], [xd])
            TT("dve", v3(xdd[:]), v3(xst[:]), bcs(sm["dtds"], 0, 16, 64), ALU.mult, [xst, sm["dtds"]], [xdd])
            CP("act", Hbf[:], H[:], [H], [Hbf])
            for g in range(2):
                MM(pb_tok[:, :], cTt[:, g * 128:(g + 1) * 128], Hbf[:, g * 512:(g + 1) * 512], True, True, [cTt, Hbf], [pb_tok])
                TT("dve", v3(ytmp[:, g * 512:(g + 1) * 512]), v3(pb_tok[:, :]), bcs(sm["ea"], g * 8, 8, 64), ALU.mult,
                   [pb_tok, sm["ea"]], [ytmp])
            TT("dve", ytmp[:], ytmp[:], y_init[:], ALU.add, [ytmp, y_init], [ytmp])
            for g in range(2):
                for hh in range(8):
                    h = g * 8 + hh
                    MM(pb_cv[:, hh * 64:(hh + 1) * 64], Mm[:, h, :], xd[:, h * 64:(h + 1) * 64], True, True, [Mm, xd], [pb_cv])
                TT("dve", y_dst[:, g * 512:(g + 1) * 512], ytmp[:, g * 512:(g + 1) * 512], pb_cv[:, :], ALU.add,
                   [ytmp, pb_cv], [y_dst])
            for g in range(2):
                MM(pb_tok[:, :], bt[:, g * 128:(g + 1) * 128], xdd[:, g * 512:(g + 1) * 512], True, True, [bt, xdd], [pb_tok])
                TT("dve", v3(Htmp[:, g * 512:(g + 1) * 512]), v3(H[:, g * 512:(g + 1) * 512]), bcs(sm["edec"], g * 8, 8, 64),
                   ALU.mult, [H, sm["edec"]], [Htmp])
                TT("dve", H[:, g * 512:(g + 1) * 512], Htmp[:, g * 512:(g + 1) * 512], pb_tok[:, :], ALU.add,
                   [Htmp, pb_tok], [H])

        with phSA as ph:
            WQ0, WKV0, WX0, WDT0 = 0, 512, 768, 2304
            Wa = P.sbuf("Wa", [128, 8, 2336], BF16, ph)
            stage = [P.sbuf(f"stg{i}", [128, 2048], F32, ph) for i in range(2)]
            for i in range(4):
                for g in range(2):
                    h = 4 * g + i
                    load_w(Wa, (2 * i + g) * 64, w_in_d[:, OQ + h * 64:OQ + (h + 1) * 64], 64, stage)
            load_w(Wa, WKV0, w_in_d[:, OK_:OK_ + 256], 256, stage)
            load_w(Wa, WX0, w_in_d[:, OXS:OXS + 1536], 1536, stage)
            load_w(Wa, WDT0, w_in_d[:, ODT:ODT + 32], 32, stage)

            tmp = {"sq": P.sbuf("sq", [128, D], F32, ph), "ss": P.sbuf("ss", [128, 8], F32, ph)}
            x_main = [P.sbuf(f"xm{i}", [128, D], F32, ph) for i in range(2)]
            x_halo = [P.sbuf(f"xh{i}", [4, D], F32, ph) for i in range(2)]
            hbuf = P.sbuf("hbuf", [128, 2, D], BF16, ph)
            hT = [P.sbuf(f"hT{i}", [128, 8, 132], BF16, ph) for i in range(2)]
            pre = P.sbuf("pre", [128, 12, 132], BF16, ph)
            xs_tok = [P.sbuf(f"xstok{i}", [128, 1024], BF16, ph) for i in range(2)]
            b_tok = [P.sbuf(f"btok{i}", [128, 256], BF16, ph) for i in range(2)]
            bT = [P.sbuf(f"bT{i}", [128, 256], BF16, ph) for i in range(2)]
            cT = [P.sbuf(f"cT{i}", [128, 256], BF16, ph) for i in range(2)]
            poss = P.sbuf("poss", [128, max(NS, 1), 2], F32, ph)
            poso = P.sbuf("poso", [128, NO, 2], F32, ph)
            mk = P.sbuf("mk", [128, max(NS, 1), 2], F32, ph)
            cos_t = P.sbuf("cos_t", [128, 32], F32, ph)
            sin_t = P.sbuf("sin_t", [128, 32], F32, ph)
            ang_t = P.sbuf("ang_t", [128, 32], F32, ph)
            rr_x = P.sbuf("rr_x", [128, 32], F32, ph)
            rr_k = P.sbuf("rr_k", [128, 32], F32, ph)
            rr_i = P.sbuf("rr_i", [128, 32], mybir.dt.int32, ph)
            qf = P.sbuf("qf", [128, 512], F32, ph)
            sq2 = P.sbuf("sq2", [128, 512], F32, ph)
            ss2 = P.sbuf("ss2", [128, 8], F32, ph)
            qn = P.sbuf("qn", [128, 512], F32, ph)
            t1 = P.sbuf("t1", [128, 128], F32, ph)
            t2 = P.sbuf("t2", [128, 128], F32, ph)
            sp_x = P.sbuf("sp_x", [128, 32], F32, ph)
            sp_a = P.sbuf("sp_a", [128, 32], F32, ph)
            sp_e = P.sbuf("sp_e", [128, 32], F32, ph)
            dt = P.sbuf("dt", [128, 32], F32, ph)
            adt = P.sbuf("adt", [128, 32], F32, ph)
            krot = [P.sbuf(f"krot{i}", [128, 128], BF16, ph) for i in range(2)]
            ktb = [P.sbuf(f"ktb{i}", [128, 128], BF16, ph) for i in range(2)]
            vb = [P.sbuf(f"vb{i}", [128, 128], BF16, ph) for i in range(2)]
            qrot = P.sbuf("qrot", [128, 512], BF16, ph)
            qtb = [P.sbuf(f"qtb{i}", [128, 512], BF16, ph) for i in range(2)]
            sm = {n_: P.sbuf(n_, [128, 16], F32, ph) for n_ in
                  ("ds", "offf", "offb", "tmp16", "dtds", "ea", "nacs", "acs_sb", "edec")}
            xdd = P.sbuf("xdd", [128, 1024], BF16, ph)
            xd = P.sbuf("xd", [128, 1024], BF16, ph)
            bsel = P.sbuf("bsel", [128, 2, 256], BF16, ph)
            Rm = P.sbuf("Rm", [128, 16, 128], F32, ph)
            Em = P.sbuf("Em", [128, 16, 128], BF16, ph)
            Mm = P.sbuf("Mm", [128, 16, 128], BF16, ph)
            Hbf = P.sbuf("Hbf", [128, 1024], BF16, ph)
            ytmp = P.sbuf("ytmp", [128, 1024], F32, ph)
            yout = [P.sbuf(f"yout{i}", [128, 1024], F32, ph) for i in range(2)]
            Htmp = P.sbuf("Htmp", [128, 1024], F32, ph)
            cbt = P.sbuf("cbt", [128, 256], F32, ph)

            pb_bf = P.psum("pb_bf", [128, 1024], BF16, ph)
            tpb = tph = tpk = tpq = pb_bf
            pb_tok = P.psum("pb_tok", [128, 512], F32, ph)
            pb_pre = P.psum("pb_pre", [128, 512], F32, ph)
            pb_st = Buf("pb_st", pb_pre.t[:, 400:512], root=pb_pre)
            st_acs = st_tot = pb_st
            prebank = [pb_pre] * 3
            pb_cv = P.psum("pb_cv", [128, 512], F32, ph)
            accs = [P.psum(f"acc{i}", [128, 512], F32, ph) for i in range(4)]

            if NS:
                DMA("sp", poss[:], poss_d, [], [poss])
                DMA("sp", mk[:], mk_d, [], [mk])
            DMA("sp", poso[:], poso_d, [], [poso])
            MEMSET("pool", sm["offf"][:], 0.0, [sm["offf"]])
            MEMSET("pool", sm["offb"][:], 0.0, [sm["offb"]])

            def rope_chunk(pos_buf, ti):
                nt = pos_buf.t.shape[1]
                for a in range(2):
                    TT("dve", ang_t[:, a * 16:(a + 1) * 16],
                       ap_(pos_buf.t[:], ti * 2 + a, [[nt * 2, 128], [0, 16]]),
                       invf[:, a * 16:(a + 1) * 16], ALU.mult, [pos_buf, invf], [ang_t])
                for dst, shift in ((sin_t, 0.0), (cos_t, 0.5 * math.pi)):
                    TS("dve", rr_x[:], ang_t[:], shift, None, ALU.add, None, [ang_t], [rr_x])
                    TS("dve", rr_i[:], rr_x[:], 1.0 / (2 * math.pi), None, ALU.mult, None, [rr_x], [rr_i])
                    CP("dve", rr_k[:], rr_i[:], [rr_i], [rr_k])
                    STT(dst[:], rr_k[:], -2 * math.pi, rr_x[:], ALU.mult, ALU.add, [rr_k, rr_x], [dst])
                    TS("dve", rr_k[:], dst[:], math.pi, -2 * math.pi, ALU.is_gt, ALU.mult, [dst], [rr_k])
                    TT("dve", dst[:], dst[:], rr_k[:], ALU.add, [dst, rr_k], [dst])
                    TS("dve", rr_k[:], dst[:], -math.pi, 2 * math.pi, ALU.is_lt, ALU.mult, [dst], [rr_k])
                    TT("dve", dst[:], dst[:], rr_k[:], ALU.add, [dst, rr_k], [dst])
                    ACT(dst[:], dst[:], AF.Sin, [dst], [dst])

            def qknorm_rope(src_ps, src_buf, H, gain, out_bf):
                n = H * 64
                CP("act", qf[:, 0:n], src_ps, [src_buf], [qf])
                TT("dve", sq2[:, 0:n], qf[:, 0:n], qf[:, 0:n], ALU.mult, [qf], [sq2])
                RSUM(ss2[:, 0:H], sq2[:, 0:n].rearrange("p (h d) -> p h d", d=64), [sq2], [ss2])
                ACT(ss2[:, 0:H], ss2[:, 0:H], AF.Sqrt, [ss2, epsc], [ss2], scale=1.0 / 64, bias=epsc[:])
                RECIP(ss2[:, 0:H], ss2[:, 0:H], [ss2], [ss2])
                TT("dve", qn[:, 0:n].rearrange("p (h d) -> p h d", d=64), qf[:, 0:n].rearrange("p (h d) -> p h d", d=64),
                   ap_(ss2.t[:], 0, [[8, 128], [1, H], [0, 64]]), ALU.mult, [qf, ss2], [qn])
                TT("dve", qn[:, 0:n].rearrange("p (h d) -> p h d", d=64), qn[:, 0:n].rearrange("p (h d) -> p h d", d=64),
                   ap_(gain.t[:], 0, [[64, 128], [0, H], [1, 64]]), ALU.mult, [qn, gain], [qn])
                ow = out_bf.t.shape[1]
                for a in range(2):
                    def xv(buf, half, wdt):
                        return ap_(buf.t[:], a * 32 + half * 16, [[wdt, 128], [64, H], [1, 16]])
                    cs = ap_(cos_t.t[:], a * 16, [[32, 128], [0, H], [1, 16]])
                    sn = ap_(sin_t.t[:], a * 16, [[32, 128], [0, H], [1, 16]])
                    t1v = ap_(t1.t[:], 0, [[128, 128], [16, H], [1, 16]])
                    t2v = ap_(t2.t[:], 0, [[128, 128], [16, H], [1, 16]])
                    TT("dve", t1v, xv(qn, 0, 512), cs, ALU.mult, [qn, cos_t], [t1])
                    TT("dve", t2v, xv(qn, 1, 512), sn, ALU.mult, [qn, sin_t], [t2])
                    TT("dve", xv(out_bf, 0, ow), t1v, t2v, ALU.subtract, [t1, t2], [out_bf])
                    TT("dve", t1v, xv(qn, 1, 512), cs, ALU.mult, [qn, cos_t], [t1])
                    TT("dve", t2v, xv(qn, 0, 512), sn, ALU.mult, [qn, sin_t], [t2])
                    TT("dve", xv(out_bf, 1, ow), t1v, t2v, ALU.add, [t1, t2], [out_bf])

            def front(x_src_ap, hTi, xm, xh):
                DMA("sp", xm[:], x_src_ap[2:130, :], [], [xm])
                DMA("sp", xh[0:2, :], x_src_ap[0:2, :], [], [xh])
                DMA("sp", xh[2:4, :], x_src_ap[130:132, :], [], [xh])
                rmsnorm_tok(xm[:], 128, gn1a[:], hbuf[:, 0, :], [xm, gn1a], [hbuf], tmp)
                rmsnorm_tok(xh[0:4, :], 4, gn1a[0:4, :], hbuf[0:4, 1, :], [xh, gn1a], [hbuf], tmp)
                transpose8(lambda kk: hbuf[:, 0, kk * 128:(kk + 1) * 128],
                           lambda half: hTi[:, half * 4:(half + 1) * 4, 2:130], [hbuf], [hTi], pb_bf, tpb)
                for kk in range(8):
                    TR(pb_bf[:, 512 + kk * 4:512 + (kk + 1) * 4], hbuf[0:4, 1, kk * 128:(kk + 1) * 128], ident[0:4, 0:4],
                       [hbuf, ident], [tph])
                hv = pb_bf[:, 512:544].rearrange("p (k t) -> p k t", t=4)
                CP("dve", hTi[:, :, 0:2], hv[:, :, 0:2], [tph], [hTi])
                CP("dve", hTi[:, :, 130:132], hv[:, :, 2:4], [tph], [hTi])

            def featmaj_pre(hTi, ntiles):
                for ct in range(ntiles):
                    pbk = pb_pre if ct % 2 == 0 else pb_cv
                    pv = pbk[:, 0:132]
                    for k in range(8):
                        MM(pv, Wa[:, k, WX0 + ct * 128:WX0 + (ct + 1) * 128], hTi[:, k, 0:132], k == 0, k == 7, [Wa, hTi], [pbk])
                    CP("act" if ct % 2 else "dve", pre[:, ct, :], pv, [pbk], [pre])

            def conv_tok(tiles, cb_col0, ncols):
                MM(pb_cv[:, 0:ncols], onesb[0:1, 0:128], cbrow[0:1, cb_col0:cb_col0 + ncols], True, False, [onesb, cbrow], [pb_cv])
                n = len(tiles)
                for i, ct in enumerate(tiles):
                    for j in range(5):
                        MM(pb_cv[:, i * 128:(i + 1) * 128], pre[:, ct, j:j + 128], diag[:, j, ct, :], False,
                           (i == n - 1 and j == 4), [pre, diag], [pb_cv])

            def softplus_dt():
                TT("dve", sp_x[:], pb_tok[:, 256:288], dtbias[:], ALU.add, [pb_tok, dtbias], [sp_x])
                TS("dve", sp_a[:], sp_x[:], -1.0, None, ALU.mult, None, [sp_x], [sp_a])
                TT("dve", sp_a[:], sp_a[:], sp_x[:], ALU.max, [sp_a, sp_x], [sp_a])
                ACT(sp_e[:], sp_a[:], AF.Exp, [sp_a], [sp_e], scale=-1.0)
                ACT(sp_e[:], sp_e[:], AF.Ln, [sp_e, onec], [sp_e], bias=onec[:])
                TS("dve", sp_x[:], sp_x[:], 0.0, None, ALU.max, None, [sp_x], [sp_x])
                TT("dve", dt[:], sp_x[:], sp_e[:], ALU.add, [sp_x, sp_e], [dt])

            def kv_mm(hTi):
                for k in range(8):
                    MM(pb_tok[:, 0:256], hTi[:, k, 2:130], Wa[:, k, WKV0:WKV0 + 256], k == 0, k == 7, [hTi, Wa], [pb_tok])
                for k in range(8):
                    MM(pb_tok[:, 256:288], hTi[:, k, 2:130], Wa[:, k, WDT0:WDT0 + 32], k == 0, k == 7, [hTi, Wa], [pb_tok])

            def kv_post(blk, r):
                CP("act", vb[r][:], pb_tok[:, 128:256], [pb_tok], [vb[r]])
                DMA("pool", s_v[blk], vb[r][:], [vb[r]], [D_v[blk]])
                qknorm_rope(pb_tok[:, 0:128], pb_tok, 2, gk, krot[r])
                TR(pb_bf[:, 640:768], krot[r][:], ident[:], [krot[r], ident], [tpk])
                CP("act", ktb[r][:], pb_bf[:, 640:768], [tpk], [ktb[r]])
                DMA("pool", s_kt[blk], ktb[r][:], [ktb[r]], [D_kt[blk]])

            def xs_b_tok(xst, bt):
                for half in range(2):
                    conv_tok(list(range(half * 4, half * 4 + 4)), half * 512, 512)
                    ACT(xst[:, half * 512:(half + 1) * 512], pb_cv[:, 0:512], AF.Silu, [pb_cv], [xst])
                conv_tok([8, 9], 1024, 256)
                ACT(bt[:], pb_cv[:, 0:256], AF.Silu, [pb_cv], [bt])

            bc16 = lambda buf, c0, n, rep: ap_(buf.t[:], c0, [[buf.t.shape[1], 128], [1, n], [0, rep]])
            v3 = lambda ap: ap.rearrange("p (h d) -> p h d", d=64)

            for s in range(NS):
                r = s % 2
                hTi = hT[r]
                front(xs_d[s], hTi, x_main[r], x_halo[r])
                rope_chunk(poss, s)
                kv_mm(hTi)
                featmaj_pre(hTi, 10)
                softplus_dt()
                kv_post(NO + s, r)
                for a in range(2):
                    TS("dve", dt[:, a * 16:(a + 1) * 16], dt[:, a * 16:(a + 1) * 16], mk[:, s, a:a + 1], None,
                       ALU.mult, None, [dt, mk], [dt])
                TT("dve", adt[:], dt[:], aneg[:], ALU.mult, [dt, aneg], [adt])
                MM(pb_st[:, 0:16], tri[:], adt[:, 0:16], True, False, [tri, adt], [st_acs])
                MM(pb_st[:, 0:16], triu[:], adt[:, 16:32], False, True, [triu, adt], [st_acs])
                MM(pb_st[:, 16:32], onesf[:], adt[:, 0:16], True, False, [onesf, adt], [st_tot])
                MM(pb_st[:, 16:32], onesf[:], adt[:, 16:32], False, True, [onesf, adt], [st_tot])
                TS("dve", sm["tmp16"][:], sm["offf"][:], mk[:, s, 0:1], None, ALU.mult, None, [sm["offf"], mk], [sm["tmp16"]])
                STT(sm["tmp16"][:], sm["offb"][:], mk[:, s, 1:2], sm["tmp16"][:], ALU.mult, ALU.add,
                    [sm["offb"], mk, sm["tmp16"]], [sm["tmp16"]])
                TT("dve", sm["ds"][:], pb_st[:, 16:32], sm["tmp16"][:], ALU.add, [st_tot, sm["tmp16"]], [sm["ds"]])
                TT("dve", sm["ds"][:], sm["ds"][:], pb_st[:, 0:16], ALU.subtract, [sm["ds"], st_acs], [sm["ds"]])
                ACT(sm["ds"][:], sm["ds"][:], AF.Exp, [sm["ds"]], [sm["ds"]])
                TT("dve", sm["dtds"][:], dt[:, 0:16], dt[:, 16:32], ALU.add, [dt], [sm["dtds"]])
                TT("dve", sm["dtds"][:], sm["dtds"][:], sm["ds"][:], ALU.mult, [sm["dtds"], sm["ds"]], [sm["dtds"]])
                STT(sm["offf"][:], pb_st[:, 16:32], mk[:, s, 0:1], sm["offf"][:], ALU.mult, ALU.add,
                    [st_tot, mk, sm["offf"]], [sm["offf"]])
                STT(sm["offb"][:], pb_st[:, 16:32], mk[:, s, 1:2], sm["offb"][:], ALU.mult, ALU.add,
                    [st_tot, mk, sm["offb"]], [sm["offb"]])
                xst, bt = xs_tok[r], b_tok[r]
                xs_b_tok(xst, bt)
                TT("dve", v3(xdd[:]), v3(xst[:]), bc16(sm["dtds"], 0, 16, 64), ALU.mult, [xst, sm["dtds"]], [xdd])
                for a in range(2):
                    TS("dve", bsel[:, a, :], bt[:], mk[:, s, a:a + 1], None, ALU.mult, None, [bt, mk], [bsel])
                for a in range(2):
                    for g in range(2):
                        MM(accs[a * 2 + g][:, :], bsel[:, a, g * 128:(g + 1) * 128], xdd[:, g * 512:(g + 1) * 512],
                           s == 0, s == NS - 1, [bsel, xdd], [accs[a * 2 + g]])
            if NS:
                for g in range(2):
                    CP("dve", Hf[:, g * 512:(g + 1) * 512], accs[g][:, :], [accs[g]], [Hf])
                    CP("act", Hb[:, g * 512:(g + 1) * 512], accs[2 + g][:, :], [accs[2 + g]], [Hb])
            else:
                MEMSET("pool", Hf[:], 0.0, [Hf])
                MEMSET("pool", Hb[:], 0.0, [Hb])

            for j in range(NO):
                r = j % 2
                hTi = hT[r]
                front(xo_d[j], hTi, x_main[r], x_halo[r])
                rope_chunk(poso, j)
                kv_mm(hTi)
                for k in range(8):
                    MM(accs[0][:, :], hTi[:, k, 2:130], Wa[:, k, WQ0:WQ0 + 512], k == 0, k == 7, [hTi, Wa], [accs[0]])
                featmaj_pre(hTi, 12)
                softplus_dt()
                kv_post(j, r)
                qknorm_rope(accs[0][:, :], accs[0], 8, gq, qrot)
                for i in range(4):
                    TR(pb_bf[:, 768:896], qrot[:, i * 128:(i + 1) * 128], ident[:], [qrot, ident], [tpq])
                    CP("act", qtb[r][:, i * 128:(i + 1) * 128], pb_bf[:, 768:896], [tpq], [qtb[r]])
                DMA("pool", s_qt[j], qtb[r][:], [qtb[r]], [D_qt[j]])
                CP("dve", dtb_own[:, j, :], dt[:, 16:32], [dt], [dtb_own])
                TT("dve", adt[:], dt[:], aneg[:], ALU.mult, [dt, aneg], [adt])
                xst, bt, bTt, cTt = xs_tok[r], b_tok[r], bT[r], cT[r]
                xs_b_tok(xst, bt)
                for i, (ct, dstb) in enumerate(((8, bTt), (9, bTt), (10, cTt), (11, cTt))):
                    for jj in range(5):
                        MM(pb_cv[:, i * 128:(i + 1) * 128], diag[:, jj, ct, :], pre[:, ct, jj:jj + 128], jj == 0, jj == 4,
                           [pre, diag], [pb_cv])
                    ACT(dstb[:, (i % 2) * 128:(i % 2 + 1) * 128], pb_cv[:, i * 128:(i + 1) * 128], AF.Silu, [pb_cv, cbcol], [dstb],
                        bias=cbcol[:, ct:ct + 1])
                DMA("pool", s_xs[j], xst[:], [xst], [D_xs[j]])
                DMA("pool", s_bt[j], bt[:], [bt], [D_bt[j]])
                DMA("pool", s_bT[j], bTt[:], [bTt], [D_bT[j]])
                DMA("pool", s_cT[j], cTt[:], [cTt], [D_cT[j]])
                yo = yout[r]
                TT("dve", v3(yo[:]), v3(xst[:]), bc16(dskip, 0, 16, 64), ALU.mult, [xst, dskip], [yo])
                ssd_env = dict(pb_st=pb_st, st_acs=st_acs, st_tot=st_tot, pb_cv=pb_cv, pb_tok=pb_tok, accs=accs, sm=sm,
                               Rm=Rm, Em=Em, Mm=Mm, xd=xd, xdd=xdd, Hbf=Hbf, ytmp=ytmp, Htmp=Htmp, cbt=cbt)
                ssd_chunk(ssd_env, 0, adt, 0, dt, 0, xst, bt, bTt, cTt, Hf, yo, yo)
                DMA("pool", s_yf[j], yo[:], [yo], [D_yf[j]])
        P.barrier()

        with ExitStack() as ph:
            KT = P.sbuf("KT", [128, NB, 128], BF16, ph)
            Vs = P.sbuf("Vs", [128, NB, 2, 65], BF16, ph)
            QT = P.sbuf("QT", [128, 4, NO, 128], BF16, ph)
            MEMSET("pool", Vs[:], 1.0, [Vs])
            for b0 in range(0, NB, 16):
                b1 = min(NB, b0 + 16)
                DMA("sp", KT[:, b0:b1, :], s_kt[b0:b1].rearrange("b p k -> p b k"), D_kt[b0:b1], [KT])
                for g in range(2):
                    DMA("sp", Vs[:, b0:b1, g, 0:64], s_v[b0:b1, :, g * 64:(g + 1) * 64].rearrange("b p d -> p b d"),
                        D_v[b0:b1], [Vs])
            for i in range(4):
                for j0 in range(0, NO, 16):
                    j1 = min(NO, j0 + 16)
                    DMA("sp", QT[:, i, j0:j1, :], s_qt[j0:j1, :, i * 128:(i + 1) * 128].rearrange("j p t -> p j t"),
                        D_qt[j0:j1], [QT])
            Wp = P.sbuf("Wp", [64, 8, D], BF16, ph)
            stg = P.sbuf("stgp", [64, 8, D], F32, ph)
            DMA("sp", stg[:], wap_d.rearrange("(h p) n -> p h n", p=64), [], [stg])
            CP("dve", Wp[:], stg[:], [stg], [Wp])
            PT = [P.sbuf(f"PT{i}", [128, 2, 512], BF16, ph) for i in range(3)]
            oacc = [P.sbuf(f"oacc{i}", [65, 512], F32, ph) for i in range(2)]
            rrow = [P.sbuf(f"rrow{i}", [65, 512], F32, ph) for i in range(2)]
            OTn = P.sbuf("OTn", [64, 8, 512], BF16, ph)
            aosb = [P.sbuf(f"aosb{i}", [128, D], F32, ph) for i in range(2)]
            ps_s = [P.psum(f"ps_s{i}", [128, 2, 512], F32, ph) for i in range(2)]
            ps_o = [P.psum(f"ps_o{i}", [128, 512], F32, ph) for i in range(2)]
            ps_bc = P.psum("ps_bc", [128, 512], F32, ph)
            ps_pj = [P.psum(f"ps_pj{i}", [128, 512], F32, ph) for i in range(1)]
            it = 0
            CQ = QW // 128
            deferred = []

            def flush():
                for f_ in deferred:
                    f_()
                del deferred[:]

            def pv_ops(pt, kb):
                def run():
                    for g in range(2):
                        MM(ps_o[g][0:65, 0:QW], Vs[:, kb, g, :], pt[:, g, 0:QW], kb == 0, kb == NB - 1, [Vs, pt], [ps_o[g]])
                return run

            def fin_ops(i):
                def run():
                    for g in range(2):
                        h = 4 * g + i
                        po = ps_o[g]
                        CP("dve", oacc[g][:, 0:QW], po[0:65, 0:QW], [po], [oacc[g]])
                        RECIP(rrow[g][64:65, 0:QW], oacc[g][64:65, 0:QW], [oacc[g]], [rrow[g]])
                        MM(ps_bc[0:64, 0:QW], onesf[64:65, 0:64], rrow[g][64:65, 0:QW], True, True, [onesf, rrow[g]], [ps_bc])
                        TT("dve", OTn[:, h, 0:QW], oacc[g][0:64, 0:QW], ps_bc[0:64, 0:QW], ALU.mult, [oacc[g], ps_bc], [OTn])
                return run

            for qt in range(NQT):
                c0 = qt * CQ
                for i in range(4):
                    for kb in range(NB):
                        ss_ = ps_s[it % 2]
                        pt = PT[it % 3]
                        it += 1
                        for g in range(2):
                            MM(ss_[:, g, 0:QW], KT[g * 64:(g + 1) * 64, kb, :],
                               QT[g * 64:(g + 1) * 64, i, c0:c0 + CQ, :], True, True, [KT, QT], [ss_])
                        ACT(pt[:, :, 0:QW], ss_[:, :, 0:QW], AF.Exp, [ss_, negB], [pt], scale=0.125, bias=negB[:])
                        flush()
                        deferred.append(pv_ops(pt, kb))
                        if kb == NB - 1:
                            deferred.append(fin_ops(i))
                flush()
                for tt in range(CQ):
                    cj = c0 + tt
                    ao = aosb[cj % 2]
                    for half in range(2):
                        pj = ps_pj[0]
                        for h in range(8):
                            MM(pj[:, :], OTn[:, h, tt * 128:(tt + 1) * 128], Wp[:, h, half * 512:(half + 1) * 512],
                               h == 0, h == 7, [OTn, Wp], [pj])
                        CP("act" if half else "dve", ao[:, half * 512:(half + 1) * 512], pj[:, :], [pj], [ao])
                    DMA("pool", s_ao[cj], ao[:], [ao], [D_ao[cj]])
        P.barrier()

        with ExitStack() as ph:
            Wz = P.sbuf("Wz", [128, 8, 1024], BF16, ph)
            stage = [P.sbuf(f"stgb{i}", [128, 2048], F32, ph) for i in range(2)]
            load_w(Wz, 0, w_in_d[:, OZ:OZ + 1024], 1024, stage)
            gsn = P.sbuf("gsn", [128, D], F32, ph)
            DMA("sp", gsn[:], ap_(sn_d, 0, [[0, 128], [1, D]]), [], [gsn])
            tmp = {"sq": P.sbuf("sqb", [128, D], F32, ph), "ss": P.sbuf("ssb", [128, 8], F32, ph)}
            x_main = [P.sbuf(f"xmb{i}", [128, D], F32, ph) for i in range(2)]
            hbuf = P.sbuf("hbufb", [128, D], BF16, ph)
            hTb = P.sbuf("hTb", [128, 8, 128], BF16, ph)
            xs_tok = [P.sbuf(f"xstokb{i}", [128, 1024], BF16, ph) for i in range(2)]
            b_tok = [P.sbuf(f"btokb{i}", [128, 256], BF16, ph) for i in range(2)]
            bT = [P.sbuf(f"bTb{i}", [128, 256], BF16, ph) for i in range(2)]
            cT = [P.sbuf(f"cTb{i}", [128, 256], BF16, ph) for i in range(2)]
            yfb = [P.sbuf(f"yfb{i}", [128, 1024], F32, ph) for i in range(2)]
            dtB = P.sbuf("dtB", [128, 16], F32, ph)
            adtB = P.sbuf("adtB", [128, 16], F32, ph)
            sm = {n_: P.sbuf(n_ + "B", [128, 16], F32, ph) for n_ in
                  ("ds", "dtds", "ea", "nacs", "acs_sb", "edec")}
            env = dict(
                pb_st=P.psum("pb_stB", [128, 512], F32, ph), pb_cv=P.psum("pb_cvB", [128, 512], F32, ph),
                pb_tok=P.psum("pb_tokB", [128, 512], F32, ph),
                accs=[P.psum(f"accB{i}", [128, 512], F32, ph) for i in range(4)], sm=sm,
                Rm=P.sbuf("RmB", [128, 16, 128], F32, ph), Em=P.sbuf("EmB", [128, 16, 128], BF16, ph),
                Mm=P.sbuf("MmB", [128, 16, 128], BF16, ph), xd=P.sbuf("xdB", [128, 1024], BF16, ph),
                xdd=P.sbuf("xddB", [128, 1024], BF16, ph), Hbf=P.sbuf("HbfB", [128, 1024], BF16, ph),
                ytmp=P.sbuf("ytmpB", [128, 1024], F32, ph), Htmp=P.sbuf("HtmpB", [128, 1024], F32, ph),
                cbt=P.sbuf("cbtB", [128, 256], F32, ph))
            env["st_acs"] = env["pb_st"]
            env["st_tot"] = env["pb_st"]
            pb_bf = P.psum("pb_bfB", [128, 1024], BF16, ph)
            tpb = pb_bf
            pb_tok = env["pb_tok"]
            yout = P.sbuf("youtB", [128, 1024], F32, ph)
            zs = P.sbuf("zs", [128, 1024], F32, ph)
            ybf = [P.sbuf(f"ybf{i}", [128, 1024], BF16, ph) for i in range(2)]
            for jj in range(NO):
                j = NO - 1 - jj
                r = jj % 2
                xm = x_main[r]
                DMA("sp", xm[:], xo_d[j][2:130, :], [], [xm])
                DMA("sp", xs_tok[r][:], s_xs[j], [D_xs[j]], [xs_tok[r]])
                DMA("sp", b_tok[r][:], s_bt[j], [D_bt[j]], [b_tok[r]])
                DMA("sp", bT[r][:], s_bT[j], [D_bT[j]], [bT[r]])
                DMA("sp", cT[r][:], s_cT[j], [D_cT[j]], [cT[r]])
                DMA("sp", yfb[r][:], s_yf[j], [D_yf[j]], [yfb[r]])
                CP("dve", dtB[:], dtb_own[:, j, :], [dtb_own], [dtB])
                TT("dve", adtB[:], dtB[:], aneg[:, 16:32], ALU.mult, [dtB, aneg], [adtB])
                ssd_chunk(env, 1, adtB, 0, dtB, 0, xs_tok[r], b_tok[r], bT[r], cT[r], Hb, yfb[r], yout)
                rmsnorm_tok(xm[:], 128, gn1a[:], hbuf[:], [xm, gn1a], [hbuf], tmp)
                transpose8(lambda kk: hbuf[:, kk * 128:(kk + 1) * 128], lambda half: hTb[:, half * 4:(half + 1) * 4, :],
                           [hbuf], [hTb], pb_bf, tpb)
                for half in range(2):
                    for k in range(8):
                        MM(pb_tok[:, :], hTb[:, k, :], Wz[:, k, half * 512:(half + 1) * 512], k == 0, k == 7, [hTb, Wz], [pb_tok])
                    ACT(zs[:, half * 512:(half + 1) * 512], pb_tok[:, :], AF.Silu, [pb_tok], [zs])
                TT("dve", yout[:], yout[:], zs[:], ALU.mult, [yout, zs], [yout])
                rmsnorm_tok(yout[:], 128, gsn[:], ybf[r][:], [yout, gsn], [ybf[r]], tmp)
                DMA("pool", s_yb[j], ybf[r][:], [ybf[r]], [D_yb[j]])
        P.barrier()

        with ExitStack() as ph:
            Wg = P.sbuf("Wg", [128, 8, 2048], BF16, ph)
            Wsp = P.sbuf("Wsp", [128, 8, D], BF16, ph)
            Wo = P.sbuf("Wo", [128, 8, D], BF16, ph)
            stage = [P.sbuf(f"stgc{i}", [128, 2048], F32, ph) for i in range(2)]
            load_w(Wg, 0, w_in_d[:, OG:OG + 2048], 2048, stage)
            load_w(Wsp, 0, wsp_d, D, stage)
            load_w(Wo, 0, wo_d, D, stage)
            gn1b = P.sbuf("gn1b", [128, D], F32, ph)
            DMA("sp", gn1b[:], ap_(n1b_d, 0, [[0, 128], [1, D]]), [], [gn1b])
            tmp = {"sq": P.sbuf("sqc", [128, D], F32, ph), "ss": P.sbuf("ssc", [128, 8], F32, ph)}
            x_main = [P.sbuf(f"xmc{i}", [128, D], F32, ph) for i in range(2)]
            aob = [P.sbuf(f"aob{i}", [128, D], F32, ph) for i in range(2)]
            ybf = [P.sbuf(f"ybfc{i}", [128, D], BF16, ph) for i in range(2)]
            hbuf = P.sbuf("hbufc", [128, D], BF16, ph)
            hTb = P.sbuf("hTc", [128, 8, 128], BF16, ph)
            yT = P.sbuf("yT", [128, 8, 128], BF16, ph)
            mT = P.sbuf("mT", [128, 8, 128], BF16, ph)
            gsig = P.sbuf("gsig", [128, 2048], F32, ph)
            mix = P.sbuf("mix", [128, D], F32, ph)
            sso = P.sbuf("sso", [128, D], F32, ph)
            mixbf = P.sbuf("mixbf", [128, D], BF16, ph)
            x1 = [P.sbuf(f"x1_{i}", [128, D], F32, ph) for i in range(2)]
            pb_bf = P.psum("pb_bfC", [128, 1024], BF16, ph)
            tpb = pb_bf
            pbs = [P.psum(f"pbC{i}", [128, 512], F32, ph) for i in range(4)]
            for j in range(NO):
                r = j % 2
                xm = x_main[r]
                DMA("sp", xm[:], xo_d[j][2:130, :], [], [xm])
                DMA("sp", aob[r][:], s_ao[j], [D_ao[j]], [aob[r]])
                DMA("sp", ybf[r][:], s_yb[j], [D_yb[j]], [ybf[r]])
                rmsnorm_tok(xm[:], 128, gn1a[:], hbuf[:], [xm, gn1a], [hbuf], tmp)
                transpose8(lambda kk: hbuf[:, kk * 128:(kk + 1) * 128], lambda half: hTb[:, half * 4:(half + 1) * 4, :],
                           [hbuf], [hTb], pb_bf, tpb)
                for qd in range(4):
                    for k in range(8):
                        MM(pbs[qd][:, :], hTb[:, k, :], Wg[:, k, qd * 512:(qd + 1) * 512], k == 0, k == 7, [hTb, Wg], [pbs[qd]])
                    ACT(gsig[:, qd * 512:(qd + 1) * 512], pbs[qd][:, :], AF.Sigmoid, [pbs[qd]], [gsig])
                transpose8(lambda kk: ybf[r][:, kk * 128:(kk + 1) * 128], lambda half: yT[:, half * 4:(half + 1) * 4, :],
                           [ybf[r]], [yT], pb_bf, tpb)
                TT("dve", mix[:], aob[r][:], gsig[:, 0:1024], ALU.mult, [aob[r], gsig], [mix])
                for half in range(2):
                    pb = pbs[half]
                    for k in range(8):
                        MM(pb[:, :], yT[:, k, :], Wsp[:, k, half * 512:(half + 1) * 512], k == 0, k == 7, [yT, Wsp], [pb])
                    TT("dve", sso[:, half * 512:(half + 1) * 512], pb[:, :], gsig[:, 1024 + half * 512:1024 + (half + 1) * 512],
                       ALU.mult, [pb, gsig], [sso])
                TT("dve", mixbf[:], mix[:], sso[:], ALU.add, [mix, sso], [mixbf])
                transpose8(lambda kk: mixbf[:, kk * 128:(kk + 1) * 128], lambda half: mT[:, half * 4:(half + 1) * 4, :],
                           [mixbf], [mT], pb_bf, tpb)
                for half in range(2):
                    pb = pbs[2 + half]
                    for k in range(8):
                        MM(pb[:, :], mT[:, k, :], Wo[:, k, half * 512:(half + 1) * 512], k == 0, k == 7, [mT, Wo], [pb])
                    CP("act", mix[:, half * 512:(half + 1) * 512], pb[:, :], [pb], [mix])
                rmsnorm_tok(mix[:], 128, gn1b[:], sso[:], [mix, gn1b], [sso], tmp)
                TT("dve", x1[r][:], sso[:], xm[:], ALU.add, [sso, xm], [x1[r]])
                DMA("pool", s_x1[j], x1[r][:], [x1[r]], [D_x1[j]])
        P.barrier()

        NF = D_FF // 128
        NFH = NF // 2
        UW = min(512, T)
        NT = UW // 128
        for hf in range(2):
            with ExitStack() as ph:
                Wgu = P.sbuf("Wgu", [128, 8, 2 * NFH * 128], BF16, ph)
                Wd = P.sbuf("Wd", [128, NFH, D], BF16, ph)
                stage = [P.sbuf(f"stgd{i}", [128, 2816], F32, ph) for i in range(2)]
                f0 = hf * NFH * 128
                load_w(Wgu, 0, wgu_d[:, f0:f0 + NFH * 128], NFH * 128, stage)
                load_w(Wgu, NFH * 128, wgu_d[:, D_FF + f0:D_FF + f0 + NFH * 128], NFH * 128, stage)
                load_w(Wd, 0, wd_d[f0:f0 + NFH * 128, :], D, stage)
                gn2a = P.sbuf("gn2a", [128, D], F32, ph)
                gn2b = P.sbuf("gn2b", [128, D], F32, ph)
                DMA("sp", gn2a[:], ap_(n2a_d, 0, [[0, 128], [1, D]]), [], [gn2a])
                DMA("sp", gn2b[:], ap_(n2b_d, 0, [[0, 128], [1, D]]), [], [gn2b])
                tmp = {"sq": P.sbuf("sqd", [128, D], F32, ph), "ss": P.sbuf("ssd", [128, 8], F32, ph)}
                x1b = [P.sbuf(f"x1c{i}", [128, NT, D], F32, ph) for i in range(2)]
                hb = P.sbuf("hbc", [128, D], BF16, ph)
                h2T = P.sbuf("h2T", [128, 8, UW], BF16, ph)
                gact = P.sbuf("gact", [128, UW], F32, ph)
                actT = P.sbuf("actT", [128, NFH, UW], BF16, ph)
                ffn = [P.sbuf(f"ffn{i}", [128, D], F32, ph) for i in range(2)]
                ffp = [P.sbuf(f"ffp{i}", [128, D], F32, ph) for i in range(2)]
                ob = [P.sbuf(f"ob{i}", [128, D], F32, ph) for i in range(2)]
                pb_bf = P.psum("pb_bfD", [128, 1024], BF16, ph)
                tpb = pb_bf
                pg = [P.psum(f"pg{i}", [128, 512], F32, ph) for i in range(2)]
                pu = [P.psum(f"pu{i}", [128, 512], F32, ph) for i in range(2)]
                pd = [P.psum(f"pd{i}", [128, 512], F32, ph) for i in range(2)]
                for u in range(T // UW):
                    xb = x1b[u % 2]
                    for t in range(NT):
                        cj = u * NT + t
                        DMA("sp", xb[:, t, :], s_x1[cj], [D_x1[cj]], [xb])
                    for t in range(NT):
                        rmsnorm_tok(xb[:, t, :], 128, gn2a[:], hb[:], [xb, gn2a], [hb], tmp)
                        transpose8(lambda kk: hb[:, kk * 128:(kk + 1) * 128],
                                   lambda half: h2T[:, half * 4:(half + 1) * 4, t * 128:(t + 1) * 128], [hb], [h2T], pb_bf, tpb)
                    for f in range(NFH):
                        g_, u_ = pg[f % 2], pu[f % 2]
                        for k in range(8):
                            MM(g_[:, 0:UW], Wgu[:, k, f * 128:(f + 1) * 128], h2T[:, k, :], k == 0, k == 7, [Wgu, h2T], [g_])
                        for k in range(8):
                            MM(u_[:, 0:UW], Wgu[:, k, (NFH + f) * 128:(NFH + f + 1) * 128], h2T[:, k, :], k == 0, k == 7,
                               [Wgu, h2T], [u_])
                        ACT(gact[:, :], g_[:, 0:UW], AF.Silu, [g_], [gact])
                        TT("dve", actT[:, f, :], gact[:, :], u_[:, 0:UW], ALU.mult, [gact, u_], [actT])
                    for t in range(NT):
                        cj = u * NT + t
                        ff = ffn[cj % 2]
                        if hf == 1:
                            DMA("sp", ffp[cj % 2][:], s_ff[cj], [D_ff[cj]], [ffp[cj % 2]])
                        for half in range(2):
                            p_ = pd[half]
                            for f in range(NFH):
                                MM(p_[:, :], actT[:, f, t * 128:(t + 1) * 128], Wd[:, f, half * 512:(half + 1) * 512],
                                   f == 0, f == NFH - 1, [actT, Wd], [p_])
                            if hf == 0:
                                CP("act", ff[:, half * 512:(half + 1) * 512], p_[:, :], [p_], [ff])
                            else:
                                TT("dve", ff[:, half * 512:(half + 1) * 512], p_[:, :], ffp[cj % 2][:, half * 512:(half + 1) * 512],
                                   ALU.add, [p_, ffp[cj % 2]], [ff])
                        if hf == 0:
                            DMA("pool", s_ff[cj], ff[:], [ff], [D_ff[cj]])
                        else:
                            o_ = ob[cj % 2]
                            rmsnorm_tok(ff[:], 128, gn2b[:], o_[:], [ff, gn2b], [o_], tmp)
                            TT("dve", o_[:], o_[:], xb[:, t, :], ALU.add, [o_, xb], [o_])
                            DMA("pool", out_d[cj], o_[:], [o_], [D_out[cj]])
            P.barrier()
        P.emit()
    return nc


def make_in_maps(inputs, S, T, n_cores):
    x = np.asarray(inputs["x"], dtype=np.float32)
    B = x.shape[0]
    NQ = S // T
    NO = T // 128
    NS = (S - T) // 128
    f32 = lambda a: np.ascontiguousarray(np.asarray(a, dtype=np.float32))
    common = {
        "w_in": f32(inputs["w_in"][0]),
        "q_norm": f32(inputs["q_norm"][0])[None, :],
        "k_norm": f32(inputs["k_norm"][0])[None, :],
        "conv_w": f32(inputs["conv_w"][0]),
        "conv_b": f32(inputs["conv_b"][0])[None, :],
        "dt_bias": f32(np.concatenate([inputs["dt_bias_f"][0], inputs["dt_bias_b"][0]]))[None, :],
        "a_log": f32(np.concatenate([inputs["a_log_f"][0], inputs["a_log_b"][0]]))[None, :],
        "d_skip": f32(inputs["d_skip"][0])[None, :],
        "ssd_norm": f32(inputs["ssd_norm"][0])[None, :],
        "w_attn_proj": f32(inputs["w_attn_proj"][0]),
        "w_ssd_proj": f32(inputs["w_ssd_proj"][0]),
        "w_out": f32(inputs["w_out"][0]),
        "norm1_pre": f32(inputs["norm1_pre"][0])[None, :],
        "norm1_post": f32(inputs["norm1_post"][0])[None, :],
        "norm2_pre": f32(inputs["norm2_pre"][0])[None, :],
        "norm2_post": f32(inputs["norm2_post"][0])[None, :],
        "w_gate_up": f32(inputs["w_gate_up"][0]),
        "w_down": f32(inputs["w_down"][0]),
        "c_ident": np.eye(128, dtype=np.float32),
        "c_tri": np.triu(np.ones((128, 128), np.float32)),
        "c_triu": np.tril(np.ones((128, 128), np.float32)),
    }
    inv = (10000.0 ** (-np.arange(0, 32, 2, dtype=np.float32) / 32)).astype(np.float32)
    common["c_invf"] = np.concatenate([inv, inv])[None, :].astype(np.float32)
    maps = []
    for c in range(n_cores):
        b, q = c // NQ, c % NQ
        xp = np.zeros((S + 4, D), np.float32)
        xp[2:S + 2] = x[b]
        nch = S // 128
        own = list(range(q * NO, (q + 1) * NO))
        prev = list(range(q * NO - 1, -1, -1))
        nxt = list(range((q + 1) * NO, nch))
        slots = prev + nxt

        def gather(chs):
            if not chs:
                return np.zeros((0, 132, D), np.float32)
            return np.stack([xp[ch * 128:ch * 128 + 132] for ch in chs])

        def pos(chs):
            n = max(len(chs), 1)
            p = np.zeros((128, n, 2), np.float32)
            for i, ch in enumerate(chs):
                t = ch * 128 + np.arange(128)
                p[:, i, 0] = t // GRID_W
                p[:, i, 1] = t % GRID_W
            return p
        mk = np.zeros((128, max(NS, 1), 2), np.float32)
        mk[:, :len(prev), 0] = 1.0
        mk[:, len(prev):len(slots), 1] = 1.0
        m = dict(common)
        m["xs"] = gather(slots)
        m["xo"] = gather(own)
        m["poss"] = pos(slots)
        m["poso"] = pos(own)
        m["mk"] = mk
        maps.append(m)
    return maps


_NC_CACHE = {}


def kernel(**inputs):
    x = np.asarray(inputs["x"])
    B, S, _ = x.shape
    n_cores = 8
    NQ = n_cores // B
    T = S // NQ
    key = (S, T)
    if key not in _NC_CACHE:
        _NC_CACHE[key] = build(S, T)
    nc = _NC_CACHE[key]
    maps = make_in_maps(inputs, S, T, n_cores)
    res = run_bass_kernel_spmd(nc, maps, core_ids=list(range(n_cores)))
    out = np.zeros((B, S, D), np.float32)
    for c in range(n_cores):
        b, q = c // NQ, c % NQ
        out[b, q * T:(q + 1) * T] = np.asarray(res.results[c]["out"]).reshape(T, D)
    return out
```
